# Optimizing a Trainium2 kernel written in Bass

```python
import jax, jax.numpy as jnp
from jax import lax
import numpy as np

D_MODEL = 2048
BATCH = 4
SEQ = 2048
DEPTH = 1
DEC_BATCH = 128
DEC_SEQ = 4
PAST_LEN = 16384
PAGE_SIZE = 128

N_META = 16
MIX_WIDTH = D_MODEL
RWKV_WIDTH = MIX_WIDTH // 2
RWKV_HEAD = 64
RWKV_HEADS = RWKV_WIDTH // RWKV_HEAD
DECAY_LORA = 64
AAA_LORA = 64
GATE_LORA = 160
RWKV_COLS = 3 * RWKV_WIDTH + DECAY_LORA + AAA_LORA + GATE_LORA
RET_WIDTH = MIX_WIDTH - RWKV_WIDTH
RET_HEADS = 4
RET_HEAD = RET_WIDTH // RET_HEADS
RET_COLS = 4 * RET_WIDTH
IN_COLS = RWKV_COLS + RET_COLS
RET_CHUNK = 128
D_FF = -(-8 * D_MODEL // (3 * 256)) * 256
RMS_EPS = 1e-6
RWKV_GN_EPS = 64e-5
RET_GN_EPS = 1e-6
ROPE_BASE = 10000.0

kernel_name = 'hybrid_rwkv7_retention_decode_step'

F32 = jnp.float32


def rms_norm(x, g):
    xf = x.astype(F32)
    y = xf * lax.rsqrt(jnp.mean(xf * xf, axis=-1, keepdims=True) + RMS_EPS)
    return (y * g.astype(F32)).astype(x.dtype)


def rwkv7_scan(r, w, k, v, a_vec, b_vec, S0):
    def step(S, inp):
        r_t, w_t, k_t, v_t, a_t, b_t = inp
        sa = jnp.einsum('bhij,bhj->bhi', S, a_t)
        S = S * w_t[:, :, None, :] + sa[..., None] * b_t[:, :, None, :] + v_t[..., None] * k_t[:, :, None, :]
        return S, jnp.einsum('bhij,bhj->bhi', S, r_t)
    xs = tuple(jnp.swapaxes(t, 0, 1) for t in (r, w, k, v, a_vec, b_vec))
    S, out = lax.scan(step, S0, xs)
    return jnp.swapaxes(out, 0, 1), S


def rwkv7_mixer(z, z_prev, S0, mu, w0, w2, a0, a2, g2, k_k, k_a, r_k, ln_w, ln_b):
    B, T, _ = z.shape
    W = RWKV_WIDTH
    zs = z + mu * (z_prev - z)
    r, k, v, wd, ad, gd = jnp.split(zs, [W, 2 * W, 3 * W, 3 * W + DECAY_LORA, 3 * W + DECAY_LORA + AAA_LORA], axis=-1)
    heads = lambda t: t.reshape(B, T, RWKV_HEADS, RWKV_HEAD)
    w = -jax.nn.softplus(-(w0 + jnp.tanh(wd) @ w2)) - 0.5
    decay = jnp.exp(-jnp.exp(w))
    a = jax.nn.sigmoid(a0 + ad @ a2)
    g = jax.nn.sigmoid(gd) @ g2
    kk = heads(k * k_k)
    kk = kk / jnp.maximum(jnp.sqrt(jnp.sum(kk * kk, axis=-1, keepdims=True)), 1e-12)
    k = k * (1.0 + (a - 1.0) * k_a)
    rh, kh, vh = heads(r), heads(k), heads(v)
    o, S = rwkv7_scan(rh, heads(decay), kh, vh, -kk, kk * heads(a), S0)
    mean = jnp.mean(o, axis=-1, keepdims=True)
    var = jnp.mean(jnp.square(o - mean), axis=-1, keepdims=True)
    o = ((o - mean) * lax.rsqrt(var + RWKV_GN_EPS)).reshape(B, T, W) * ln_w + ln_b
    bonus = jnp.sum(rh * kh * r_k, axis=-1, keepdims=True) * vh
    o = o + bonus.reshape(B, T, W)
    return o * g, S


def rotary(x, pos):
    inv_freq = 1.0 / (ROPE_BASE ** jnp.linspace(0.0, 1.0, RET_HEAD // 2, dtype=F32))
    ang = pos.astype(F32)[:, None] * inv_freq[None, :]
    cos = jnp.cos(ang)[None, :, None, :]
    sin = jnp.sin(ang)[None, :, None, :]
    x0 = x[..., 0::2]
    x1 = x[..., 1::2]
    return jnp.stack([x0 * cos - x1 * sin, x0 * sin + x1 * cos], axis=-1).reshape(x.shape)


def retention_chunk(q, k, v, S, log_gamma):
    C = q.shape[1]
    idx = jnp.arange(C, dtype=F32)
    diff = idx[:, None] - idx[None, :]
    dmask = jnp.where(diff[None] >= 0, jnp.exp(log_gamma[:, None, None] * jnp.maximum(diff, 0.0)[None]), 0.0)
    scores = jnp.einsum('bihd,bjhd->bhij', q, k) * dmask[None]
    o_intra = jnp.einsum('bhij,bjhe->bihe', scores, v)
    inter_scale = jnp.exp(log_gamma[None, :] * (idx + 1.0)[:, None])
    o_inter = jnp.einsum('bihd,bhde->bihe', q, S) * inter_scale[None, :, :, None]
    k_scale = jnp.exp(log_gamma[:, None] * (C - 1.0 - idx)[None, :])
    S_new = jnp.exp(log_gamma * C)[None, :, None, None] * S + jnp.einsum('bjhd,hj,bjhe->bhde', k, k_scale, v)
    return o_intra + o_inter, S_new


def retention_mixer(z, S0, pos, prompt):
    B, T, _ = z.shape
    q, k, v, g = jnp.split(z, 4, axis=-1)
    heads = lambda t: t.reshape(B, T, RET_HEADS, RET_HEAD)
    q = rotary(heads(q), pos)
    k = rotary(heads(k), pos) * (RET_HEAD ** -0.5)
    v = heads(v)
    log_gamma = jnp.log(1.0 - 2.0 ** (-5.0 - jnp.arange(RET_HEADS, dtype=F32)))
    if prompt:
        o_m, S = retention_chunk(q[:, :N_META], k[:, :N_META], v[:, :N_META], S0, log_gamma)
        n_chunks = (T - N_META) // RET_CHUNK
        to_chunks = lambda t: jnp.swapaxes(t[:, N_META:].reshape(B, n_chunks, RET_CHUNK, RET_HEADS, RET_HEAD), 0, 1)

        def body(S_c, qkv):
            o_c, S_c = retention_chunk(qkv[0], qkv[1], qkv[2], S_c, log_gamma)
            return S_c, o_c

        S, o_c = lax.scan(body, S, (to_chunks(q), to_chunks(k), to_chunks(v)))
        o = jnp.concatenate([o_m, jnp.swapaxes(o_c, 0, 1).reshape(B, T - N_META, RET_HEADS, RET_HEAD)], axis=1)
    else:
        o, S = retention_chunk(q, k, v, S0, log_gamma)
    o = o * lax.rsqrt(jnp.mean(o * o, axis=-1, keepdims=True) + RET_GN_EPS)
    return o.reshape(B, T, RET_WIDTH) * jax.nn.silu(g), S


def decoder_layer(x, shift0, S_a0, S_b0, pos, prompt, norm_mix, w_in, rwkv_mu, rwkv_w0, rwkv_w2,
                  rwkv_a0, rwkv_a2, rwkv_g2, rwkv_kk, rwkv_ka, rwkv_rk, rwkv_ln_w, rwkv_ln_b,
                  w_out, norm_ffn, w_gate, w_up, w_down):
    f = lambda t: t.astype(F32)
    xn = f(rms_norm(x, norm_mix))
    z = xn @ f(w_in)
    z_a, z_b = z[..., :RWKV_COLS], z[..., RWKV_COLS:]
    z_prev = jnp.concatenate([f(shift0)[:, None], z_a[:, :-1]], axis=1)
    o_a, S_a = rwkv7_mixer(z_a, z_prev, f(S_a0), f(rwkv_mu), f(rwkv_w0), f(rwkv_w2), f(rwkv_a0),
                           f(rwkv_a2), f(rwkv_g2), f(rwkv_kk), f(rwkv_ka), f(rwkv_rk),
                           f(rwkv_ln_w), f(rwkv_ln_b))
    o_b, S_b = retention_mixer(z_b, f(S_b0), pos, prompt)
    h = x + (jnp.concatenate([o_a, o_b], axis=-1) @ f(w_out)).astype(x.dtype)
    hn = rms_norm(h, norm_ffn)
    h = h + (jax.nn.silu(hn @ w_gate) * (hn @ w_up)) @ w_down
    return h, z_a[:, -1], S_a, S_b


def setup_inputs(seed: int = 0) -> dict:
    key = jax.random.key(seed)
    ks = jax.random.split(key, 32)
    n = lambda i, shape: jax.random.normal(ks[i], shape, F32)
    L = DEPTH
    return {
        'x_prompt': n(0, (BATCH, SEQ, D_MODEL)),
        'x_sample': n(1, (DEC_BATCH, DEC_SEQ, D_MODEL)),
        'state_shift': n(2, (L, DEC_BATCH, RWKV_COLS)),
        'state_rwkv': n(3, (L, DEC_BATCH, RWKV_HEADS, RWKV_HEAD, RWKV_HEAD)),
        'state_ret': 0.5 * n(4, (L, DEC_BATCH, RET_HEADS, RET_HEAD, RET_HEAD)),
        'meta_tokens': n(5, (N_META, D_MODEL)),
        'norm_mix': 1.0 + 0.05 * n(6, (L, D_MODEL)),
        'w_in': n(7, (L, D_MODEL, IN_COLS)) * D_MODEL ** -0.5,
        'rwkv_mu': jax.random.uniform(ks[8], (L, RWKV_COLS), F32),
        'rwkv_w0': jax.random.uniform(ks[9], (L, RWKV_WIDTH), F32, -6.0, -1.0),
        'rwkv_w2': n(10, (L, DECAY_LORA, RWKV_WIDTH)) * 0.1 * DECAY_LORA ** -0.5,
        'rwkv_a0': 0.1 * n(11, (L, RWKV_WIDTH)),
        'rwkv_a2': n(12, (L, AAA_LORA, RWKV_WIDTH)) * 0.5 * AAA_LORA ** -0.5,
        'rwkv_g2': n(13, (L, GATE_LORA, RWKV_WIDTH)) * GATE_LORA ** -0.5,
        'rwkv_kk': 0.85 + 0.05 * n(14, (L, RWKV_WIDTH)),
        'rwkv_ka': 1.0 + 0.05 * n(15, (L, RWKV_WIDTH)),
        'rwkv_rk': 0.1 * n(16, (L, RWKV_HEADS, RWKV_HEAD)),
        'rwkv_ln_w': 1.0 + 0.05 * n(17, (L, RWKV_WIDTH)),
        'rwkv_ln_b': 0.02 * n(18, (L, RWKV_WIDTH)),
        'w_out': n(19, (L, MIX_WIDTH, D_MODEL)) * 0.5 * MIX_WIDTH ** -0.5,
        'norm_ffn': 1.0 + 0.05 * n(20, (L, D_MODEL)),
        'w_gate': n(21, (L, D_MODEL, D_FF)) * D_MODEL ** -0.5,
        'w_up': n(22, (L, D_MODEL, D_FF)) * D_MODEL ** -0.5,
        'w_down': n(23, (L, D_FF, D_MODEL)) * D_FF ** -0.5,
        'norm_final': 1.0 + 0.05 * n(24, (D_MODEL,)),
    }


def reference(x_prompt, x_sample, state_shift, state_rwkv, state_ret, meta_tokens, norm_mix, w_in,
              rwkv_mu, rwkv_w0, rwkv_w2, rwkv_a0, rwkv_a2, rwkv_g2, rwkv_kk, rwkv_ka, rwkv_rk,
              rwkv_ln_w, rwkv_ln_b, w_out, norm_ffn, w_gate, w_up, w_down, norm_final):
    B_p = x_prompt.shape[0]
    meta = jnp.broadcast_to(meta_tokens[None].astype(x_prompt.dtype), (B_p, N_META, x_prompt.shape[-1]))
    h_p = jnp.concatenate([meta, x_prompt], axis=1)
    h_s = x_sample
    pos_p = jnp.arange(h_p.shape[1])
    pos_s = PAST_LEN + jnp.arange(x_sample.shape[1])
    shift0_p = jnp.zeros((B_p, RWKV_COLS), F32)
    rwkv0_p = jnp.zeros((B_p, RWKV_HEADS, RWKV_HEAD, RWKV_HEAD), F32)
    ret0_p = jnp.zeros((B_p, RET_HEADS, RET_HEAD, RET_HEAD), F32)
    sh_p, ra_p, rb_p, sh_s, ra_s, rb_s = [], [], [], [], [], []
    for l in range(DEPTH):
        lw = (norm_mix[l], w_in[l], rwkv_mu[l], rwkv_w0[l], rwkv_w2[l], rwkv_a0[l], rwkv_a2[l],
              rwkv_g2[l], rwkv_kk[l], rwkv_ka[l], rwkv_rk[l], rwkv_ln_w[l], rwkv_ln_b[l],
              w_out[l], norm_ffn[l], w_gate[l], w_up[l], w_down[l])
        h_p, s1, s2, s3 = decoder_layer(h_p, shift0_p, rwkv0_p, ret0_p, pos_p, True, *lw)
        sh_p.append(s1); ra_p.append(s2); rb_p.append(s3)
        h_s, s1, s2, s3 = decoder_layer(h_s, state_shift[l], state_rwkv[l], state_ret[l], pos_s, False, *lw)
        sh_s.append(s1); ra_s.append(s2); rb_s.append(s3)
    y_prompt = rms_norm(h_p, norm_final)[:, N_META:]
    y_sample = rms_norm(h_s, norm_final)
    return (y_prompt, y_sample, jnp.stack(sh_p), jnp.stack(ra_p), jnp.stack(rb_p),
            jnp.stack(sh_s), jnp.stack(ra_s), jnp.stack(rb_s))
```

```python
import numpy as np
from contextlib import ExitStack
import concourse.bass as bass
import concourse.mybir as mybir
from concourse.bass_utils import run_bass_kernel_spmd

F32 = mybir.dt.float32
BF16 = mybir.dt.bfloat16
ALU = mybir.AluOpType
AF = mybir.ActivationFunctionType
AX = mybir.AxisListType

D = 2048
KC = 16
NMETA = 16
SEQ = 2048
TP = NMETA + SEQ
NS = 64
T = TP + NS
NCH = 59
NFM = 43
DFF = 5632
NFB = DFF // 128
OWN = 1024 + NS
C0 = float(np.exp(-0.5))
GAM = [1.0 - 2.0 ** (-5.0 - h) for h in range(4)]
TILES = [(0, 16)] + [(16 + 128 * i, 128) for i in range(16)] + [(TP, NS)]
BLKS = [(0, 512), (512, 512), (1024, 512), (1536, 512), (2048, T - 2048)]


class Buf:
    __slots__ = ("w", "r")

    def __init__(self):
        self.w = None
        self.r = []


class Prog:
    ENG = ("pe", "act", "dve", "pool", "sp")

    def __init__(self, nc, n_dma=48):
        self.nc = nc
        self.ops = {k: [] for k in self.ENG}
        self.cnt = {k: 0 for k in self.ENG}
        self.seen = {k: {} for k in self.ENG}
        self.n_dma = n_dma
        self.dma_use = [0] * n_dma
        self.dma_rr = 0
        self.dma_rr_pool = 0
        self.n_sp = n_dma - 12

    def _need(self, waits, eng, ev, kind):
        if ev is None:
            return
        sk, val, src = ev
        if src == eng and not isinstance(sk, tuple):
            if eng == "pe":
                return
        if waits.get(sk, 0) < val:
            waits[sk] = val

    def op(self, eng, fn, reads=(), writes=(), dma=False):
        waits = {}
        for b in reads:
            self._need(waits, eng, b.w, "raw")
        for b in writes:
            self._need(waits, eng, b.w, "waw")
            for ev in b.r:
                self._need(waits, eng, ev, "war")
        if dma:
            if eng == "pool":
                idx = self.n_sp + self.dma_rr_pool
                self.dma_rr_pool = (self.dma_rr_pool + 1) % (self.n_dma - self.n_sp)
            else:
                idx = self.dma_rr
                self.dma_rr = (self.dma_rr + 1) % self.n_sp
            sk = ("dma", idx)
            prev = 16 * self.dma_use[idx]
            if prev > 0 and waits.get(sk, 0) < prev:
                waits[sk] = prev
            self.dma_use[idx] += 1
            ev = (sk, 16 * self.dma_use[idx], eng)
            inc = (sk, 16)
        else:
            self.cnt[eng] += 1
            ev = (eng, self.cnt[eng], eng)
            inc = (eng, 1)
        seen = self.seen[eng]
        wl = []
        for sk, val in waits.items():
            if seen.get(sk, 0) < val:
                seen[sk] = val
                wl.append((sk, val))
        self.ops[eng].append((wl, fn, inc))
        for b in writes:
            b.w = ev
            b.r = []
        for b in reads:
            b.r.append(ev)
        return ev

    def barrier(self):
        allw = [(("dma", i), 16 * self.dma_use[i]) for i in range(self.n_dma) if self.dma_use[i] > 0]
        allw += [(k, self.cnt[k]) for k in self.ENG if self.cnt[k] > 0]
        for k in self.ENG:
            wl = []
            for sk, val in allw:
                if sk == k:
                    continue
                if self.seen[k].get(sk, 0) < val:
                    self.seen[k][sk] = val
                    wl.append((sk, val))
            self.ops[k].append((wl, None, None))

    def finish(self):
        waits = []
        for i in range(self.n_dma):
            if self.dma_use[i] > 0:
                waits.append((("dma", i), 16 * self.dma_use[i]))
        for k in self.ENG:
            if k != "sp" and self.cnt[k] > 0:
                waits.append((k, self.cnt[k]))
        self.ops["sp"].append((waits, None, None))

    def begin(self, st):
        nc = self.nc
        self.sems = {}
        for k in self.ENG:
            self.sems[k] = st.enter_context(nc.semaphore("s_" + k))
        for i in range(self.n_dma):
            self.sems[("dma", i)] = st.enter_context(nc.semaphore("d_%d" % i))

    def flush(self):
        nc = self.nc
        sems = self.sems
        with nc.Block() as block:

            def run(e, lst):
                for wl, fn, inc in lst:
                    for sk, val in wl:
                        e.wait_ge(sems[sk], val)
                    if fn is None:
                        continue
                    fn(e).then_inc(sems[inc[0]], inc[1])

            @block.tensor
            def _(e):
                run(e, self.ops["pe"])

            @block.scalar
            def _(e):
                run(e, self.ops["act"])

            @block.vector
            def _(e):
                run(e, self.ops["dve"])

            @block.gpsimd
            def _(e):
                run(e, self.ops["pool"])

            @block.sync
            def _(e):
                run(e, self.ops["sp"])
        self.ops = {k: [] for k in self.ENG}


class TL:
    def __init__(self, t):
        self.t = t
        self.b = Buf()

    def __getitem__(self, k):
        return self.t[k]


def host_consts():
    c = {}
    idx = np.arange(128)
    su = (idx[:, None] < idx[None, :]).astype(np.float32)
    ui = (idx[:, None] <= idx[None, :]).astype(np.float32)
    sl = (idx[:, None] > idx[None, :]).astype(np.float32)
    bd = ((idx[:, None] // 64) == (idx[None, :] // 64)).astype(np.float32)
    dm = np.zeros((128, 4, 128), np.float64)
    gq = np.zeros((128, 4, 128), np.float64)
    gk = np.zeros((128, 3, 4), np.float64)
    dms = np.zeros((128, 4, 64), np.float64)
    gqs = np.zeros((128, 4, 64), np.float64)
    for h in range(4):
        g = GAM[h]
        dd = idx[None, :] - idx[:, None]
        dm[:, h, :] = np.where(dd >= 0, g ** np.maximum(dd, 0), 0.0)
        gq[:, h, :] = g ** (idx[None, :] + 1.0)
        gk[:16, 0, h] = g ** (15.0 - idx[:16])
        gk[:, 1, h] = g ** (127.0 - idx)
        gk[:64, 2, h] = g ** (3.0 - (idx[:64] % 4))
        j = idx[:64]
        same = (j[:, None] // 4) == (j[None, :] // 4)
        dt_ = (j[None, :] % 4) - (j[:, None] % 4)
        dms[:64, h, :] = np.where(same & (dt_ >= 0), g ** np.maximum(dt_, 0), 0.0)
        gqs[:, h, :] = g ** ((j[None, :] % 4) + 1.0)
    bsel = np.zeros((128, 16, 64), np.float32)
    for b in range(16):
        bsel[:, b, 4 * b:4 * b + 4] = 1.0
    rsel = np.zeros((128, 16), np.float32)
    for r in range(64):
        rsel[r, r // 4] = 1.0
    m4 = np.concatenate([su, ui, su, ui], axis=1)
    blk = ((idx[:, None] // 4) == (idx[None, :] // 4)).astype(np.float32)
    m4s = np.concatenate([su * blk, ui * blk, su * blk, ui * blk], axis=1)
    slb = sl * blk
    parts = [su, ui, sl, bd, m4, m4s, slb, dm.reshape(128, -1), gq.reshape(128, -1), gk.reshape(128, -1),
             dms.reshape(128, -1), gqs.reshape(128, -1), bsel.reshape(128, -1), rsel]
    offs = {}
    o = 0
    names = ["su", "ui", "sl", "bd", "m4", "m4s", "slb", "dm", "gq", "gk", "dms", "gqs", "bsel", "rsel"]
    for n_, p in zip(names, parts):
        offs[n_] = o
        o += p.shape[1]
    cst = np.concatenate([p.astype(np.float32) for p in parts], axis=1)
    inv_freq = (1.0 / (10000.0 ** np.linspace(0.0, 1.0, 128, dtype=np.float32))).astype(np.float32)
    pos = np.concatenate([np.arange(TP), np.tile(16384 + np.arange(4), 16)]).astype(np.float32)
    ang = (pos[None, :] * inv_freq[:, None]).astype(np.float32)
    cs = np.cos(ang.astype(np.float64)).astype(np.float32)
    sn = np.sin(ang.astype(np.float64)).astype(np.float32)
    return cst, offs, cs, sn


CST, COFF, COS_T, SIN_T = host_consts()
NCST = CST.shape[1]


def build(debug=False, stop=None):
    nc = bass.Bass("TRN2", target_bir_lowering=False)
    DT = lambda name, shape, dt=F32, kind="ExternalInput": nc.dram_tensor(name, list(shape), dt, kind=kind).ap()
    xs = DT("xs", [T, D])
    x_own = DT("x_own", [OWN, D])
    sel_d = DT("sel", [128, 2])
    w_in = DT("w_in", [NCH, 128, KC, 128])
    w_out = DT("w_out", [4, 128, KC, 512])
    w_gate = DT("w_gate", [NFB, 128, KC, 128])
    w_up = DT("w_up", [NFB, 128, KC, 128])
    w_down = DT("w_down", [NFB, 128, D])
    gam_d = DT("gam", [3, D])
    mu_d = DT("mu", [128, 27])
    prm_d = DT("prm", [128, 7, 8])
    w2_d = DT("w2", [64, 1024])
    a2_d = DT("a2", [64, 1024])
    g2_d = DT("g2", [160, 1024])
    ssh_d = DT("ssh", [27, 128, 16])
    srw_d = DT("srw", [16, 16, 64, 64])
    srt_d = DT("srt", [16, 4, 256, 256])
    cst_d = DT("cst", [128, NCST])
    cos_d = DT("cos", [128, T])
    sin_d = DT("sin", [128, T])
    y_d = DT("y", [OWN, D], kind="ExternalOutput")
    sho_d = DT("sho", [17, 27 * 128], kind="ExternalOutput")
    rwo_d = DT("rwo", [17, 16, 64, 64], kind="ExternalOutput")
    rto_d = DT("rto", [17, 4, 256, 256], kind="ExternalOutput")
    zT_d = nc.dram_tensor("zT_scr", [NFM * 128, T], F32).ap()
    ztk_d = nc.dram_tensor("ztk_scr", [T, 16 * 128], F32).ap()
    dbg = {}
    if debug:
        dbg["oT"] = DT("dbg_oT", [128, KC, OWN], BF16, kind="ExternalOutput")
        dbg["h"] = DT("dbg_h", [OWN, D], kind="ExternalOutput")

    P = Prog(nc)
    try:
        _build_body(nc, P, debug, stop, locals())
    except _Stop:
        pass
    return nc


class _Stop(Exception):
    pass


def _build_body(nc, P, debug, stop, L):
    globals().update({k: v for k, v in L.items() if k not in ("nc", "P", "debug", "stop")})
    (xs, x_own, sel_d, w_in, w_out, w_gate, w_up, w_down, gam_d, mu_d, prm_d, w2_d, a2_d, g2_d, ssh_d, srw_d, srt_d,
     cst_d, cos_d, sin_d, y_d, sho_d, rwo_d, rto_d, zT_d, ztk_d, dbg) = [L[k] for k in (
        "xs", "x_own", "sel_d", "w_in", "w_out", "w_gate", "w_up", "w_down", "gam_d", "mu_d", "prm_d", "w2_d", "a2_d",
        "g2_d", "ssh_d", "srw_d", "srt_d", "cst_d", "cos_d", "sin_d", "y_d", "sho_d", "rwo_d", "rto_d", "zT_d", "ztk_d", "dbg")]

    def chk(k):
        if stop == k:
            P.finish()
            P.flush()
            raise _Stop()
    with ExitStack() as st:
        P.begin(st)
        def SBs(stack, name, shape, dt=F32):
            return TL(stack.enter_context(nc.sbuf_tensor("sb_" + name, list(shape), dt)))

        def SB(name, shape, dt=F32):
            return SBs(st, name, shape, dt)

        def PS(name, shape, dt=F32):
            return TL(st.enter_context(nc.psum_tensor(name, list(shape), dt)))

        def dma(eng, out, in_, reads=(), writes=()):
            P.op(eng, lambda e: e.dma_start(out=out, in_=in_), reads=[r.b for r in reads], writes=[w.b for w in writes], dma=True)

        def op(eng, fn, reads=(), writes=()):
            P.op(eng, fn, reads=[r.b for r in reads], writes=[w.b for w in writes])

        def mm(out, lhsT, rhs, start, stop, reads, writes):
            op("pe", lambda e: e.matmul(out, lhsT=lhsT, rhs=rhs, start=start, stop=stop), reads, writes)

        def tr(out, in_, ident, reads, writes):
            op("pe", lambda e: e.transpose(out=out, in_=in_, identity=ident), reads, writes)

        def tt(eng, out, a, b, o, reads, writes):
            op(eng, lambda e: e.tensor_tensor(out=out, in0=a, in1=b, op=o), reads, writes)

        def ts(eng, out, a, s1, s2, o0, o1, reads, writes):
            if s2 is None:
                op(eng, lambda e: e.tensor_scalar(out=out, in0=a, scalar1=s1, scalar2=None, op0=o0), reads, writes)
            else:
                op(eng, lambda e: e.tensor_scalar(out=out, in0=a, scalar1=s1, scalar2=s2, op0=o0, op1=o1), reads, writes)

        def stt(out, a, s, b, o0, o1, reads, writes):
            op("dve", lambda e: e.scalar_tensor_tensor(out=out, in0=a, scalar=s, in1=b, op0=o0, op1=o1), reads, writes)

        def act(out, in_, func, reads, writes, bias=0.0, scale=1.0):
            op("act", lambda e: e.activation(out=out, in_=in_, func=func, bias=bias, scale=scale), reads, writes)

        cp_i = [0]

        def cp(out, in_, reads, writes, eng=None):
            if eng is None:
                eng = "act" if cp_i[0] % 2 == 0 else "dve"
                cp_i[0] += 1
            if eng == "act":
                op("act", lambda e: e.copy(out=out, in_=in_), reads, writes)
            else:
                op(eng, lambda e: e.tensor_copy(out=out, in_=in_), reads, writes)

        pbank = [PS("pb%d" % i, [128, 512]) for i in range(6)]
        pbf = [PS("pbf%d" % i, [128, 1024], BF16) for i in range(2)]
        prr = [0]
        pfr = [0]

        def next_bank():
            b_ = pbank[prr[0] % 5]
            prr[0] += 1
            return b_

        def next_bf():
            b_ = pbf[pfr[0] % 2]
            pfr[0] += 1
            return b_
        pacc = pbank[5]

        cst = SB("cst", [128, NCST])
        dma("sp", cst[:], cst_d[:, :], writes=[cst])

        def CS(n_, w, p0=0, p1=128):
            return cst[p0:p1, COFF[n_]:COFF[n_] + w]
        sel = SB("sel", [128, 2])
        dma("sp", sel[:], sel_d[:, :], writes=[sel])
        identf = SB("identf", [128, 128])
        identb = SB("identb", [128, 128], BF16)
        ones = SB("ones", [128, 128])
        op("pool", lambda e: e.memset(identf[:], 0.0), writes=[identf])
        op("pool", lambda e: e.affine_select(out=identf[:], in_=identf[:], pattern=[[-1, 128]], compare_op=ALU.not_equal, fill=1.0, base=0, channel_multiplier=1), reads=[identf], writes=[identf])
        op("dve", lambda e: e.tensor_copy(out=identb[:], in_=identf[:]), reads=[identf], writes=[identb])
        op("dve", lambda e: e.memset(ones[:], 1.0), writes=[ones])
        oT = SB("oT", [128, KC, OWN], BF16)
        sshT = SB("sshT", [128, 27, 16])
        dma("sp", sshT[:], ssh_d.rearrange("c p b -> p c b"), writes=[sshT])
        mu = SB("mu", [128, 27])
        dma("sp", mu[:], mu_d[:, :], writes=[mu])
        prm = SB("prm", [128, 7, 8])
        dma("sp", prm[:], prm_d[:, :, :], writes=[prm])
        omka = SB("omka", [128, 8])
        ts("dve", omka[:], prm[:, 3, :], -1.0, 1.0, ALU.mult, ALU.add, [prm], [omka])
        stat = [SB("stat%d" % i, [128, 4]) for i in range(2)]
        env = {}

        def rms_rstd(src_tile, n, sti):
            junk = env["junk"]
            s_ = stat[sti]
            op("act", lambda e: e.activation(out=junk[0:n, :], in_=src_tile[0:n, :], func=AF.Square, accum_out=s_[0:n, 0:1]), reads=[src_tile], writes=[junk, s_])
            ts("dve", s_[0:n, 1:2], s_[0:n, 0:1], 1.0 / D, 1e-6, ALU.mult, ALU.add, [s_], [s_])
            op("act", lambda e: e.sqrt(out=s_[0:n, 1:2], in_=s_[0:n, 1:2]), reads=[s_], writes=[s_])
            op("dve", lambda e: e.reciprocal(out=s_[0:n, 1:2], in_=s_[0:n, 1:2]), reads=[s_], writes=[s_])
            return s_

        def transpose16(xb_, n, dst, c0):
            for half in range(2):
                pt = next_bf()
                for k8 in range(8):
                    kc = half * 8 + k8
                    tr(pt[:, k8 * 128:k8 * 128 + n], xb_[0:n, kc * 128:(kc + 1) * 128], identb[0:n, 0:n], [xb_, identb], [pt])
                src = pt[:, :].rearrange("p (k c) -> p k c", c=128)[:, :, 0:n]
                cp(dst[:, half * 8:half * 8 + 8, c0:c0 + n], src, [pt], [dst])

        with ExitStack() as s1:
            gam = SBs(s1, "gam1", [128, D])
            env["junk"] = SBs(s1, "junk1", [128, D])
            xnT = SBs(s1, "xnT", [128, KC, T], BF16)
            xt = [SBs(s1, "xt%d" % i, [128, D]) for i in range(2)]
            xnb = [SBs(s1, "xnb%d" % i, [128, D], BF16) for i in range(2)]
            wt = [SBs(s1, "wt%d" % i, [128, KC, 128], BF16) for i in range(4)]
            zst = [SBs(s1, "zst%d" % i, [128, T]) for i in range(2)]
            ztk = [SBs(s1, "ztk%d" % i, [128, 18, 128]) for i in range(1)]
            shs = [SBs(s1, "shs%d" % i, [17, 128]) for i in range(2)]
            xl = SBs(s1, "xl", [128, KC, 17], BF16)
            dma("sp", gam[:], gam_d[0, :].partition_broadcast(128), writes=[gam])
            for ti, (c0, n) in enumerate(TILES):
                x_ = xt[ti % 2]
                xb_ = xnb[ti % 2]
                dma("sp", x_[0:n, :], xs[c0:c0 + n, :], writes=[x_])
                s_ = rms_rstd(x_, n, ti % 2)
                stt(xb_[0:n, :], x_[0:n, :], s_[0:n, 1:2], gam[0:n, :], ALU.mult, ALU.mult, [x_, s_, gam], [xb_])
                transpose16(xb_, n, xnT, c0)
            cp(xl[:, :, 1:17], xnT[:, :, TP + 3:T:4], [xnT], [xl], eng="dve")
            cp(xl[:, :, 0:1], xnT[:, :, TP - 1:TP], [xnT], [xl], eng="dve")
            chk(0)
            for ch in range(NCH):
                w_ = wt[ch % 4]
                dma("pool", w_[:], w_in[ch, :, :, :], writes=[w_])
                if ch < NFM:
                    z_ = zst[ch % 2]
                    scale = (1.0 / 16.0) if (ch >= 27 and (ch - 27) % 4 >= 2) else 1.0
                    for (b0, bn) in BLKS:
                        pb_ = next_bank()
                        for kc in range(KC):
                            mm(pb_[:, 0:bn], w_[:, kc, :], xnT[:, kc, b0:b0 + bn], kc == 0, kc == KC - 1, [w_, xnT], [pb_])
                        if cp_i[0] % 2 == 0:
                            op("act", lambda e, z_=z_, pb_=pb_, b0=b0, bn=bn, scale=scale: e.mul(out=z_[:, b0:b0 + bn], in_=pb_[:, 0:bn], mul=scale), reads=[pb_], writes=[z_])
                        else:
                            ts("dve", z_[:, b0:b0 + bn], pb_[:, 0:bn], scale, None, ALU.mult, None, [pb_], [z_])
                        cp_i[0] += 1
                    dma("sp", zT_d[ch * 128:(ch + 1) * 128, :], z_[:], reads=[z_])
                    if ch < 27:
                        pb_ = next_bank()
                        for kc in range(KC):
                            mm(pb_[0:17, 0:128], xl[:, kc, :], w_[:, kc, :], kc == 0, kc == KC - 1, [w_, xl], [pb_])
                        cp(shs[ch % 2][0:17, :], pb_[0:17, 0:128], [pb_], [shs[ch % 2]])
                        dma("sp", sho_d[0:17, ch * 128:(ch + 1) * 128], shs[ch % 2][0:17, :], reads=[shs[ch % 2]])
                else:
                    zk_ = ztk[0]
                    for g4 in range(5):
                        pb_ = next_bank()
                        tl_ = list(range(g4 * 4, min(g4 * 4 + 4, 18)))
                        for q_, ti in enumerate(tl_):
                            c0, n = TILES[ti]
                            for kc in range(KC):
                                mm(pb_[0:n, q_ * 128:(q_ + 1) * 128], xnT[:, kc, c0:c0 + n], w_[:, kc, :], kc == 0, kc == KC - 1, [w_, xnT], [pb_])
                        nt = len(tl_)
                        src = pb_[:, 0:nt * 128].rearrange("p (q c) -> p q c", c=128)
                        cp(zk_[:, g4 * 4:g4 * 4 + nt, :], src, [pb_], [zk_])
                    cc = ch - NFM
                    dma("sp", ztk_d[0:16, cc * 128:(cc + 1) * 128], zk_[0:16, 0, :], reads=[zk_])
                    dma("sp", ztk_d[16:TP, cc * 128:(cc + 1) * 128].rearrange("(t p) c -> p t c", p=128), zk_[:, 1:17, :], reads=[zk_])
                    dma("sp", ztk_d[TP:T, cc * 128:(cc + 1) * 128], zk_[0:64, 17, :], reads=[zk_])
            P.flush()

        P.barrier()

        with ExitStack() as s2:
            FB = [SBs(s2, "fb%d" % i, [128, T]) for i in range(10)]
            za, zp, rs, ks, vs, sg, av, gT, Cb, kkn = FB
            t1, t2 = za, zp
            bonus = SBs(s2, "bonus", [128, T], BF16)
            junk = SBs(s2, "junk2", [128, 256])
            AR = SBs(s2, "AR", [128, 2, T], BF16)
            KRt = SBs(s2, "KRt", [128, 2, T], BF16)
            vsb = SBs(s2, "vsb", [128, T], BF16)
            la = SBs(s2, "la", [128, T], BF16)
            sgd = SBs(s2, "sgd", [128, T], BF16)
            sgd2 = SBs(s2, "sgd2", [32, T], BF16)
            wa2 = SBs(s2, "wa2", [128, 1024], BF16)
            g2a = SBs(s2, "g2a", [128, 1024], BF16)
            g2b = SBs(s2, "g2b", [32, 1024], BF16)
            pcl = SBs(s2, "pcl", [128, 40])
            dma("pool", wa2[0:64, :], w2_d[:, :], writes=[wa2])
            dma("pool", wa2[64:128, :], a2_d[:, :], writes=[wa2])
            dma("pool", g2a[:, :], g2_d[0:128, :], writes=[g2a])
            dma("pool", g2b[:, :], g2_d[128:160, :], writes=[g2b])

            def load_shift(ch, out, out_dt_tile=None):
                dma("sp", za[:, :], zT_d[ch * 128:(ch + 1) * 128, :], writes=[za])
                tt("dve", zp[:, 1:T], za[:, 0:T - 1], za[:, 1:T], ALU.subtract, [za], [zp])
                ts("dve", zp[:, 0:1], za[:, 0:1], -1.0, None, ALU.mult, None, [za], [zp])
                tt("dve", zp[:, TP:T:4], sshT[:, ch, :], za[:, TP:T:4], ALU.subtract, [sshT, za], [zp])
                stt(out[:, :], zp[:, :], mu[:, ch:ch + 1], za[:, :], ALU.mult, ALU.add, [zp, za, mu], [out])

            load_shift(24, rs)
            act(la[0:64, :], rs[0:64, :], AF.Tanh, [rs], [la])
            cp(la[64:128, :], rs[64:128, :], [rs], [la], eng="dve")
            load_shift(25, ks)
            act(sgd[:, :], ks[:, :], AF.Sigmoid, [ks], [sgd])
            load_shift(26, vs)
            act(sgd2[0:32, :], vs[0:32, :], AF.Sigmoid, [vs], [sgd2])

            chk(2)
            onb = SBs(s2, "onb", [128, 256], BF16)
            of32 = SBs(s2, "of32", [128, 128])
            PSst = SBs(s2, "PSst", [128, 128])
            PSbb = SBs(s2, "PSbb", [128, 128], BF16)

            def write_oT(kc, slot, src, n, src_tiles, first, eng="dve"):
                if slot == "s":
                    return
                if slot[0] == "own":
                    c0 = slot[1]
                    cp(oT[:, kc, c0:c0 + n], src, src_tiles, [oT], eng=eng)
                else:
                    _, s_i, second = slot
                    c0 = s_i * 128
                    if not second:
                        ts(eng, oT[:, kc, c0:c0 + n], src, sel[:, 0:1], None, ALU.mult, None, src_tiles + [sel], [oT])
                    elif eng == "dve":
                        stt(oT[:, kc, c0:c0 + n], src, sel[:, 1:2], oT[:, kc, c0:c0 + n], ALU.mult, ALU.add, src_tiles + [sel, oT], [oT])
                    else:
                        ts(eng, src, src, sel[:, 1:2], None, ALU.mult, None, src_tiles + [sel], src_tiles)
                        tt(eng, oT[:, kc, c0:c0 + n], oT[:, kc, c0:c0 + n], src, ALU.add, src_tiles + [oT], [oT])

            def slot_of(ti, b=None):
                if ti == 0:
                    return "s"
                if ti <= 16:
                    return ("pred", (ti - 1) % 8, ti > 8)
                return ("own", 1024 + 4 * b)

            class Carve:
                def __init__(self, bufs):
                    self.bufs = bufs
                    self.i = 0
                    self.off = 0

                def get(self, shape, dt):
                    nelem = int(np.prod(shape[1:]))
                    nf = (nelem * (2 if dt == BF16 else 4) + 3) // 4
                    nf = (nf + 7) // 8 * 8
                    if self.off + nf > T:
                        self.i += 1
                        self.off = 0
                    fb = self.bufs[self.i]
                    v = fb.t[0:shape[0], self.off:self.off + nf]
                    self.off += nf
                    if dt == BF16:
                        v = v.bitcast(BF16)
                    v = v[:, 0:nelem]
                    if len(shape) == 3:
                        v = v.rearrange("p (a b) -> p a b", b=shape[2])
                    return TL(v)

            class SetNS:
                pass

            import os as _os
            NSET = int(_os.environ.get('KB_NSET', '5'))
            cv = Carve([za, zp, rs, ks, vs, sg, av, Cb, kkn])
            SETS = []
            for si in range(NSET):
                S_ = SetNS()
                S_.tokb = cv.get([128, 3, 128], BF16)
                S_.A4 = [cv.get([128, 4, 128], BF16) for h in range(2)]
                S_.MM = [cv.get([128, 4, 128], BF16) for i in range(2)]
                S_.Xb = cv.get([128, 2, 128], BF16)
                S_.Yt = cv.get([128, 2, 128], BF16)
                S_.Wb = cv.get([128, 128], BF16)
                S_.UTb = cv.get([128, 128], BF16)
                S_.Sbb = cv.get([128, 128], BF16)
                S_.onb = cv.get([128, 128], BF16)
                S_.Sst = cv.get([128, 128], F32)
                S_.tmpS = cv.get([128, 128], F32)
                S_.of32 = cv.get([128, 128], F32)
                S_.bst = cv.get([128, 2, 6], F32)
                S_.mv = cv.get([128, 2, 2], F32)
                S_.Sn = cv.get([64, 2, 64], F32)
                S_.T2 = cv.get([128, 64], F32)
                S_.So = cv.get([64, 2, 64], F32)
                SETS.append(S_)

            SG = SetNS()
            SG.SnA = cv.get([64, 8, 128], F32)
            SG.SstA = cv.get([128, 8, 128], F32)
            SG.SbbA = cv.get([128, 8, 128], BF16)
            SG.am = cv.get([128, 16, 32], BF16)
            SG.tokm = cv.get([128, 8, 256], BF16)
            SG.T2A = cv.get([128, 8, 64], F32)
            SG.tmp4 = cv.get([128, 4, 128], F32)

            free_f = []
            free_b = []

            def wait_f():
                while not free_f:
                    yield

            def wait_b():
                while not free_b:
                    yield

            pstate = {"next": 0}
            eps_gn = SBs(s2, "eps_gn", [128, 1])
            op("dve", lambda e: e.memset(eps_gn[:, :], 64e-5), [], [eps_gn])

            def rwkv_pre(S_, tile, n, lv, mk4, msl):
                tokb, A4, MM, Xb = S_.tokb, S_.A4, S_.MM, S_.Xb
                yield from wait_b()
                pt = free_b.pop(0)
                tr(pt[0:n, 0:128], KRt[:, 0, tile], identb[:, :], [KRt, identb], [pt])
                tr(pt[0:n, 128:256], KRt[:, 1, tile], identb[:, :], [KRt, identb], [pt])
                tr(pt[0:n, 256:384], vsb[:, tile], identb[:, :], [vsb, identb], [pt])
                yield
                cp(tokb[0:n, :, :], pt[0:n, 0:384].rearrange("p (q c) -> p q c", c=128), [pt], [tokb], eng="act")
                free_b.append(pt)
                yield
                msk4 = cst[0:n, COFF[mk4]:COFF[mk4] + 512].rearrange("p (q c) -> p q c", c=128)[:, :, 0:n]
                for h in range(2):
                    hs = slice(64 * h, 64 * h + 64)
                    yield from wait_f()
                    pn = free_f.pop(0)
                    mm(pn[0:n, 0:n], AR[hs, 0, tile], KRt[hs, 0, tile], True, True, [KRt, AR], [pn])
                    yield
                    tt("dve", MM[0][0:n, 2 * h + 1, 0:n], pn[0:n, 0:n], CS(msl, n, 0, n), ALU.mult, [pn, cst], [MM[0]])
                    free_f.append(pn)
                    yield from wait_f()
                    pb_ = free_f.pop(0)
                    for q_ in range(2):
                        mm(pb_[0:n, q_ * 128:q_ * 128 + n], KRt[hs, 0, tile], AR[hs, q_, tile], True, True, [KRt, AR], [pb_])
                        mm(pb_[0:n, 256 + q_ * 128:256 + q_ * 128 + n], KRt[hs, 1, tile], AR[hs, q_, tile], True, True, [KRt, AR], [pb_])
                    yield
                    o4 = pb_[0:n, 0:512].rearrange("p (q c) -> p q c", c=128)[:, :, 0:n]
                    tt("dve", A4[h][0:n, :, 0:n], o4, msk4, ALU.mult, [pb_, cst], [A4[h]])
                    free_f.append(pb_)
                    yield
                    tt("pool", Xb[0:n, h, 0:n], A4[h][0:n, 0, 0:n], identb[0:n, 0:n], ALU.add, [A4[h], identb], [Xb])
                for lev in range(1, lv):
                    cur = MM[(lev - 1) % 2]
                    nxt = MM[lev % 2]
                    last = (lev == lv - 1)
                    yield from wait_f()
                    pb_ = free_f.pop(0)
                    for h in range(2):
                        cM = A4[h][0:n, 0, 0:n] if lev == 1 else cur[0:n, 2 * h, 0:n]
                        cMT = cur[0:n, 2 * h + 1, 0:n]
                        rd = [cur, A4[h]]
                        if not last:
                            mm(pb_[0:n, (2 * h) * 128:(2 * h) * 128 + n], cMT, cM, True, True, rd, [pb_])
                        mm(pb_[0:n, (2 * h + 1) * 128:(2 * h + 1) * 128 + n], cM, cMT, True, True, rd, [pb_])
                    yield
                    if last:
                        for h in range(2):
                            cp(nxt[0:n, 2 * h + 1, 0:n], pb_[0:n, (2 * h + 1) * 128:(2 * h + 1) * 128 + n], [pb_], [nxt], eng=("act" if h == 0 else "dve"))
                    else:
                        cp(nxt[0:n, :, 0:n], pb_[0:n, 0:512].rearrange("p (q c) -> p q c", c=128)[:, :, 0:n], [pb_], [nxt], eng=("act" if lev % 2 == 0 else "dve"))
                    free_f.append(pb_)
                    yield
                    yield from wait_f()
                    px = free_f.pop(0)
                    Yt = S_.Yt
                    tt("pool", Yt[0:n, :, 0:n], nxt[0:n, 1:4:2, 0:n], identb[0:n, 0:n].unsqueeze(1).broadcast_to([n, 2, n]), ALU.add, [nxt, identb], [Yt])
                    yield
                    for h in range(2):
                        mm(px[0:n, h * 128:h * 128 + n], Yt[0:n, h, 0:n], Xb[0:n, h, 0:n], True, True, [Yt, Xb], [px])
                    yield
                    cp(Xb[0:n, :, 0:n], px[0:n, 0:256].rearrange("p (q c) -> p q c", c=128)[:, :, 0:n], [px], [Xb], eng=("dve" if lev % 2 == 0 else "act"))
                    free_f.append(px)
                    yield

            def rwkv_post(S_, po, hp, tile, n, slot):
                bst, mv, onb_, of_ = S_.bst, S_.mv, S_.onb, S_.of32
                for h in range(2):
                    hc = slice(64 * h, 64 * h + 64)
                    op("dve", lambda e, h=h, hc=hc: e.bn_stats(out=bst[0:n, h, :], in_=po[0:n, hc]), [po], [bst])
                    op("dve", lambda e, h=h: e.bn_aggr(out=mv[0:n, h, :], in_=bst[0:n, h, :]), [bst], [mv])
                yield
                if _os.environ.get('KB_LN', '1') == '1':
                    act(mv[0:n, :, 1], mv[0:n, :, 1], AF.Ln, [mv], [mv], bias=eps_gn[0:n, 0:1])
                    act(mv[0:n, :, 1], mv[0:n, :, 1], AF.Exp, [mv], [mv], scale=-0.5)
                else:
                    ts("dve", mv[0:n, :, 1], mv[0:n, :, 1], 64e-5, None, ALU.add, None, [mv], [mv])
                    op("act", lambda e: e.sqrt(out=mv[0:n, :, 1], in_=mv[0:n, :, 1]), [mv], [mv])
                    op("dve", lambda e: e.reciprocal(out=mv[0:n, :, 1], in_=mv[0:n, :, 1]), [mv], [mv])
                yield
                for h in range(2):
                    hc = slice(64 * h, 64 * h + 64)
                    ts("dve", onb_[0:n, hc], po[0:n, hc], mv[0:n, h, 0:1], mv[0:n, h, 1:2], ALU.subtract, ALU.mult, [po, mv], [onb_])
                free_f.append(po)
                yield
                yield from wait_b()
                pt = free_b.pop(0)
                tr(pt[:, 0:n], onb_[0:n, 0:128], identb[0:n, 0:n], [onb_, identb], [pt])
                yield
                op("act", lambda e: e.activation(out=of_[:, 0:n], in_=pt[:, 0:n], func=AF.Identity, bias=prm[:, 6, hp:hp + 1], scale=prm[:, 5, hp:hp + 1]), [pt, prm], [of_])
                free_b.append(pt)
                yield
                tt("pool", of_[:, 0:n], of_[:, 0:n], bonus[:, tile], ALU.add, [of_, bonus], [of_])
                tt("pool", of_[:, 0:n], of_[:, 0:n], gT[:, tile], ALU.mult, [of_, gT], [of_])
                write_oT(hp, slot, of_[:, 0:n], n, [of_], None, eng="pool")
                yield


            def rwkv_task(hp, c0, n, pidx, slot, S_, Sst, Sbb, ti, b):
                tile = slice(c0, c0 + n)
                lv = int(np.ceil(np.log2(n)))
                tokb, A4, MM, Xb, Wb, UTb, tmpS, bst, mv, onb_, of_ = S_.tokb, S_.A4, S_.MM, S_.Xb, S_.Wb, S_.UTb, S_.tmpS, S_.bst, S_.mv, S_.onb, S_.of32
                if b is not None:
                    Sn, T2, So = S_.Sn, S_.T2, S_.So
                    dma("sp", Sn[:, :, :], srw_d[b, 2 * hp:2 * hp + 2, :, :].rearrange("h i j -> i h j"), writes=[Sn])
                    yield from wait_f()
                    pb_ = free_f.pop(0)
                    tr(pb_[:, 0:64], Sn[:, :, :].rearrange("p h j -> p (h j)"), identf[0:64, 0:64], [Sn, identf], [pb_])
                    op("pool", lambda e: e.memset(Sst[:, :], 0.0), [], [Sst])
                    yield
                    cp(Sst[0:64, 0:64], pb_[0:64, 0:64], [pb_], [Sst], eng="dve")
                    cp(Sst[64:128, 64:128], pb_[64:128, 0:64], [pb_], [Sst], eng="act")
                    free_f.append(pb_)
                    yield
                    cp(Sbb[:, :], Sst[:, :], [Sst], [Sbb], eng="pool")
                yield from rwkv_pre(S_, tile, n, lv, "m4", "sl")
                if b is None:
                    while pstate["next"] != ti:
                        yield
                yield from wait_f()
                pw = free_f.pop(0)
                mm(pw[0:n, 0:128], AR[:, 0, tile], Sbb[:, :], True, False, [AR, Sbb], [pw])
                for h in range(2):
                    hc = slice(64 * h, 64 * h + 64)
                    mm(pw[0:n, hc], A4[h][0:n, 2, 0:n], tokb[0:n, 2, hc], False, h == 1, [A4[h], tokb], [pw])
                yield
                cp(Wb[0:n, :], pw[0:n, 0:128], [pw], [Wb], eng="act")
                free_f.append(pw)
                yield
                yield from wait_f()
                pu = free_f.pop(0)
                for h in range(2):
                    hc = slice(64 * h, 64 * h + 64)
                    mm(pu[0:n, hc], Xb[0:n, h, 0:n], Wb[0:n, hc], True, True, [Xb, Wb], [pu])
                yield
                cp(UTb[0:n, :], pu[0:n, 0:128], [pu], [UTb], eng="dve")
                free_f.append(pu)
                yield
                po = None
                if slot != "s":
                    yield from wait_f()
                    po = free_f.pop(0)
                    mm(po[0:n, 0:128], AR[:, 1, tile], Sbb[:, :], True, False, [AR, Sbb], [po])
                    for h in range(2):
                        hc = slice(64 * h, 64 * h + 64)
                        mm(po[0:n, hc], A4[h][0:n, 1, 0:n], UTb[0:n, hc], False, False, [A4[h], UTb], [po])
                        mm(po[0:n, hc], A4[h][0:n, 3, 0:n], tokb[0:n, 2, hc], False, h == 1, [A4[h], tokb], [po])
                yield from wait_f()
                psc = free_f.pop(0)
                mm(psc[:, 0:128], tokb[0:n, 0, :], UTb[0:n, :], True, False, [tokb, UTb], [psc])
                mm(psc[:, 0:128], tokb[0:n, 1, :], tokb[0:n, 2, :], False, True, [tokb], [psc])
                yield
                stt(tmpS[:, :], psc[:, 0:128], pcl[:, pidx:pidx + 1], CS("bd", 128), ALU.mult, ALU.mult, [psc, pcl, cst], [tmpS])
                free_f.append(psc)
                stt(Sst[:, :], Sst[:, :], pcl[:, pidx:pidx + 1], tmpS[:, :], ALU.mult, ALU.add, [Sst, pcl, tmpS], [Sst])
                yield
                cp(Sbb[:, :], Sst[:, :], [Sst], [Sbb], eng="pool")
                if b is None:
                    pstate["next"] = ti + 1
                yield
                if b is not None or ti == 16:
                    Sn, T2, So = S_.Sn, S_.T2, S_.So
                    cp(T2[0:64, :], Sst[0:64, 0:64], [Sst], [T2], eng="pool")
                    cp(T2[64:128, :], Sst[64:128, 64:128], [Sst], [T2], eng="pool")
                    yield from wait_f()
                    pb_ = free_f.pop(0)
                    tr(pb_[0:64, 0:128], T2[:, 0:64], identf[:, :], [T2, identf], [pb_])
                    yield
                    cp(So[:, :, :], pb_[0:64, 0:128].rearrange("p (h j) -> p h j", j=64), [pb_], [So], eng="act")
                    free_f.append(pb_)
                    row = 0 if b is None else 1 + b
                    dma("sp", rwo_d[row, 2 * hp:2 * hp + 2, :, :].rearrange("h i j -> i h j"), So[:, :, :], reads=[So])
                    yield
                if po is None:
                    return
                yield from rwkv_post(S_, po, hp, tile, n, slot)

            def rwkv_sgroup_task(hp, g_i, S_):
                ng, n, lv = 8, 32, 2
                c0 = TP + 32 * g_i
                tile = slice(c0, c0 + n)
                b0 = 8 * g_i
                tokb, A4, MM, Xb, Wb, UTb = S_.tokb, S_.A4, S_.MM, S_.Xb, S_.Wb, S_.UTb
                SnA, SstA, SbbA, am, tokm, T2A, tmp4 = SG.SnA, SG.SstA, SG.SbbA, SG.am, SG.tokm, SG.T2A, SG.tmp4
                while pstate["sg"] != g_i:
                    yield
                for h in range(2):
                    dma("sp", SnA[:, :, h * 64:(h + 1) * 64], srw_d[b0:b0 + ng, 2 * hp + h, :, :].rearrange("b i j -> i b j"), writes=[SnA])
                op("pool", lambda e: e.memset(SstA[:, :, :], 0.0), [], [SstA])
                yield from wait_f()
                pk = free_f.pop(0)
                for bb in range(ng):
                    tr(pk[:, bb * 64:(bb + 1) * 64], SnA[:, bb, :], identf[0:64, 0:64], [SnA, identf], [pk])
                yield
                cp(SstA[0:64, :, 0:64], pk[0:64, 0:512].rearrange("p (b i) -> p b i", i=64), [pk], [SstA], eng="dve")
                cp(SstA[64:128, :, 64:128], pk[64:128, 0:512].rearrange("p (b i) -> p b i", i=64), [pk], [SstA], eng="act")
                free_f.append(pk)
                yield
                cp(SbbA[:, :, :], SstA[:, :, :], [SstA], [SbbA], eng="pool")
                yield
                yield from rwkv_pre(S_, tile, n, lv, "m4s", "slb")
                bselv = cst[:, COFF["bsel"]:COFF["bsel"] + 1024].rearrange("p (b c) -> p b c", c=64)[:, 0:ng, 0:n]
                for w_ in range(2):
                    tt("pool" if w_ else "dve", am[:, w_ * ng:(w_ + 1) * ng, :], AR[:, w_, tile].unsqueeze(1).broadcast_to([128, ng, n]), bselv, ALU.mult, [AR, cst], [am])
                rselv = cst[0:n, COFF["rsel"]:COFF["rsel"] + ng].unsqueeze(2).broadcast_to([n, ng, 256])
                tt("dve", tokm[0:n, :, :], tokb[0:n, 0:2, :].rearrange("p q c -> p (q c)").unsqueeze(1).broadcast_to([n, ng, 256]), rselv, ALU.mult, [tokb, cst], [tokm])
                yield
                yield from wait_f()
                pw = free_f.pop(0)
                for bb in range(ng):
                    mm(pw[0:n, 0:128], am[:, bb, :], SbbA[:, bb, :], bb == 0, False, [am, SbbA], [pw])
                for h in range(2):
                    hc = slice(64 * h, 64 * h + 64)
                    mm(pw[0:n, hc], A4[h][0:n, 2, 0:n], tokb[0:n, 2, hc], False, h == 1, [A4[h], tokb], [pw])
                yield
                cp(Wb[0:n, :], pw[0:n, 0:128], [pw], [Wb], eng="act")
                free_f.append(pw)
                yield
                yield from wait_f()
                pu = free_f.pop(0)
                for h in range(2):
                    hc = slice(64 * h, 64 * h + 64)
                    mm(pu[0:n, hc], Xb[0:n, h, 0:n], Wb[0:n, hc], True, True, [Xb, Wb], [pu])
                yield
                cp(UTb[0:n, :], pu[0:n, 0:128], [pu], [UTb], eng="dve")
                free_f.append(pu)
                yield
                yield from wait_f()
                po = free_f.pop(0)
                for bb in range(ng):
                    mm(po[0:n, 0:128], am[:, ng + bb, :], SbbA[:, bb, :], bb == 0, False, [am, SbbA], [po])
                for h in range(2):
                    hc = slice(64 * h, 64 * h + 64)
                    mm(po[0:n, hc], A4[h][0:n, 1, 0:n], UTb[0:n, hc], False, False, [A4[h], UTb], [po])
                    mm(po[0:n, hc], A4[h][0:n, 3, 0:n], tokb[0:n, 2, hc], False, h == 1, [A4[h], tokb], [po])
                yield
                for q_ in range(2):
                    yield from wait_f()
                    psc = free_f.pop(0)
                    for b4 in range(4):
                        bb = 4 * q_ + b4
                        mm(psc[:, b4 * 128:(b4 + 1) * 128], tokm[0:n, bb, 0:128], UTb[0:n, :], True, False, [tokm, UTb], [psc])
                        mm(psc[:, b4 * 128:(b4 + 1) * 128], tokm[0:n, bb, 128:256], tokb[0:n, 2, :], False, True, [tokm, tokb], [psc])
                    yield
                    pcb = pcl[:, 17 + b0 + 4 * q_:17 + b0 + 4 * q_ + 4].unsqueeze(2).broadcast_to([128, 4, 128])
                    bdb = CS("bd", 128).unsqueeze(1).broadcast_to([128, 4, 128])
                    tt("dve", tmp4[:, :, :], psc[:, 0:512].rearrange("p (b c) -> p b c", c=128), pcb, ALU.mult, [psc, pcl], [tmp4])
                    free_f.append(psc)
                    tt("pool", tmp4[:, :, :], tmp4[:, :, :], bdb, ALU.mult, [tmp4, cst], [tmp4])
                    tt("dve", SstA[:, 4 * q_:4 * q_ + 4, :], SstA[:, 4 * q_:4 * q_ + 4, :], pcb, ALU.mult, [SstA, pcl], [SstA])
                    yield
                    tt("pool", SstA[:, 4 * q_:4 * q_ + 4, :], SstA[:, 4 * q_:4 * q_ + 4, :], tmp4[:, :, :], ALU.add, [SstA, tmp4], [SstA])
                    yield
                cp(T2A[0:64, :, :], SstA[0:64, :, 0:64], [SstA], [T2A], eng="dve")
                cp(T2A[64:128, :, :], SstA[64:128, :, 64:128], [SstA], [T2A], eng="pool")
                yield
                for q_ in range(2):
                    yield from wait_f()
                    pk = free_f.pop(0)
                    for b4 in range(4):
                        tr(pk[0:64, b4 * 128:(b4 + 1) * 128], T2A[:, 4 * q_ + b4, :], identf[:, :], [T2A, identf], [pk])
                    yield
                    cp(SnA[:, 4 * q_:4 * q_ + 4, :], pk[0:64, 0:512].rearrange("p (b c) -> p b c", c=128), [pk], [SnA], eng="act")
                    free_f.append(pk)
                    yield
                for h in range(2):
                    dma("sp", rwo_d[1 + b0:1 + b0 + ng, 2 * hp + h, :, :].rearrange("b i j -> i b j"), SnA[:, :, h * 64:(h + 1) * 64], reads=[SnA])
                yield
                pstate["sg"] = g_i + 1
                yield from rwkv_post(S_, po, hp, tile, n, ("own", 1024 + 32 * g_i))

            def run_tasks(makers, sets):
                active = []
                makers = list(makers)
                free_sets = list(sets)
                while makers or active:
                    while makers and free_sets:
                        S_ = free_sets.pop(0)
                        active.append((makers.pop(0)(S_), S_))
                    for item in list(active):
                        try:
                            next(item[0])
                        except StopIteration:
                            active.remove(item)
                            free_sets.append(item[1])

            for hp in range(8):
                load_shift(hp, rs)
                load_shift(8 + hp, ks)
                load_shift(16 + hp, vs)
                for (b0, bn) in BLKS:
                    bl = slice(b0, b0 + bn)
                    pb_ = next_bank()
                    mm(pb_[:, 0:bn], wa2[0:64, hp * 128:(hp + 1) * 128], la[0:64, bl], True, True, [wa2, la], [pb_])
                    act(sg[:, bl], pb_[:, 0:bn], AF.Sigmoid, [pb_, prm], [sg], bias=prm[:, 0, hp:hp + 1])
                    pb_ = next_bank()
                    mm(pb_[:, 0:bn], wa2[64:128, hp * 128:(hp + 1) * 128], la[64:128, bl], True, True, [wa2, la], [pb_])
                    act(av[:, bl], pb_[:, 0:bn], AF.Sigmoid, [pb_, prm], [av], bias=prm[:, 1, hp:hp + 1])
                    pb_ = next_bank()
                    mm(pb_[:, 0:bn], g2a[:, hp * 128:(hp + 1) * 128], sgd[:, bl], True, False, [g2a, sgd], [pb_])
                    mm(pb_[:, 0:bn], g2b[0:32, hp * 128:(hp + 1) * 128], sgd2[0:32, bl], False, True, [g2b, sgd2], [pb_])
                    cp(gT[:, bl], pb_[:, 0:bn], [pb_], [gT])
                ts("dve", t1[:, :], ks[:, :], prm[:, 2, hp:hp + 1], None, ALU.mult, None, [ks, prm], [t1])
                op("act", lambda e: e.square(out=t2[:, :], in_=t1[:, :]), [t1], [t2])
                for (b0, bn) in BLKS:
                    bl = slice(b0, b0 + bn)
                    pb_ = next_bank()
                    mm(pb_[:, 0:bn], CS("bd", 128), t2[:, bl], True, True, [cst, t2], [pb_])
                    op("act", lambda e, pb_=pb_, bl=bl, bn=bn: e.sqrt(out=kkn[:, bl], in_=pb_[:, 0:bn]), [pb_], [kkn])
                ts("dve", kkn[:, :], kkn[:, :], 1e-12, None, ALU.max, None, [kkn], [kkn])
                op("dve", lambda e: e.reciprocal(out=kkn[:, :], in_=kkn[:, :]), [kkn], [kkn])
                tt("pool", kkn[:, :], kkn[:, :], t1[:, :], ALU.mult, [kkn, t1], [kkn])
                tt("pool", t2[:, :], kkn[:, :], av[:, :], ALU.mult, [kkn, av], [t2])
                ts("dve", t1[:, :], av[:, :], prm[:, 3, hp:hp + 1], omka[:, hp:hp + 1], ALU.mult, ALU.add, [av, prm, omka], [t1])
                tt("dve", ks[:, :], ks[:, :], t1[:, :], ALU.mult, [ks, t1], [ks])
                stt(t1[:, :], rs[:, :], prm[:, 4, hp:hp + 1], ks[:, :], ALU.mult, ALU.mult, [rs, prm, ks], [t1])
                for (b0, bn) in BLKS:
                    bl = slice(b0, b0 + bn)
                    pb_ = next_bank()
                    mm(pb_[:, 0:bn], CS("bd", 128), t1[:, bl], True, True, [cst, t1], [pb_])
                    tt("dve", bonus[:, bl], pb_[:, 0:bn], vs[:, bl], ALU.mult, [pb_, vs], [bonus])
                for ti in range(17):
                    c0, n = TILES[ti]
                    op("dve", lambda e, c0=c0, n=n: e.tensor_tensor_scan(out=Cb[:, c0:c0 + n], data0=ones[:, 0:n], data1=sg[:, c0:c0 + n], initial=0.0, op0=ALU.mult, op1=ALU.add), [ones, sg], [Cb])
                cp(Cb[:, TP:T:4], sg[:, TP:T:4], [sg], [Cb], eng="dve")
                for t_ in range(1, 4):
                    tt("dve", Cb[:, TP + t_:T:4], Cb[:, TP + t_ - 1:T:4], sg[:, TP + t_:T:4], ALU.add, [Cb, sg], [Cb])
                act(t1[:, :], Cb[:, :], AF.Exp, [Cb], [t1], scale=-C0)
                for ti in range(17):
                    c0, n = TILES[ti]
                    cp(pcl[:, ti:ti + 1], t1[:, c0 + n - 1:c0 + n], [t1], [pcl], eng="dve")
                cp(pcl[:, 17:33], t1[:, TP + 3:T:4], [t1], [pcl], eng="dve")
                tt("pool", AR[:, 1, :], rs[:, :], t1[:, :], ALU.mult, [rs, t1], [AR])
                tt("dve", sg[:, :], Cb[:, :], sg[:, :], ALU.subtract, [Cb, sg], [sg])
                act(sg[:, :], sg[:, :], AF.Exp, [sg], [sg], scale=-C0)
                stt(AR[:, 0, :], kkn[:, :], -1.0, sg[:, :], ALU.mult, ALU.mult, [kkn, sg], [AR])
                act(rs[:, :], Cb[:, :], AF.Exp, [Cb, AR], [rs], scale=C0)
                tt("dve", KRt[:, 0, :], t2[:, :], rs[:, :], ALU.mult, [t2, rs], [KRt])
                tt("pool", KRt[:, 1, :], ks[:, :], rs[:, :], ALU.mult, [ks, rs], [KRt])
                cp(vsb[:, :], vs[:, :], [vs], [vsb], eng="act")
                op("dve", lambda e: e.memset(PSst[:, :], 0.0), [], [PSst])
                op("dve", lambda e: e.memset(PSbb[:, :], 0.0), [], [PSbb])
                P.barrier()
                free_f[:] = list(pbank[0:6])
                free_b[:] = list(pbf)
                pstate["next"] = 0
                pstate["sg"] = 0
                gens = []
                order = []
                for i in range(17):
                    order.append(("p", i))
                    if i in (1, 3):
                        order.append(("s", i // 2))
                order = order[:int(_os.environ.get('KB_NT', '99'))]
                for k_, (kind, i) in enumerate(order):
                    if kind == "p":
                        c0, n = TILES[i]
                        gens.append(lambda S_, c0=c0, n=n, i=i: rwkv_task(hp, c0, n, i, slot_of(i), S_, PSst, PSbb, i, None))
                    else:
                        gens.append(lambda S_, i=i: rwkv_sgroup_task(hp, i, S_))
                run_tasks(gens, SETS)
                chk(3)
                P.barrier()
                prr[0] = 0
                pfr[0] = 0
            chk(7)
            qe, qo, ke, ko = rs, ks, vs, sg
            cosT, sinT = av, gT
            QR = AR
            dma("sp", cosT[:, :], cos_d[:, :], writes=[cosT])
            dma("sp", sinT[:, :], sin_d[:, :], writes=[sinT])
            Sr = [SBs(s2, "Sr%d" % i, [128, 256]) for i in range(2)]
            Srb = [SBs(s2, "Srb%d" % i, [128, 256], BF16) for i in range(2)]
            QS = SBs(s2, "QS", [128, 2, 64])
            qsm = TL(Cb[:, 0:2048].rearrange("p (d b c) -> p d b c", d=2, b=16))
            qsm.b = Cb.b
            ktm = SBs(s2, "ktm", [64, 16, 256], BF16)
            Sin = [TL(kkn[:, i * 512:(i + 1) * 512].rearrange("p (d e) -> p d e", d=2)) for i in range(3)]
            op("dve", lambda e: e.memset(kkn[:, 0:1], 0.0), [], [kkn])
            for s_i in Sin:
                s_i.b.w = kkn.b.w
            rjunk = SBs(s2, "rjunk", [128, 256])

            def make_rset(get):
                R_ = SetNS()
                R_.gt = get([128, 256], F32)
                R_.of32 = get([128, 128], F32)
                R_.rstat = get([128, 4], F32)
                R_.vtb = get([128, 256], BF16)
                R_.ktk = get([128, 2, 128], BF16)
                R_.scb = get([128, 128], BF16)
                R_.qsb = get([128, 2, 128], BF16)
                R_.onb = get([128, 256], BF16)
                return R_
            rcv = Carve([za, zp])
            RSETS = [make_rset(rcv.get) for _ in range(4)]
            rcnt = [0]

            def real_get(shape, dt):
                rcnt[0] += 1
                return SBs(s2, "rs5_%d" % rcnt[0], shape, dt)
            RSETS.append(make_rset(real_get))
            rstate = {"next": 0}

            def ret_opost(po, n, h, slot, R_):
                g_, rstat, onb_, of_ = R_.gt, R_.rstat, R_.onb, R_.of32
                op("act", lambda e: e.activation(out=rjunk[0:n, 0:256], in_=po[0:n, 0:256], func=AF.Square, accum_out=rstat[0:n, 0:1]), [po], [rjunk, rstat])
                yield
                ts("dve", rstat[0:n, 1:2], rstat[0:n, 0:1], 1.0 / 256.0, 1e-6, ALU.mult, ALU.add, [rstat], [rstat])
                yield
                act(rstat[0:n, 1:2], rstat[0:n, 1:2], AF.Ln, [rstat], [rstat])
                act(rstat[0:n, 1:2], rstat[0:n, 1:2], AF.Exp, [rstat], [rstat], scale=-0.5)
                act(g_[0:n, :], g_[0:n, :], AF.Silu, [g_], [g_])
                yield
                stt(onb_[0:n, :], po[0:n, 0:256], rstat[0:n, 1:2], g_[0:n, :], ALU.mult, ALU.mult, [po, rstat, g_], [onb_])
                yield

            def ret_opost2(n, h, slot, R_):
                onb_, of_ = R_.onb, R_.of32
                yield from wait_b()
                pt = free_b.pop(0)
                for j in range(2):
                    tr(pt[:, j * 128:j * 128 + n], onb_[0:n, j * 128:(j + 1) * 128], identb[0:n, 0:n], [onb_, identb], [pt])
                yield
                for j in range(2):
                    cp(of_[:, 0:n], pt[:, j * 128:j * 128 + n], [pt], [of_], eng="act")
                    yield
                    write_oT(8 + 2 * h + j, slot, of_[:, 0:n], n, [of_], None, eng="dve")
                    yield
                free_b.append(pt)

            def ret_task(h, ti, R_):
                g = GAM[h]
                c0, n = TILES[ti]
                tile = slice(c0, c0 + n)
                g_, vtb, ktk, scb, qsb = R_.gt, R_.vtb, R_.ktk, R_.scb, R_.qsb
                dma("pool", vtb[0:n, :], ztk_d[c0:c0 + n, 2 * h * 128:(2 * h + 2) * 128], writes=[vtb])
                dma("sp", g_[0:n, :], ztk_d[c0:c0 + n, 1024 + 2 * h * 128:1024 + (2 * h + 2) * 128], writes=[g_])
                var = 0 if ti == 0 else 1
                yield from wait_b()
                pt = free_b.pop(0)
                for dc in range(2):
                    tr(pt[0:n, dc * 128:(dc + 1) * 128], KRt[:, dc, tile], identb[:, :], [KRt, identb], [pt])
                yield
                ts("dve", ktk[0:n, :, :], pt[0:n, 0:256].rearrange("p (q c) -> p q c", c=128), cst[0:n, COFF["gk"] + var * 4 + h:COFF["gk"] + var * 4 + h + 1], None, ALU.mult, None, [pt, cst], [ktk])
                free_b.append(pt)
                yield
                yield from wait_f()
                ps_ = free_f.pop(0)
                mm(ps_[0:n, 0:n], KRt[:, 0, tile], QR[:, 0, tile], True, False, [KRt, QR], [ps_])
                mm(ps_[0:n, 0:n], KRt[:, 1, tile], QR[:, 1, tile], False, True, [KRt, QR], [ps_])
                yield
                dmv = cst[0:n, COFF["dm"]:COFF["dm"] + 512].rearrange("p (h i) -> p h i", i=128)[:, h, 0:n]
                tt("dve", scb[0:n, 0:n], ps_[0:n, 0:n], dmv, ALU.mult, [ps_, cst], [scb])
                free_f.append(ps_)
                gqv = cst[:, COFF["gq"]:COFF["gq"] + 512].rearrange("p (h i) -> p h i", i=128)[:, h, 0:n]
                tt("dve", qsb[:, 0, 0:n], QR[:, 0, tile], gqv, ALU.mult, [QR, cst], [qsb])
                tt("pool", qsb[:, 1, 0:n], QR[:, 1, tile], gqv, ALU.mult, [QR, cst], [qsb])
                yield
                while rstate["next"] != ti:
                    yield
                po = None
                if ti > 0:
                    yield from wait_f()
                    po = free_f.pop(0)
                    mm(po[0:n, 0:256], scb[0:n, 0:n], vtb[0:n, :], True, False, [scb, vtb], [po])
                    mm(po[0:n, 0:256], qsb[:, 0, 0:n], Srb[0][:, :], False, False, [qsb, Srb[0]], [po])
                    mm(po[0:n, 0:256], qsb[:, 1, 0:n], Srb[1][:, :], False, True, [qsb, Srb[1]], [po])
                for dc in range(2):
                    yield from wait_f()
                    pS = free_f.pop(0)
                    mm(pS[:, 0:256], ktk[0:n, dc, :], vtb[0:n, :], True, True, [ktk, vtb], [pS])
                    yield
                    stt(Sr[dc][:, :], Sr[dc][:, :], float(g ** n), pS[:, 0:256], ALU.mult, ALU.add, [Sr[dc], pS], [Sr[dc]])
                    free_f.append(pS)
                    cp(Srb[dc][:, :], Sr[dc][:, :], [Sr[dc]], [Srb[dc]], eng="act")
                rstate["next"] = ti + 1
                yield
                if ti == 16:
                    for dc in range(2):
                        dma("sp", rto_d[0, h, :, :].rearrange("(m two) e -> m two e", two=2)[:, dc, :], Sr[dc][:, :], reads=[Sr[dc]])
                if po is not None:
                    yield from ret_opost(po, n, h, slot_of(ti), R_)
                    free_f.append(po)
                    yield from ret_opost2(n, h, slot_of(ti), R_)

            def ret_sample_task(h, R_):
                g = GAM[h]
                ti = 17
                c0, n = TILES[ti]
                tile = slice(c0, c0 + n)
                g_, vtb, ktk, scb = R_.gt, R_.vtb, R_.ktk, R_.scb
                dma("pool", vtb[0:n, :], ztk_d[c0:c0 + n, 2 * h * 128:(2 * h + 2) * 128], writes=[vtb])
                dma("sp", g_[0:n, :], ztk_d[c0:c0 + n, 1024 + 2 * h * 128:1024 + (2 * h + 2) * 128], writes=[g_])
                yield from wait_b()
                pt = free_b.pop(0)
                for dc in range(2):
                    tr(pt[0:n, dc * 128:(dc + 1) * 128], KRt[:, dc, tile], identb[:, :], [KRt, identb], [pt])
                yield
                ts("dve", ktk[0:n, :, :], pt[0:n, 0:256].rearrange("p (q c) -> p q c", c=128), cst[0:n, COFF["gk"] + 2 * 4 + h:COFF["gk"] + 2 * 4 + h + 1], None, ALU.mult, None, [pt, cst], [ktk])
                free_b.append(pt)
                yield
                yield from wait_f()
                ps_ = free_f.pop(0)
                mm(ps_[0:n, 0:n], KRt[:, 0, tile], QR[:, 0, tile], True, False, [KRt, QR], [ps_])
                mm(ps_[0:n, 0:n], KRt[:, 1, tile], QR[:, 1, tile], False, True, [KRt, QR], [ps_])
                yield
                dmv = cst[0:n, COFF["dms"]:COFF["dms"] + 256].rearrange("p (h i) -> p h i", i=64)[:, h, 0:n]
                tt("dve", scb[0:n, 0:n], ps_[0:n, 0:n], dmv, ALU.mult, [ps_, cst], [scb])
                free_f.append(ps_)
                gqsv = cst[:, COFF["gqs"]:COFF["gqs"] + 256].rearrange("p (h i) -> p h i", i=64)[:, h, :]
                bselv = cst[:, COFF["bsel"]:COFF["bsel"] + 1024].rearrange("p (b c) -> p b c", c=64)
                for dc in range(2):
                    tt("pool", QS[:, dc, :], QS[:, dc, :], gqsv, ALU.mult, [QS, cst], [QS])
                    tt("pool", qsm[:, dc, :, :], QS[:, dc, :].unsqueeze(1).broadcast_to([128, 16, 64]), bselv, ALU.mult, [QS, cst], [qsm])
                yield
                for b in range(16):
                    ts("dve", ktm[0:64, b, :], ktk[0:64, :, :].rearrange("p q c -> p (q c)"), cst[0:64, COFF["rsel"] + b:COFF["rsel"] + b + 1], None, ALU.mult, None, [ktk, cst], [ktm])
                    yield
                mm(pacc[0:64, 0:256], scb[0:64, 0:64], vtb[0:64, :], True, False, [scb, vtb], [pacc])
                for b in range(16):
                    si = Sin[b % 3]
                    dma("sp", si[:, :, :], srt_d[b, h, :, :].rearrange("(m two) e -> m two e", two=2), writes=[si])
                    yield
                    for dc in range(2):
                        mm(pacc[0:64, 0:256], qsm[:, dc, b, :], si[:, dc, :], False, (b == 15 and dc == 1), [qsm, si], [pacc])
                    for dc in range(2):
                        yield from wait_f()
                        pS = free_f.pop(0)
                        mm(pS[:, 0:256], ktm[0:64, b, dc * 128:(dc + 1) * 128], vtb[0:64, :], True, True, [ktm, vtb], [pS])
                        yield
                        stt(si[:, dc, :], si[:, dc, :], float(g ** 4), pS[:, 0:256], ALU.mult, ALU.add, [si, pS], [si])
                        free_f.append(pS)
                    dma("sp", rto_d[1 + b, h, :, :].rearrange("(m two) e -> m two e", two=2), si[:, :, :], reads=[si])
                    yield
                yield from ret_opost(pacc, 64, h, ("own", 1024), R_)
                yield from ret_opost2(64, h, ("own", 1024), R_)

            for h in range(4):
                for i_, dst in enumerate((qe, qo, ke, ko)):
                    ch = 27 + 4 * h + i_
                    dma("sp", dst[:, :], zT_d[ch * 128:(ch + 1) * 128, :], writes=[dst])
                for (xe, xo, OUT) in ((qe, qo, QR), (ke, ko, KRt)):
                    tt("dve", t1[:, :], xe[:, :], cosT[:, :], ALU.mult, [xe, cosT], [t1])
                    tt("pool", t2[:, :], xo[:, :], sinT[:, :], ALU.mult, [xo, sinT], [t2])
                    tt("dve", OUT[:, 0, :], t1[:, :], t2[:, :], ALU.subtract, [t1, t2], [OUT])
                    if OUT is QR:
                        tt("dve", QS[:, 0, :], t1[:, TP:T], t2[:, TP:T], ALU.subtract, [t1, t2], [QS])
                    tt("dve", t1[:, :], xe[:, :], sinT[:, :], ALU.mult, [xe, sinT], [t1])
                    tt("pool", t2[:, :], xo[:, :], cosT[:, :], ALU.mult, [xo, cosT], [t2])
                    tt("dve", OUT[:, 1, :], t1[:, :], t2[:, :], ALU.add, [t1, t2], [OUT])
                    if OUT is QR:
                        tt("dve", QS[:, 1, :], t1[:, TP:T], t2[:, TP:T], ALU.add, [t1, t2], [QS])
                for dc in range(2):
                    op("dve", lambda e, dc=dc: e.memset(Sr[dc][:, :], 0.0), [], [Sr[dc]])
                    op("dve", lambda e, dc=dc: e.memset(Srb[dc][:, :], 0.0), [], [Srb[dc]])
                P.barrier()
                free_f[:] = list(pbank[0:5])
                free_b[:] = list(pbf)
                rstate["next"] = 0
                makers = [(lambda R_, h=h: ret_sample_task(h, R_))]
                for ti in range(17):
                    makers.append(lambda R_, h=h, ti=ti: ret_task(h, ti, R_))
                run_tasks(makers, RSETS)
                P.barrier()
                prr[0] = 0
                pfr[0] = 0
            chk(8)
            if debug:
                dma("sp", dbg["oT"][:, :, :], oT[:], reads=[oT])
            P.flush()
        P.barrier()

        with ExitStack() as s3:
            gam = SBs(s3, "gam3", [128, D])
            env["junk"] = SBs(s3, "junk3", [128, D])
            hres = [SBs(s3, "h%d" % i, [128, D]) for i in range(9)]
            hnT = SBs(s3, "hnT", [128, KC, OWN], BF16)
            OT = [(i * 128, 128) for i in range(8)] + [(1024, 64)]
            with ExitStack() as s3a:
                wo = SBs(s3a, "wo", [128, KC, 512], BF16)
                xo_ = [SBs(s3a, "xo%d" % i, [128, 512]) for i in range(2)]
                for blk in range(4):
                    dma("pool", wo[:], w_out[blk, :, :, :], writes=[wo])
                    for ti, (c0, n) in enumerate(OT):
                        x_ = xo_[(blk * 9 + ti) % 2]
                        dma("sp", x_[0:n, :], x_own[c0:c0 + n, blk * 512:(blk + 1) * 512], writes=[x_])
                        pb_ = next_bank()
                        for kc in range(KC):
                            mm(pb_[0:n, :], oT[:, kc, c0:c0 + n], wo[:, kc, :], kc == 0, kc == KC - 1, [oT, wo], [pb_])
                        tt("dve", hres[ti][0:n, blk * 512:(blk + 1) * 512], pb_[0:n, :], x_[0:n, :], ALU.add, [pb_, x_], [hres[ti]])
                if debug:
                    for ti, (c0, n) in enumerate(OT):
                        dma("sp", dbg["h"][c0:c0 + n, :], hres[ti][0:n, :], reads=[hres[ti]])
                P.flush()
            P.barrier()
            with ExitStack() as s3b:
                hb = [SBs(s3b, "hb%d" % i, [128, D], BF16) for i in range(2)]
                dma("sp", gam[:], gam_d[1, :].partition_broadcast(128), writes=[gam])
                for ti, (c0, n) in enumerate(OT):
                    s_ = rms_rstd(hres[ti], n, ti % 2)
                    hb_ = hb[ti % 2]
                    stt(hb_[0:n, :], hres[ti][0:n, :], s_[0:n, 1:2], gam[0:n, :], ALU.mult, ALU.mult, [hres[ti], s_, gam], [hb_])
                    transpose16(hb_, n, hnT, c0)
                P.flush()
            P.barrier()
            with ExitStack() as s3c:
                G = 4
                wg = [SBs(s3c, "wg%d" % i, [128, KC, 128], BF16) for i in range(2)]
                wu = [SBs(s3c, "wu%d" % i, [128, KC, 128], BF16) for i in range(2)]
                actT = SBs(s3c, "actT", [128, G, OWN], BF16)
                sgl = SBs(s3c, "sgl", [128, 512])
                oflat = oT[:, :, :].rearrange("p k c -> p (k c)")
                wd = []
                for i in range(2 * G):
                    w_ = TL(oflat[:, i * D:(i + 1) * D])
                    w_.b.r = list(oT.b.r)
                    w_.b.w = oT.b.w
                    wd.append(w_)
                RB = [(0, 512), (512, 512), (1024, 64)]
                for grp in range(NFB // G):
                    for fi in range(G):
                        fb = grp * G + fi
                        g_ = wg[fb % 2]
                        u_ = wu[fb % 2]
                        d_ = wd[fb % (2 * G)]
                        dma("pool", g_[:], w_gate[fb, :, :, :], writes=[g_])
                        dma("pool", u_[:], w_up[fb, :, :, :], writes=[u_])
                        dma("pool", d_[:, :], w_down[fb, :, :], writes=[d_])
                        for (r0, rn) in RB:
                            pg = next_bank()
                            for kc in range(KC):
                                mm(pg[:, 0:rn], g_[:, kc, :], hnT[:, kc, r0:r0 + rn], kc == 0, kc == KC - 1, [g_, hnT], [pg])
                            pu_ = next_bank()
                            for kc in range(KC):
                                mm(pu_[:, 0:rn], u_[:, kc, :], hnT[:, kc, r0:r0 + rn], kc == 0, kc == KC - 1, [u_, hnT], [pu_])
                            act(sgl[:, 0:rn], pg[:, 0:rn], AF.Silu, [pg], [sgl])
                            tt("dve", actT[:, fi, r0:r0 + rn], sgl[:, 0:rn], pu_[:, 0:rn], ALU.mult, [sgl, pu_], [actT])
                    for ti, (c0, n) in enumerate(OT):
                        for blk in range(4):
                            pb_ = next_bank()
                            for fi in range(G):
                                d_ = wd[(grp * G + fi) % (2 * G)]
                                mm(pb_[0:n, :], actT[:, fi, c0:c0 + n], d_[:, blk * 512:(blk + 1) * 512], fi == 0, fi == G - 1, [actT, d_], [pb_])
                            tt("dve", hres[ti][0:n, blk * 512:(blk + 1) * 512], hres[ti][0:n, blk * 512:(blk + 1) * 512], pb_[0:n, :], ALU.add, [hres[ti], pb_], [hres[ti]])
                dma("sp", gam[:], gam_d[2, :].partition_broadcast(128), writes=[gam])
                for ti, (c0, n) in enumerate(OT):
                    s_ = rms_rstd(hres[ti], n, ti % 2)
                    stt(hres[ti][0:n, :], hres[ti][0:n, :], s_[0:n, 1:2], gam[0:n, :], ALU.mult, ALU.mult, [hres[ti], s_, gam], [hres[ti]])
                    dma("sp", y_d[c0:c0 + n, :], hres[ti][0:n, :], reads=[hres[ti]])
                P.finish()
                P.flush()


_NC_CACHE = {}


def _prep_weights(inp):
    f = lambda a: np.ascontiguousarray(np.asarray(a, dtype=np.float32))
    w_in = f(inp["w_in"])[0]
    cols = list(range(0, 3360)) + [-1] * 96
    for h in range(4):
        for base in (3360, 3360 + 1024):
            cols += list(range(base + h * 256, base + (h + 1) * 256, 2))
            cols += list(range(base + h * 256 + 1, base + (h + 1) * 256, 2))
    fm = list(range(0, 3360)) + [-1] * 96
    for h in range(4):
        qb, kb = 3360 + h * 256, 3360 + 1024 + h * 256
        fm += list(range(qb, qb + 256, 2)) + list(range(qb + 1, qb + 256, 2))
        fm += list(range(kb, kb + 256, 2)) + list(range(kb + 1, kb + 256, 2))
    fm += list(range(3360 + 2048, 3360 + 4096))
    fm = np.array(fm)
    wp = np.zeros((D, NCH * 128), np.float32)
    valid = fm >= 0
    wp[:, valid] = w_in[:, fm[valid]]
    w_in_r = np.ascontiguousarray(wp.reshape(KC, 128, NCH, 128).transpose(2, 1, 0, 3))
    w_out = f(inp["w_out"])[0]
    w_out_r = np.ascontiguousarray(w_out.reshape(KC, 128, 4, 512).transpose(2, 1, 0, 3))
    wg = np.ascontiguousarray(f(inp["w_gate"])[0].reshape(KC, 128, NFB, 128).transpose(2, 1, 0, 3))
    wu = np.ascontiguousarray(f(inp["w_up"])[0].reshape(KC, 128, NFB, 128).transpose(2, 1, 0, 3))
    wd = np.ascontiguousarray(f(inp["w_down"])[0].reshape(NFB, 128, D))
    gam = np.stack([f(inp["norm_mix"])[0], f(inp["norm_ffn"])[0], f(inp["norm_final"])])
    mu = np.zeros((27 * 128,), np.float32)
    mu[:3360] = f(inp["rwkv_mu"])[0]
    mu = np.ascontiguousarray(mu.reshape(27, 128).T)
    names = ["rwkv_w0", "rwkv_a0", "rwkv_kk", "rwkv_ka", "rwkv_rk", "rwkv_ln_w", "rwkv_ln_b"]
    prm = np.stack([f(inp[n_])[0].reshape(8, 128).T for n_ in names], axis=1)
    return dict(w_in=w_in_r, w_out=w_out_r, w_gate=wg, w_up=wu, w_down=wd, gam=np.ascontiguousarray(gam), mu=mu,
                prm=np.ascontiguousarray(prm), w2=f(inp["rwkv_w2"])[0], a2=f(inp["rwkv_a2"])[0], g2=f(inp["rwkv_g2"])[0],
                cst=CST, cos=COS_T, sin=SIN_T)


def kernel(**inp):
    f = lambda a: np.asarray(a, dtype=np.float32)
    shared = _prep_weights(inp)
    xp = f(inp["x_prompt"])
    xsm = f(inp["x_sample"])
    meta = f(inp["meta_tokens"])
    ssh = f(inp["state_shift"])[0]
    srw = f(inp["state_rwkv"])[0]
    srt = f(inp["state_ret"])[0]
    in_maps = []
    for c in range(8):
        b, hh = c // 2, c % 2
        xs_c = xsm[16 * c:16 * c + 16].reshape(64, D)
        xs = np.concatenate([meta, xp[b], xs_c], axis=0)
        x_own = np.concatenate([xp[b][hh * 1024:(hh + 1) * 1024], xs_c], axis=0)
        sel = np.zeros((128, 2), np.float32)
        sel[:, hh] = 1.0
        sshp = np.zeros((16, 27 * 128), np.float32)
        sshp[:, :3360] = ssh[16 * c:16 * c + 16]
        ssh_r = np.ascontiguousarray(sshp.reshape(16, 27, 128).transpose(1, 2, 0))
        m = dict(shared)
        m.update(xs=np.ascontiguousarray(xs), x_own=np.ascontiguousarray(x_own), sel=sel, ssh=ssh_r,
                 srw=np.ascontiguousarray(srw[16 * c:16 * c + 16]), srt=np.ascontiguousarray(srt[16 * c:16 * c + 16]))
        in_maps.append(m)
    if inp.get("_maps_only"):
        return in_maps
    if "nc" not in _NC_CACHE:
        _NC_CACHE["nc"] = build()
    res = run_bass_kernel_spmd(_NC_CACHE["nc"], in_maps, core_ids=list(range(8)))
    R = res.results
    y_prompt = np.zeros((4, SEQ, D), np.float32)
    y_sample = np.zeros((128, 4, D), np.float32)
    shift_p = np.zeros((1, 4, 3360), np.float32)
    rwkv_p = np.zeros((1, 4, 16, 64, 64), np.float32)
    ret_p = np.zeros((1, 4, 4, 256, 256), np.float32)
    shift_s = np.zeros((1, 128, 3360), np.float32)
    rwkv_s = np.zeros((1, 128, 16, 64, 64), np.float32)
    ret_s = np.zeros((1, 128, 4, 256, 256), np.float32)
    for c in range(8):
        b, hh = c // 2, c % 2
        r = R[c]
        y_prompt[b, hh * 1024:(hh + 1) * 1024] = r["y"][:1024]
        y_sample[16 * c:16 * c + 16] = r["y"][1024:].reshape(16, 4, D)
        shift_s[0, 16 * c:16 * c + 16] = r["sho"][1:17, :3360]
        rwkv_s[0, 16 * c:16 * c + 16] = r["rwo"][1:17]
        ret_s[0, 16 * c:16 * c + 16] = r["rto"][1:17]
        if hh == 0:
            shift_p[0, b] = r["sho"][0, :3360]
            rwkv_p[0, b] = r["rwo"][0]
            ret_p[0, b] = r["rto"][0]
    return (y_prompt, y_sample, shift_p, rwkv_p, ret_p, shift_s, rwkv_s, ret_s)
```

```python
import numpy as np
from contextlib import ExitStack
import concourse.bass as bass
import concourse.mybir as mybir
from concourse.bass_utils import run_bass_kernel_spmd

F32 = mybir.dt.float32
BF16 = mybir.dt.bfloat16
ALU = mybir.AluOpType
AF = mybir.ActivationFunctionType
AX = mybir.AxisListType

D = 2048
KC = 16
NMETA = 16
SEQ = 2048
TP = NMETA + SEQ
NS = 64
T = TP + NS
NCH = 59
NFM = 43
DFF = 5632
NFB = DFF // 128
OWN = 1024 + NS
C0 = float(np.exp(-0.5))
GAM = [1.0 - 2.0 ** (-5.0 - h) for h in range(4)]
TILES = [(0, 16)] + [(16 + 128 * i, 128) for i in range(16)] + [(TP, NS)]
BLKS = [(0, 512), (512, 512), (1024, 512), (1536, 512), (2048, T - 2048)]


class Buf:
    __slots__ = ("w", "r")

    def __init__(self):
        self.w = None
        self.r = []


class Prog:
    ENG = ("pe", "act", "dve", "pool", "sp")

    def __init__(self, nc, n_dma=48):
        self.nc = nc
        self.ops = {k: [] for k in self.ENG}
        self.cnt = {k: 0 for k in self.ENG}
        self.seen = {k: {} for k in self.ENG}
        self.n_dma = n_dma
        self.dma_use = [0] * n_dma
        self.dma_rr = 0
        self.dma_rr_pool = 0
        self.n_sp = n_dma - 12

    def _need(self, waits, eng, ev, kind):
        if ev is None:
            return
        sk, val, src = ev
        if src == eng and not isinstance(sk, tuple):
            if eng == "pe":
                return
        if waits.get(sk, 0) < val:
            waits[sk] = val

    def op(self, eng, fn, reads=(), writes=(), dma=False):
        waits = {}
        for b in reads:
            self._need(waits, eng, b.w, "raw")
        for b in writes:
            self._need(waits, eng, b.w, "waw")
            for ev in b.r:
                self._need(waits, eng, ev, "war")
        if dma:
            if eng == "pool":
                idx = self.n_sp + self.dma_rr_pool
                self.dma_rr_pool = (self.dma_rr_pool + 1) % (self.n_dma - self.n_sp)
            else:
                idx = self.dma_rr
                self.dma_rr = (self.dma_rr + 1) % self.n_sp
            sk = ("dma", idx)
            prev = 16 * self.dma_use[idx]
            if prev > 0 and waits.get(sk, 0) < prev:
                waits[sk] = prev
            self.dma_use[idx] += 1
            ev = (sk, 16 * self.dma_use[idx], eng)
            inc = (sk, 16)
        else:
            self.cnt[eng] += 1
            ev = (eng, self.cnt[eng], eng)
            inc = (eng, 1)
        seen = self.seen[eng]
        wl = []
        for sk, val in waits.items():
            if seen.get(sk, 0) < val:
                seen[sk] = val
                wl.append((sk, val))
        self.ops[eng].append((wl, fn, inc))
        for b in writes:
            b.w = ev
            b.r = []
        for b in reads:
            b.r.append(ev)
        return ev

    def barrier(self):
        allw = [(("dma", i), 16 * self.dma_use[i]) for i in range(self.n_dma) if self.dma_use[i] > 0]
        allw += [(k, self.cnt[k]) for k in self.ENG if self.cnt[k] > 0]
        for k in self.ENG:
            wl = []
            for sk, val in allw:
                if sk == k:
                    continue
                if self.seen[k].get(sk, 0) < val:
                    self.seen[k][sk] = val
                    wl.append((sk, val))
            self.ops[k].append((wl, None, None))

    def finish(self):
        waits = []
        for i in range(self.n_dma):
            if self.dma_use[i] > 0:
                waits.append((("dma", i), 16 * self.dma_use[i]))
        for k in self.ENG:
            if k != "sp" and self.cnt[k] > 0:
                waits.append((k, self.cnt[k]))
        self.ops["sp"].append((waits, None, None))

    def begin(self, st):
        nc = self.nc
        self.sems = {}
        for k in self.ENG:
            self.sems[k] = st.enter_context(nc.semaphore("s_" + k))
        for i in range(self.n_dma):
            self.sems[("dma", i)] = st.enter_context(nc.semaphore("d_%d" % i))

    def flush(self):
        nc = self.nc
        sems = self.sems
        with nc.Block() as block:

            def run(e, lst):
                for wl, fn, inc in lst:
                    for sk, val in wl:
                        e.wait_ge(sems[sk], val)
                    if fn is None:
                        continue
                    fn(e).then_inc(sems[inc[0]], inc[1])

            @block.tensor
            def _(e):
                run(e, self.ops["pe"])

            @block.scalar
            def _(e):
                run(e, self.ops["act"])

            @block.vector
            def _(e):
                run(e, self.ops["dve"])

            @block.gpsimd
            def _(e):
                run(e, self.ops["pool"])

            @block.sync
            def _(e):
                run(e, self.ops["sp"])
        self.ops = {k: [] for k in self.ENG}


class TL:
    def __init__(self, t):
        self.t = t
        self.b = Buf()

    def __getitem__(self, k):
        return self.t[k]


def host_consts():
    c = {}
    idx = np.arange(128)
    su = (idx[:, None] < idx[None, :]).astype(np.float32)
    ui = (idx[:, None] <= idx[None, :]).astype(np.float32)
    sl = (idx[:, None] > idx[None, :]).astype(np.float32)
    bd = ((idx[:, None] // 64) == (idx[None, :] // 64)).astype(np.float32)
    dm = np.zeros((128, 4, 128), np.float64)
    gq = np.zeros((128, 4, 128), np.float64)
    gk = np.zeros((128, 3, 4), np.float64)
    dms = np.zeros((128, 4, 64), np.float64)
    gqs = np.zeros((128, 4, 64), np.float64)
    for h in range(4):
        g = GAM[h]
        dd = idx[None, :] - idx[:, None]
        dm[:, h, :] = np.where(dd >= 0, g ** np.maximum(dd, 0), 0.0)
        gq[:, h, :] = g ** (idx[None, :] + 1.0)
        gk[:16, 0, h] = g ** (15.0 - idx[:16])
        gk[:, 1, h] = g ** (127.0 - idx)
        gk[:64, 2, h] = g ** (3.0 - (idx[:64] % 4))
        j = idx[:64]
        same = (j[:, None] // 4) == (j[None, :] // 4)
        dt_ = (j[None, :] % 4) - (j[:, None] % 4)
        dms[:64, h, :] = np.where(same & (dt_ >= 0), g ** np.maximum(dt_, 0), 0.0)
        gqs[:, h, :] = g ** ((j[None, :] % 4) + 1.0)
    bsel = np.zeros((128, 16, 64), np.float32)
    for b in range(16):
        bsel[:, b, 4 * b:4 * b + 4] = 1.0
    rsel = np.zeros((128, 16), np.float32)
    for r in range(64):
        rsel[r, r // 4] = 1.0
    m4 = np.concatenate([su, ui, su, ui], axis=1)
    blk = ((idx[:, None] // 4) == (idx[None, :] // 4)).astype(np.float32)
    m4s = np.concatenate([su * blk, ui * blk, su * blk, ui * blk], axis=1)
    slb = sl * blk
    parts = [su, ui, sl, bd, m4, m4s, slb, dm.reshape(128, -1), gq.reshape(128, -1), gk.reshape(128, -1),
             dms.reshape(128, -1), gqs.reshape(128, -1), bsel.reshape(128, -1), rsel]
    offs = {}
    o = 0
    names = ["su", "ui", "sl", "bd", "m4", "m4s", "slb", "dm", "gq", "gk", "dms", "gqs", "bsel", "rsel"]
    for n_, p in zip(names, parts):
        offs[n_] = o
        o += p.shape[1]
    cst = np.concatenate([p.astype(np.float32) for p in parts], axis=1)
    inv_freq = (1.0 / (10000.0 ** np.linspace(0.0, 1.0, 128, dtype=np.float32))).astype(np.float32)
    pos = np.concatenate([np.arange(TP), np.tile(16384 + np.arange(4), 16)]).astype(np.float32)
    ang = (pos[None, :] * inv_freq[:, None]).astype(np.float32)
    cs = np.cos(ang.astype(np.float64)).astype(np.float32)
    sn = np.sin(ang.astype(np.float64)).astype(np.float32)
    return cst, offs, cs, sn


CST, COFF, COS_T, SIN_T = host_consts()
NCST = CST.shape[1]


def build(debug=False, stop=None):
    nc = bass.Bass("TRN2", target_bir_lowering=False)
    DT = lambda name, shape, dt=F32, kind="ExternalInput": nc.dram_tensor(name, list(shape), dt, kind=kind).ap()
    xs = DT("xs", [T, D])
    x_own = DT("x_own", [OWN, D])
    sel_d = DT("sel", [128, 2])
    w_in = DT("w_in", [NCH, 128, KC, 128])
    w_out = DT("w_out", [4, 128, KC, 512])
    w_gate = DT("w_gate", [NFB, 128, KC, 128])
    w_up = DT("w_up", [NFB, 128, KC, 128])
    w_down = DT("w_down", [NFB, 128, D])
    gam_d = DT("gam", [3, D])
    mu_d = DT("mu", [128, 27])
    prm_d = DT("prm", [128, 7, 8])
    w2_d = DT("w2", [64, 1024])
    a2_d = DT("a2", [64, 1024])
    g2_d = DT("g2", [160, 1024])
    ssh_d = DT("ssh", [27, 128, 16])
    srw_d = DT("srw", [16, 16, 64, 64])
    srt_d = DT("srt", [16, 4, 256, 256])
    cst_d = DT("cst", [128, NCST])
    cos_d = DT("cos", [128, T])
    sin_d = DT("sin", [128, T])
    y_d = DT("y", [OWN, D], kind="ExternalOutput")
    sho_d = DT("sho", [17, 27 * 128], kind="ExternalOutput")
    rwo_d = DT("rwo", [17, 16, 64, 64], kind="ExternalOutput")
    rto_d = DT("rto", [17, 4, 256, 256], kind="ExternalOutput")
    zT_d = nc.dram_tensor("zT_scr", [NFM * 128, T], F32).ap()
    ztk_d = nc.dram_tensor("ztk_scr", [T, 16 * 128], F32).ap()
    dbg = {}
    if debug:
        dbg["oT"] = DT("dbg_oT", [128, KC, OWN], BF16, kind="ExternalOutput")
        dbg["h"] = DT("dbg_h", [OWN, D], kind="ExternalOutput")

    P = Prog(nc)
    try:
        _build_body(nc, P, debug, stop, locals())
    except _Stop:
        pass
    return nc


class _Stop(Exception):
    pass


def _build_body(nc, P, debug, stop, L):
    globals().update({k: v for k, v in L.items() if k not in ("nc", "P", "debug", "stop")})
    (xs, x_own, sel_d, w_in, w_out, w_gate, w_up, w_down, gam_d, mu_d, prm_d, w2_d, a2_d, g2_d, ssh_d, srw_d, srt_d,
     cst_d, cos_d, sin_d, y_d, sho_d, rwo_d, rto_d, zT_d, ztk_d, dbg) = [L[k] for k in (
        "xs", "x_own", "sel_d", "w_in", "w_out", "w_gate", "w_up", "w_down", "gam_d", "mu_d", "prm_d", "w2_d", "a2_d",
        "g2_d", "ssh_d", "srw_d", "srt_d", "cst_d", "cos_d", "sin_d", "y_d", "sho_d", "rwo_d", "rto_d", "zT_d", "ztk_d", "dbg")]

    def chk(k):
        if stop == k:
            P.finish()
            P.flush()
            raise _Stop()
    with ExitStack() as st:
        P.begin(st)
        def SBs(stack, name, shape, dt=F32):
            return TL(stack.enter_context(nc.sbuf_tensor("sb_" + name, list(shape), dt)))

        def SB(name, shape, dt=F32):
            return SBs(st, name, shape, dt)

        def PS(name, shape, dt=F32):
            return TL(st.enter_context(nc.psum_tensor(name, list(shape), dt)))

        def dma(eng, out, in_, reads=(), writes=()):
            P.op(eng, lambda e: e.dma_start(out=out, in_=in_), reads=[r.b for r in reads], writes=[w.b for w in writes], dma=True)

        def op(eng, fn, reads=(), writes=()):
            P.op(eng, fn, reads=[r.b for r in reads], writes=[w.b for w in writes])

        def mm(out, lhsT, rhs, start, stop, reads, writes):
            op("pe", lambda e: e.matmul(out, lhsT=lhsT, rhs=rhs, start=start, stop=stop), reads, writes)

        def tr(out, in_, ident, reads, writes):
            op("pe", lambda e: e.transpose(out=out, in_=in_, identity=ident), reads, writes)

        def tt(eng, out, a, b, o, reads, writes):
            op(eng, lambda e: e.tensor_tensor(out=out, in0=a, in1=b, op=o), reads, writes)

        def ts(eng, out, a, s1, s2, o0, o1, reads, writes):
            if s2 is None:
                op(eng, lambda e: e.tensor_scalar(out=out, in0=a, scalar1=s1, scalar2=None, op0=o0), reads, writes)
            else:
                op(eng, lambda e: e.tensor_scalar(out=out, in0=a, scalar1=s1, scalar2=s2, op0=o0, op1=o1), reads, writes)

        def stt(out, a, s, b, o0, o1, reads, writes):
            op("dve", lambda e: e.scalar_tensor_tensor(out=out, in0=a, scalar=s, in1=b, op0=o0, op1=o1), reads, writes)

        def act(out, in_, func, reads, writes, bias=0.0, scale=1.0):
            op("act", lambda e: e.activation(out=out, in_=in_, func=func, bias=bias, scale=scale), reads, writes)

        cp_i = [0]

        def cp(out, in_, reads, writes, eng=None):
            if eng is None:
                eng = "act" if cp_i[0] % 2 == 0 else "dve"
                cp_i[0] += 1
            if eng == "act":
                op("act", lambda e: e.copy(out=out, in_=in_), reads, writes)
            else:
                op(eng, lambda e: e.tensor_copy(out=out, in_=in_), reads, writes)

        pbank = [PS("pb%d" % i, [128, 512]) for i in range(6)]
        pbf = [PS("pbf%d" % i, [128, 1024], BF16) for i in range(2)]
        prr = [0]
        pfr = [0]

        def next_bank():
            b_ = pbank[prr[0] % 5]
            prr[0] += 1
            return b_

        def next_bf():
            b_ = pbf[pfr[0] % 2]
            pfr[0] += 1
            return b_
        pacc = pbank[5]

        cst = SB("cst", [128, NCST])
        dma("sp", cst[:], cst_d[:, :], writes=[cst])

        def CS(n_, w, p0=0, p1=128):
            return cst[p0:p1, COFF[n_]:COFF[n_] + w]
        sel = SB("sel", [128, 2])
        dma("sp", sel[:], sel_d[:, :], writes=[sel])
        identf = SB("identf", [128, 128])
        identb = SB("identb", [128, 128], BF16)
        ones = SB("ones", [128, 128])
        op("pool", lambda e: e.memset(identf[:], 0.0), writes=[identf])
        op("pool", lambda e: e.affine_select(out=identf[:], in_=identf[:], pattern=[[-1, 128]], compare_op=ALU.not_equal, fill=1.0, base=0, channel_multiplier=1), reads=[identf], writes=[identf])
        op("dve", lambda e: e.tensor_copy(out=identb[:], in_=identf[:]), reads=[identf], writes=[identb])
        op("dve", lambda e: e.memset(ones[:], 1.0), writes=[ones])
        oT = SB("oT", [128, KC, OWN], BF16)
        sshT = SB("sshT", [128, 27, 16])
        dma("sp", sshT[:], ssh_d.rearrange("c p b -> p c b"), writes=[sshT])
        mu = SB("mu", [128, 27])
        dma("sp", mu[:], mu_d[:, :], writes=[mu])
        prm = SB("prm", [128, 7, 8])
        dma("sp", prm[:], prm_d[:, :, :], writes=[prm])
        omka = SB("omka", [128, 8])
        ts("dve", omka[:], prm[:, 3, :], -1.0, 1.0, ALU.mult, ALU.add, [prm], [omka])
        stat = [SB("stat%d" % i, [128, 4]) for i in range(2)]
        env = {}

        def rms_rstd(src_tile, n, sti):
            junk = env["junk"]
            s_ = stat[sti]
            op("act", lambda e: e.activation(out=junk[0:n, :], in_=src_tile[0:n, :], func=AF.Square, accum_out=s_[0:n, 0:1]), reads=[src_tile], writes=[junk, s_])
            ts("dve", s_[0:n, 1:2], s_[0:n, 0:1], 1.0 / D, 1e-6, ALU.mult, ALU.add, [s_], [s_])
            op("act", lambda e: e.sqrt(out=s_[0:n, 1:2], in_=s_[0:n, 1:2]), reads=[s_], writes=[s_])
            op("dve", lambda e: e.reciprocal(out=s_[0:n, 1:2], in_=s_[0:n, 1:2]), reads=[s_], writes=[s_])
            return s_

        def transpose16(xb_, n, dst, c0):
            for half in range(2):
                pt = next_bf()
                for k8 in range(8):
                    kc = half * 8 + k8
                    tr(pt[:, k8 * 128:k8 * 128 + n], xb_[0:n, kc * 128:(kc + 1) * 128], identb[0:n, 0:n], [xb_, identb], [pt])
                src = pt[:, :].rearrange("p (k c) -> p k c", c=128)[:, :, 0:n]
                cp(dst[:, half * 8:half * 8 + 8, c0:c0 + n], src, [pt], [dst])

        with ExitStack() as s1:
            gam = SBs(s1, "gam1", [128, D])
            env["junk"] = SBs(s1, "junk1", [128, D])
            xnT = SBs(s1, "xnT", [128, KC, T], BF16)
            xt = [SBs(s1, "xt%d" % i, [128, D]) for i in range(2)]
            xnb = [SBs(s1, "xnb%d" % i, [128, D], BF16) for i in range(2)]
            wt = [SBs(s1, "wt%d" % i, [128, KC, 128], BF16) for i in range(4)]
            zst = [SBs(s1, "zst%d" % i, [128, T]) for i in range(2)]
            ztk = [SBs(s1, "ztk%d" % i, [128, 18, 128]) for i in range(1)]
            shs = [SBs(s1, "shs%d" % i, [17, 128]) for i in range(2)]
            xl = SBs(s1, "xl", [128, KC, 17], BF16)
            dma("sp", gam[:], gam_d[0, :].partition_broadcast(128), writes=[gam])
            for ti, (c0, n) in enumerate(TILES):
                x_ = xt[ti % 2]
                xb_ = xnb[ti % 2]
                dma("sp", x_[0:n, :], xs[c0:c0 + n, :], writes=[x_])
                s_ = rms_rstd(x_, n, ti % 2)
                stt(xb_[0:n, :], x_[0:n, :], s_[0:n, 1:2], gam[0:n, :], ALU.mult, ALU.mult, [x_, s_, gam], [xb_])
                transpose16(xb_, n, xnT, c0)
            cp(xl[:, :, 1:17], xnT[:, :, TP + 3:T:4], [xnT], [xl], eng="dve")
            cp(xl[:, :, 0:1], xnT[:, :, TP - 1:TP], [xnT], [xl], eng="dve")
            chk(0)
            for ch in range(NCH):
                w_ = wt[ch % 4]
                dma("pool", w_[:], w_in[ch, :, :, :], writes=[w_])
                if ch < NFM:
                    z_ = zst[ch % 2]
                    scale = (1.0 / 16.0) if (ch >= 27 and (ch - 27) % 4 >= 2) else 1.0
                    for (b0, bn) in BLKS:
                        pb_ = next_bank()
                        for kc in range(KC):
                            mm(pb_[:, 0:bn], w_[:, kc, :], xnT[:, kc, b0:b0 + bn], kc == 0, kc == KC - 1, [w_, xnT], [pb_])
                        if cp_i[0] % 2 == 0:
                            op("act", lambda e, z_=z_, pb_=pb_, b0=b0, bn=bn, scale=scale: e.mul(out=z_[:, b0:b0 + bn], in_=pb_[:, 0:bn], mul=scale), reads=[pb_], writes=[z_])
                        else:
                            ts("dve", z_[:, b0:b0 + bn], pb_[:, 0:bn], scale, None, ALU.mult, None, [pb_], [z_])
                        cp_i[0] += 1
                    dma("sp", zT_d[ch * 128:(ch + 1) * 128, :], z_[:], reads=[z_])
                    if ch < 27:
                        pb_ = next_bank()
                        for kc in range(KC):
                            mm(pb_[0:17, 0:128], xl[:, kc, :], w_[:, kc, :], kc == 0, kc == KC - 1, [w_, xl], [pb_])
                        cp(shs[ch % 2][0:17, :], pb_[0:17, 0:128], [pb_], [shs[ch % 2]])
                        dma("sp", sho_d[0:17, ch * 128:(ch + 1) * 128], shs[ch % 2][0:17, :], reads=[shs[ch % 2]])
                else:
                    zk_ = ztk[0]
                    for g4 in range(5):
                        pb_ = next_bank()
                        tl_ = list(range(g4 * 4, min(g4 * 4 + 4, 18)))
                        for q_, ti in enumerate(tl_):
                            c0, n = TILES[ti]
                            for kc in range(KC):
                                mm(pb_[0:n, q_ * 128:(q_ + 1) * 128], xnT[:, kc, c0:c0 + n], w_[:, kc, :], kc == 0, kc == KC - 1, [w_, xnT], [pb_])
                        nt = len(tl_)
                        src = pb_[:, 0:nt * 128].rearrange("p (q c) -> p q c", c=128)
                        cp(zk_[:, g4 * 4:g4 * 4 + nt, :], src, [pb_], [zk_])
                    cc = ch - NFM
                    dma("sp", ztk_d[0:16, cc * 128:(cc + 1) * 128], zk_[0:16, 0, :], reads=[zk_])
                    dma("sp", ztk_d[16:TP, cc * 128:(cc + 1) * 128].rearrange("(t p) c -> p t c", p=128), zk_[:, 1:17, :], reads=[zk_])
                    dma("sp", ztk_d[TP:T, cc * 128:(cc + 1) * 128], zk_[0:64, 17, :], reads=[zk_])
            P.flush()

        P.barrier()

        with ExitStack() as s2:
            FB = [SBs(s2, "fb%d" % i, [128, T]) for i in range(10)]
            za, zp, rs, ks, vs, sg, av, gT, Cb, kkn = FB
            t1, t2 = za, zp
            bonus = SBs(s2, "bonus", [128, T], BF16)
            junk = SBs(s2, "junk2", [128, 256])
            AR = SBs(s2, "AR", [128, 2, T], BF16)
            KRt = SBs(s2, "KRt", [128, 2, T], BF16)
            vsb = SBs(s2, "vsb", [128, T], BF16)
            la = SBs(s2, "la", [128, T], BF16)
            sgd = SBs(s2, "sgd", [128, T], BF16)
            sgd2 = SBs(s2, "sgd2", [32, T], BF16)
            wa2 = SBs(s2, "wa2", [128, 1024], BF16)
            g2a = SBs(s2, "g2a", [128, 1024], BF16)
            g2b = SBs(s2, "g2b", [32, 1024], BF16)
            pcl = SBs(s2, "pcl", [128, 40])
            dma("pool", wa2[0:64, :], w2_d[:, :], writes=[wa2])
            dma("pool", wa2[64:128, :], a2_d[:, :], writes=[wa2])
            dma("pool", g2a[:, :], g2_d[0:128, :], writes=[g2a])
            dma("pool", g2b[:, :], g2_d[128:160, :], writes=[g2b])

            def load_shift(ch, out, out_dt_tile=None):
                dma("sp", za[:, :], zT_d[ch * 128:(ch + 1) * 128, :], writes=[za])
                tt("dve", zp[:, 1:T], za[:, 0:T - 1], za[:, 1:T], ALU.subtract, [za], [zp])
                ts("dve", zp[:, 0:1], za[:, 0:1], -1.0, None, ALU.mult, None, [za], [zp])
                tt("dve", zp[:, TP:T:4], sshT[:, ch, :], za[:, TP:T:4], ALU.subtract, [sshT, za], [zp])
                stt(out[:, :], zp[:, :], mu[:, ch:ch + 1], za[:, :], ALU.mult, ALU.add, [zp, za, mu], [out])

            load_shift(24, rs)
            act(la[0:64, :], rs[0:64, :], AF.Tanh, [rs], [la])
            cp(la[64:128, :], rs[64:128, :], [rs], [la], eng="dve")
            load_shift(25, ks)
            act(sgd[:, :], ks[:, :], AF.Sigmoid, [ks], [sgd])
            load_shift(26, vs)
            act(sgd2[0:32, :], vs[0:32, :], AF.Sigmoid, [vs], [sgd2])

            chk(2)
            onb = SBs(s2, "onb", [128, 256], BF16)
            of32 = SBs(s2, "of32", [128, 128])
            PSst = SBs(s2, "PSst", [128, 128])
            PSbb = SBs(s2, "PSbb", [128, 128], BF16)

            def write_oT(kc, slot, src, n, src_tiles, first, eng="dve"):
                if slot == "s":
                    return
                if slot[0] == "own":
                    c0 = slot[1]
                    cp(oT[:, kc, c0:c0 + n], src, src_tiles, [oT], eng=eng)
                else:
                    _, s_i, second = slot
                    c0 = s_i * 128
                    if not second:
                        ts(eng, oT[:, kc, c0:c0 + n], src, sel[:, 0:1], None, ALU.mult, None, src_tiles + [sel], [oT])
                    elif eng == "dve":
                        stt(oT[:, kc, c0:c0 + n], src, sel[:, 1:2], oT[:, kc, c0:c0 + n], ALU.mult, ALU.add, src_tiles + [sel, oT], [oT])
                    else:
                        ts(eng, src, src, sel[:, 1:2], None, ALU.mult, None, src_tiles + [sel], src_tiles)
                        tt(eng, oT[:, kc, c0:c0 + n], oT[:, kc, c0:c0 + n], src, ALU.add, src_tiles + [oT], [oT])

            def slot_of(ti, b=None):
                if ti == 0:
                    return "s"
                if ti <= 16:
                    return ("pred", (ti - 1) % 8, ti > 8)
                return ("own", 1024 + 4 * b)

            class Carve:
                def __init__(self, bufs):
                    self.bufs = bufs
                    self.i = 0
                    self.off = 0

                def get(self, shape, dt):
                    nelem = int(np.prod(shape[1:]))
                    nf = (nelem * (2 if dt == BF16 else 4) + 3) // 4
                    nf = (nf + 7) // 8 * 8
                    if self.off + nf > T:
                        self.i += 1
                        self.off = 0
                    fb = self.bufs[self.i]
                    v = fb.t[0:shape[0], self.off:self.off + nf]
                    self.off += nf
                    if dt == BF16:
                        v = v.bitcast(BF16)
                    v = v[:, 0:nelem]
                    if len(shape) == 3:
                        v = v.rearrange("p (a b) -> p a b", b=shape[2])
                    return TL(v)

            class SetNS:
                pass

            import os as _os
            NSET = int(_os.environ.get('KB_NSET', '5'))
            cv = Carve([za, zp, rs, ks, vs, sg, av, Cb, kkn])
            SETS = []
            for si in range(NSET):
                S_ = SetNS()
                S_.tokb = cv.get([128, 3, 128], BF16)
                S_.A4 = [cv.get([128, 4, 128], BF16) for h in range(2)]
                S_.MM = [cv.get([128, 4, 128], BF16) for i in range(2)]
                S_.Xb = cv.get([128, 2, 128], BF16)
                S_.Wb = cv.get([128, 128], BF16)
                S_.UTb = cv.get([128, 128], BF16)
                S_.Sbb = cv.get([128, 128], BF16)
                S_.onb = cv.get([128, 128], BF16)
                S_.Sst = cv.get([128, 128], F32)
                S_.tmpS = cv.get([128, 128], F32)
                S_.of32 = cv.get([128, 128], F32)
                S_.bst = cv.get([128, 2, 6], F32)
                S_.mv = cv.get([128, 2, 2], F32)
                S_.Sn = cv.get([64, 2, 64], F32)
                S_.T2 = cv.get([128, 64], F32)
                S_.So = cv.get([64, 2, 64], F32)
                SETS.append(S_)

            SG = SetNS()
            SG.SnA = cv.get([64, 8, 128], F32)
            SG.SstA = cv.get([128, 8, 128], F32)
            SG.SbbA = cv.get([128, 8, 128], BF16)
            SG.am = cv.get([128, 16, 32], BF16)
            SG.tokm = cv.get([128, 8, 256], BF16)
            SG.T2A = cv.get([128, 8, 64], F32)
            SG.tmp4 = cv.get([128, 4, 128], F32)

            free_f = []
            free_b = []

            def wait_f():
                while not free_f:
                    yield

            def wait_b():
                while not free_b:
                    yield

            pstate = {"next": 0}
            eps_gn = SBs(s2, "eps_gn", [128, 1])
            op("dve", lambda e: e.memset(eps_gn[:, :], 64e-5), [], [eps_gn])

            def rwkv_pre(S_, tile, n, lv, mk4, msl):
                tokb, A4, MM, Xb = S_.tokb, S_.A4, S_.MM, S_.Xb
                yield from wait_b()
                pt = free_b.pop(0)
                tr(pt[0:n, 0:128], KRt[:, 0, tile], identb[:, :], [KRt, identb], [pt])
                tr(pt[0:n, 128:256], KRt[:, 1, tile], identb[:, :], [KRt, identb], [pt])
                tr(pt[0:n, 256:384], vsb[:, tile], identb[:, :], [vsb, identb], [pt])
                yield
                cp(tokb[0:n, :, :], pt[0:n, 0:384].rearrange("p (q c) -> p q c", c=128), [pt], [tokb], eng="act")
                free_b.append(pt)
                yield
                msk4 = cst[0:n, COFF[mk4]:COFF[mk4] + 512].rearrange("p (q c) -> p q c", c=128)[:, :, 0:n]
                for h in range(2):
                    hs = slice(64 * h, 64 * h + 64)
                    yield from wait_f()
                    pn = free_f.pop(0)
                    mm(pn[0:n, 0:n], AR[hs, 0, tile], KRt[hs, 0, tile], True, True, [KRt, AR], [pn])
                    yield
                    tt("dve", MM[0][0:n, 2 * h + 1, 0:n], pn[0:n, 0:n], CS(msl, n, 0, n), ALU.mult, [pn, cst], [MM[0]])
                    free_f.append(pn)
                    yield from wait_f()
                    pb_ = free_f.pop(0)
                    for q_ in range(2):
                        mm(pb_[0:n, q_ * 128:q_ * 128 + n], KRt[hs, 0, tile], AR[hs, q_, tile], True, True, [KRt, AR], [pb_])
                        mm(pb_[0:n, 256 + q_ * 128:256 + q_ * 128 + n], KRt[hs, 1, tile], AR[hs, q_, tile], True, True, [KRt, AR], [pb_])
                    yield
                    o4 = pb_[0:n, 0:512].rearrange("p (q c) -> p q c", c=128)[:, :, 0:n]
                    tt("dve", A4[h][0:n, :, 0:n], o4, msk4, ALU.mult, [pb_, cst], [A4[h]])
                    free_f.append(pb_)
                    yield
                    tt("pool", Xb[0:n, h, 0:n], A4[h][0:n, 0, 0:n], identb[0:n, 0:n], ALU.add, [A4[h], identb], [Xb])
                for lev in range(1, lv):
                    cur = MM[(lev - 1) % 2]
                    nxt = MM[lev % 2]
                    last = (lev == lv - 1)
                    yield from wait_f()
                    pb_ = free_f.pop(0)
                    for h in range(2):
                        cM = A4[h][0:n, 0, 0:n] if lev == 1 else cur[0:n, 2 * h, 0:n]
                        cMT = cur[0:n, 2 * h + 1, 0:n]
                        rd = [cur, A4[h]]
                        if not last:
                            mm(pb_[0:n, (2 * h) * 128:(2 * h) * 128 + n], cMT, cM, True, True, rd, [pb_])
                        mm(pb_[0:n, (2 * h + 1) * 128:(2 * h + 1) * 128 + n], cM, cMT, True, True, rd, [pb_])
                    yield
                    if last:
                        for h in range(2):
                            cp(nxt[0:n, 2 * h + 1, 0:n], pb_[0:n, (2 * h + 1) * 128:(2 * h + 1) * 128 + n], [pb_], [nxt], eng=("act" if h == 0 else "dve"))
                    else:
                        cp(nxt[0:n, :, 0:n], pb_[0:n, 0:512].rearrange("p (q c) -> p q c", c=128)[:, :, 0:n], [pb_], [nxt], eng=("act" if lev % 2 == 0 else "dve"))
                    free_f.append(pb_)
                    yield
                    yield from wait_f()
                    px = free_f.pop(0)
                    for h in range(2):
                        mm(px[0:n, h * 128:h * 128 + n], identb[0:n, 0:n], Xb[0:n, h, 0:n], True, False, [identb, Xb], [px])
                        mm(px[0:n, h * 128:h * 128 + n], nxt[0:n, 2 * h + 1, 0:n], Xb[0:n, h, 0:n], False, True, [nxt, Xb], [px])
                    yield
                    cp(Xb[0:n, :, 0:n], px[0:n, 0:256].rearrange("p (q c) -> p q c", c=128)[:, :, 0:n], [px], [Xb], eng=("dve" if lev % 2 == 0 else "act"))
                    free_f.append(px)
                    yield

            def rwkv_post(S_, po, hp, tile, n, slot):
                bst, mv, onb_, of_ = S_.bst, S_.mv, S_.onb, S_.of32
                for h in range(2):
                    hc = slice(64 * h, 64 * h + 64)
                    op("dve", lambda e, h=h, hc=hc: e.bn_stats(out=bst[0:n, h, :], in_=po[0:n, hc]), [po], [bst])
                    op("dve", lambda e, h=h: e.bn_aggr(out=mv[0:n, h, :], in_=bst[0:n, h, :]), [bst], [mv])
                yield
                if _os.environ.get('KB_LN', '1') == '1':
                    act(mv[0:n, :, 1], mv[0:n, :, 1], AF.Ln, [mv], [mv], bias=eps_gn[0:n, 0:1])
                    act(mv[0:n, :, 1], mv[0:n, :, 1], AF.Exp, [mv], [mv], scale=-0.5)
                else:
                    ts("dve", mv[0:n, :, 1], mv[0:n, :, 1], 64e-5, None, ALU.add, None, [mv], [mv])
                    op("act", lambda e: e.sqrt(out=mv[0:n, :, 1], in_=mv[0:n, :, 1]), [mv], [mv])
                    op("dve", lambda e: e.reciprocal(out=mv[0:n, :, 1], in_=mv[0:n, :, 1]), [mv], [mv])
                yield
                for h in range(2):
                    hc = slice(64 * h, 64 * h + 64)
                    ts("dve", onb_[0:n, hc], po[0:n, hc], mv[0:n, h, 0:1], mv[0:n, h, 1:2], ALU.subtract, ALU.mult, [po, mv], [onb_])
                free_f.append(po)
                yield
                yield from wait_b()
                pt = free_b.pop(0)
                tr(pt[:, 0:n], onb_[0:n, 0:128], identb[0:n, 0:n], [onb_, identb], [pt])
                yield
                op("act", lambda e: e.activation(out=of_[:, 0:n], in_=pt[:, 0:n], func=AF.Identity, bias=prm[:, 6, hp:hp + 1], scale=prm[:, 5, hp:hp + 1]), [pt, prm], [of_])
                free_b.append(pt)
                yield
                tt("pool", of_[:, 0:n], of_[:, 0:n], bonus[:, tile], ALU.add, [of_, bonus], [of_])
                tt("pool", of_[:, 0:n], of_[:, 0:n], gT[:, tile], ALU.mult, [of_, gT], [of_])
                write_oT(hp, slot, of_[:, 0:n], n, [of_], None, eng="pool")
                yield


            def rwkv_task(hp, c0, n, pidx, slot, S_, Sst, Sbb, ti, b):
                tile = slice(c0, c0 + n)
                lv = int(np.ceil(np.log2(n)))
                tokb, A4, MM, Xb, Wb, UTb, tmpS, bst, mv, onb_, of_ = S_.tokb, S_.A4, S_.MM, S_.Xb, S_.Wb, S_.UTb, S_.tmpS, S_.bst, S_.mv, S_.onb, S_.of32
                if b is not None:
                    Sn, T2, So = S_.Sn, S_.T2, S_.So
                    dma("sp", Sn[:, :, :], srw_d[b, 2 * hp:2 * hp + 2, :, :].rearrange("h i j -> i h j"), writes=[Sn])
                    yield from wait_f()
                    pb_ = free_f.pop(0)
                    tr(pb_[:, 0:64], Sn[:, :, :].rearrange("p h j -> p (h j)"), identf[0:64, 0:64], [Sn, identf], [pb_])
                    op("pool", lambda e: e.memset(Sst[:, :], 0.0), [], [Sst])
                    yield
                    cp(Sst[0:64, 0:64], pb_[0:64, 0:64], [pb_], [Sst], eng="dve")
                    cp(Sst[64:128, 64:128], pb_[64:128, 0:64], [pb_], [Sst], eng="act")
                    free_f.append(pb_)
                    yield
                    cp(Sbb[:, :], Sst[:, :], [Sst], [Sbb], eng="pool")
                yield from rwkv_pre(S_, tile, n, lv, "m4", "sl")
                if b is None:
                    while pstate["next"] != ti:
                        yield
                yield from wait_f()
                pw = free_f.pop(0)
                mm(pw[0:n, 0:128], AR[:, 0, tile], Sbb[:, :], True, False, [AR, Sbb], [pw])
                for h in range(2):
                    hc = slice(64 * h, 64 * h + 64)
                    mm(pw[0:n, hc], A4[h][0:n, 2, 0:n], tokb[0:n, 2, hc], False, h == 1, [A4[h], tokb], [pw])
                yield
                cp(Wb[0:n, :], pw[0:n, 0:128], [pw], [Wb], eng="act")
                free_f.append(pw)
                yield
                yield from wait_f()
                pu = free_f.pop(0)
                for h in range(2):
                    hc = slice(64 * h, 64 * h + 64)
                    mm(pu[0:n, hc], Xb[0:n, h, 0:n], Wb[0:n, hc], True, True, [Xb, Wb], [pu])
                yield
                cp(UTb[0:n, :], pu[0:n, 0:128], [pu], [UTb], eng="dve")
                free_f.append(pu)
                yield
                yield from wait_f()
                psc = free_f.pop(0)
                mm(psc[:, 0:128], tokb[0:n, 0, :], UTb[0:n, :], True, False, [tokb, UTb], [psc])
                mm(psc[:, 0:128], tokb[0:n, 1, :], tokb[0:n, 2, :], False, True, [tokb], [psc])
                po = None
                if slot != "s":
                    yield from wait_f()
                    po = free_f.pop(0)
                    mm(po[0:n, 0:128], AR[:, 1, tile], Sbb[:, :], True, False, [AR, Sbb], [po])
                    for h in range(2):
                        hc = slice(64 * h, 64 * h + 64)
                        mm(po[0:n, hc], A4[h][0:n, 1, 0:n], UTb[0:n, hc], False, False, [A4[h], UTb], [po])
                        mm(po[0:n, hc], A4[h][0:n, 3, 0:n], tokb[0:n, 2, hc], False, h == 1, [A4[h], tokb], [po])
                yield
                stt(tmpS[:, :], psc[:, 0:128], pcl[:, pidx:pidx + 1], CS("bd", 128), ALU.mult, ALU.mult, [psc, pcl, cst], [tmpS])
                free_f.append(psc)
                stt(Sst[:, :], Sst[:, :], pcl[:, pidx:pidx + 1], tmpS[:, :], ALU.mult, ALU.add, [Sst, pcl, tmpS], [Sst])
                yield
                cp(Sbb[:, :], Sst[:, :], [Sst], [Sbb], eng="act")
                if b is None:
                    pstate["next"] = ti + 1
                yield
                if b is not None or ti == 16:
                    Sn, T2, So = S_.Sn, S_.T2, S_.So
                    cp(T2[0:64, :], Sst[0:64, 0:64], [Sst], [T2], eng="pool")
                    cp(T2[64:128, :], Sst[64:128, 64:128], [Sst], [T2], eng="pool")
                    yield from wait_f()
                    pb_ = free_f.pop(0)
                    tr(pb_[0:64, 0:128], T2[:, 0:64], identf[:, :], [T2, identf], [pb_])
                    yield
                    cp(So[:, :, :], pb_[0:64, 0:128].rearrange("p (h j) -> p h j", j=64), [pb_], [So], eng="act")
                    free_f.append(pb_)
                    row = 0 if b is None else 1 + b
                    dma("sp", rwo_d[row, 2 * hp:2 * hp + 2, :, :].rearrange("h i j -> i h j"), So[:, :, :], reads=[So])
                    yield
                if po is None:
                    return
                yield from rwkv_post(S_, po, hp, tile, n, slot)

            def rwkv_sgroup_task(hp, g_i, S_):
                ng, n, lv = 8, 32, 2
                c0 = TP + 32 * g_i
                tile = slice(c0, c0 + n)
                b0 = 8 * g_i
                tokb, A4, MM, Xb, Wb, UTb = S_.tokb, S_.A4, S_.MM, S_.Xb, S_.Wb, S_.UTb
                SnA, SstA, SbbA, am, tokm, T2A, tmp4 = SG.SnA, SG.SstA, SG.SbbA, SG.am, SG.tokm, SG.T2A, SG.tmp4
                for h in range(2):
                    dma("sp", SnA[:, :, h * 64:(h + 1) * 64], srw_d[b0:b0 + ng, 2 * hp + h, :, :].rearrange("b i j -> i b j"), writes=[SnA])
                op("pool", lambda e: e.memset(SstA[:, :, :], 0.0), [], [SstA])
                yield from wait_f()
                pk = free_f.pop(0)
                for bb in range(ng):
                    tr(pk[:, bb * 64:(bb + 1) * 64], SnA[:, bb, :], identf[0:64, 0:64], [SnA, identf], [pk])
                yield
                cp(SstA[0:64, :, 0:64], pk[0:64, 0:512].rearrange("p (b i) -> p b i", i=64), [pk], [SstA], eng="dve")
                cp(SstA[64:128, :, 64:128], pk[64:128, 0:512].rearrange("p (b i) -> p b i", i=64), [pk], [SstA], eng="act")
                free_f.append(pk)
                yield
                cp(SbbA[:, :, :], SstA[:, :, :], [SstA], [SbbA], eng="pool")
                yield
                yield from rwkv_pre(S_, tile, n, lv, "m4s", "slb")
                bselv = cst[:, COFF["bsel"]:COFF["bsel"] + 1024].rearrange("p (b c) -> p b c", c=64)[:, 0:ng, 0:n]
                for w_ in range(2):
                    tt("pool" if w_ else "dve", am[:, w_ * ng:(w_ + 1) * ng, :], AR[:, w_, tile].unsqueeze(1).broadcast_to([128, ng, n]), bselv, ALU.mult, [AR, cst], [am])
                rselv = cst[0:n, COFF["rsel"]:COFF["rsel"] + ng].unsqueeze(2).broadcast_to([n, ng, 256])
                tt("dve", tokm[0:n, :, :], tokb[0:n, 0:2, :].rearrange("p q c -> p (q c)").unsqueeze(1).broadcast_to([n, ng, 256]), rselv, ALU.mult, [tokb, cst], [tokm])
                yield
                yield from wait_f()
                pw = free_f.pop(0)
                for bb in range(ng):
                    mm(pw[0:n, 0:128], am[:, bb, :], SbbA[:, bb, :], bb == 0, False, [am, SbbA], [pw])
                for h in range(2):
                    hc = slice(64 * h, 64 * h + 64)
                    mm(pw[0:n, hc], A4[h][0:n, 2, 0:n], tokb[0:n, 2, hc], False, h == 1, [A4[h], tokb], [pw])
                yield
                cp(Wb[0:n, :], pw[0:n, 0:128], [pw], [Wb], eng="act")
                free_f.append(pw)
                yield
                yield from wait_f()
                pu = free_f.pop(0)
                for h in range(2):
                    hc = slice(64 * h, 64 * h + 64)
                    mm(pu[0:n, hc], Xb[0:n, h, 0:n], Wb[0:n, hc], True, True, [Xb, Wb], [pu])
                yield
                cp(UTb[0:n, :], pu[0:n, 0:128], [pu], [UTb], eng="dve")
                free_f.append(pu)
                yield
                yield from wait_f()
                po = free_f.pop(0)
                for bb in range(ng):
                    mm(po[0:n, 0:128], am[:, ng + bb, :], SbbA[:, bb, :], bb == 0, False, [am, SbbA], [po])
                for h in range(2):
                    hc = slice(64 * h, 64 * h + 64)
                    mm(po[0:n, hc], A4[h][0:n, 1, 0:n], UTb[0:n, hc], False, False, [A4[h], UTb], [po])
                    mm(po[0:n, hc], A4[h][0:n, 3, 0:n], tokb[0:n, 2, hc], False, h == 1, [A4[h], tokb], [po])
                yield
                for q_ in range(2):
                    yield from wait_f()
                    psc = free_f.pop(0)
                    for b4 in range(4):
                        bb = 4 * q_ + b4
                        mm(psc[:, b4 * 128:(b4 + 1) * 128], tokm[0:n, bb, 0:128], UTb[0:n, :], True, False, [tokm, UTb], [psc])
                        mm(psc[:, b4 * 128:(b4 + 1) * 128], tokm[0:n, bb, 128:256], tokb[0:n, 2, :], False, True, [tokm, tokb], [psc])
                    yield
                    pcb = pcl[:, 17 + b0 + 4 * q_:17 + b0 + 4 * q_ + 4].unsqueeze(2).broadcast_to([128, 4, 128])
                    bdb = CS("bd", 128).unsqueeze(1).broadcast_to([128, 4, 128])
                    tt("dve", tmp4[:, :, :], psc[:, 0:512].rearrange("p (b c) -> p b c", c=128), pcb, ALU.mult, [psc, pcl], [tmp4])
                    free_f.append(psc)
                    tt("pool", tmp4[:, :, :], tmp4[:, :, :], bdb, ALU.mult, [tmp4, cst], [tmp4])
                    tt("dve", SstA[:, 4 * q_:4 * q_ + 4, :], SstA[:, 4 * q_:4 * q_ + 4, :], pcb, ALU.mult, [SstA, pcl], [SstA])
                    yield
                    tt("pool", SstA[:, 4 * q_:4 * q_ + 4, :], SstA[:, 4 * q_:4 * q_ + 4, :], tmp4[:, :, :], ALU.add, [SstA, tmp4], [SstA])
                    yield
                cp(T2A[0:64, :, :], SstA[0:64, :, 0:64], [SstA], [T2A], eng="dve")
                cp(T2A[64:128, :, :], SstA[64:128, :, 64:128], [SstA], [T2A], eng="pool")
                yield
                for q_ in range(2):
                    yield from wait_f()
                    pk = free_f.pop(0)
                    for b4 in range(4):
                        tr(pk[0:64, b4 * 128:(b4 + 1) * 128], T2A[:, 4 * q_ + b4, :], identf[:, :], [T2A, identf], [pk])
                    yield
                    cp(SnA[:, 4 * q_:4 * q_ + 4, :], pk[0:64, 0:512].rearrange("p (b c) -> p b c", c=128), [pk], [SnA], eng="act")
                    free_f.append(pk)
                    yield
                for h in range(2):
                    dma("sp", rwo_d[1 + b0:1 + b0 + ng, 2 * hp + h, :, :].rearrange("b i j -> i b j"), SnA[:, :, h * 64:(h + 1) * 64], reads=[SnA])
                yield
                yield from rwkv_post(S_, po, hp, tile, n, ("own", 1024 + 32 * g_i))

            def run_tasks(makers, sets):
                active = []
                makers = list(makers)
                free_sets = list(sets)
                while makers or active:
                    while makers and free_sets:
                        S_ = free_sets.pop(0)
                        active.append((makers.pop(0)(S_), S_))
                    for item in list(active):
                        try:
                            next(item[0])
                        except StopIteration:
                            active.remove(item)
                            free_sets.append(item[1])

            for hp in range(8):
                load_shift(hp, rs)
                load_shift(8 + hp, ks)
                load_shift(16 + hp, vs)
                for (b0, bn) in BLKS:
                    bl = slice(b0, b0 + bn)
                    pb_ = next_bank()
                    mm(pb_[:, 0:bn], wa2[0:64, hp * 128:(hp + 1) * 128], la[0:64, bl], True, True, [wa2, la], [pb_])
                    act(sg[:, bl], pb_[:, 0:bn], AF.Sigmoid, [pb_, prm], [sg], bias=prm[:, 0, hp:hp + 1])
                    pb_ = next_bank()
                    mm(pb_[:, 0:bn], wa2[64:128, hp * 128:(hp + 1) * 128], la[64:128, bl], True, True, [wa2, la], [pb_])
                    act(av[:, bl], pb_[:, 0:bn], AF.Sigmoid, [pb_, prm], [av], bias=prm[:, 1, hp:hp + 1])
                    pb_ = next_bank()
                    mm(pb_[:, 0:bn], g2a[:, hp * 128:(hp + 1) * 128], sgd[:, bl], True, False, [g2a, sgd], [pb_])
                    mm(pb_[:, 0:bn], g2b[0:32, hp * 128:(hp + 1) * 128], sgd2[0:32, bl], False, True, [g2b, sgd2], [pb_])
                    cp(gT[:, bl], pb_[:, 0:bn], [pb_], [gT])
                ts("dve", t1[:, :], ks[:, :], prm[:, 2, hp:hp + 1], None, ALU.mult, None, [ks, prm], [t1])
                op("act", lambda e: e.square(out=t2[:, :], in_=t1[:, :]), [t1], [t2])
                for (b0, bn) in BLKS:
                    bl = slice(b0, b0 + bn)
                    pb_ = next_bank()
                    mm(pb_[:, 0:bn], CS("bd", 128), t2[:, bl], True, True, [cst, t2], [pb_])
                    op("act", lambda e, pb_=pb_, bl=bl, bn=bn: e.sqrt(out=kkn[:, bl], in_=pb_[:, 0:bn]), [pb_], [kkn])
                ts("dve", kkn[:, :], kkn[:, :], 1e-12, None, ALU.max, None, [kkn], [kkn])
                op("dve", lambda e: e.reciprocal(out=kkn[:, :], in_=kkn[:, :]), [kkn], [kkn])
                tt("pool", kkn[:, :], kkn[:, :], t1[:, :], ALU.mult, [kkn, t1], [kkn])
                tt("pool", t2[:, :], kkn[:, :], av[:, :], ALU.mult, [kkn, av], [t2])
                ts("dve", t1[:, :], av[:, :], prm[:, 3, hp:hp + 1], omka[:, hp:hp + 1], ALU.mult, ALU.add, [av, prm, omka], [t1])
                tt("dve", ks[:, :], ks[:, :], t1[:, :], ALU.mult, [ks, t1], [ks])
                stt(t1[:, :], rs[:, :], prm[:, 4, hp:hp + 1], ks[:, :], ALU.mult, ALU.mult, [rs, prm, ks], [t1])
                for (b0, bn) in BLKS:
                    bl = slice(b0, b0 + bn)
                    pb_ = next_bank()
                    mm(pb_[:, 0:bn], CS("bd", 128), t1[:, bl], True, True, [cst, t1], [pb_])
                    tt("dve", bonus[:, bl], pb_[:, 0:bn], vs[:, bl], ALU.mult, [pb_, vs], [bonus])
                for ti in range(17):
                    c0, n = TILES[ti]
                    op("dve", lambda e, c0=c0, n=n: e.tensor_tensor_scan(out=Cb[:, c0:c0 + n], data0=ones[:, 0:n], data1=sg[:, c0:c0 + n], initial=0.0, op0=ALU.mult, op1=ALU.add), [ones, sg], [Cb])
                cp(Cb[:, TP:T:4], sg[:, TP:T:4], [sg], [Cb], eng="dve")
                for t_ in range(1, 4):
                    tt("dve", Cb[:, TP + t_:T:4], Cb[:, TP + t_ - 1:T:4], sg[:, TP + t_:T:4], ALU.add, [Cb, sg], [Cb])
                act(t1[:, :], Cb[:, :], AF.Exp, [Cb], [t1], scale=-C0)
                for ti in range(17):
                    c0, n = TILES[ti]
                    cp(pcl[:, ti:ti + 1], t1[:, c0 + n - 1:c0 + n], [t1], [pcl], eng="dve")
                cp(pcl[:, 17:33], t1[:, TP + 3:T:4], [t1], [pcl], eng="dve")
                tt("pool", AR[:, 1, :], rs[:, :], t1[:, :], ALU.mult, [rs, t1], [AR])
                tt("dve", sg[:, :], Cb[:, :], sg[:, :], ALU.subtract, [Cb, sg], [sg])
                act(sg[:, :], sg[:, :], AF.Exp, [sg], [sg], scale=-C0)
                stt(AR[:, 0, :], kkn[:, :], -1.0, sg[:, :], ALU.mult, ALU.mult, [kkn, sg], [AR])
                act(rs[:, :], Cb[:, :], AF.Exp, [Cb, AR], [rs], scale=C0)
                tt("dve", KRt[:, 0, :], t2[:, :], rs[:, :], ALU.mult, [t2, rs], [KRt])
                tt("pool", KRt[:, 1, :], ks[:, :], rs[:, :], ALU.mult, [ks, rs], [KRt])
                cp(vsb[:, :], vs[:, :], [vs], [vsb], eng="act")
                op("dve", lambda e: e.memset(PSst[:, :], 0.0), [], [PSst])
                op("dve", lambda e: e.memset(PSbb[:, :], 0.0), [], [PSbb])
                P.barrier()
                free_f[:] = list(pbank[0:6])
                free_b[:] = list(pbf)
                pstate["next"] = 0
                gens = []
                order = []
                for i in range(17):
                    order.append(("p", i))
                    if i in (1, 3):
                        order.append(("s", i // 2))
                order = order[:int(_os.environ.get('KB_NT', '99'))]
                for k_, (kind, i) in enumerate(order):
                    if kind == "p":
                        c0, n = TILES[i]
                        gens.append(lambda S_, c0=c0, n=n, i=i: rwkv_task(hp, c0, n, i, slot_of(i), S_, PSst, PSbb, i, None))
                    else:
                        gens.append(lambda S_, i=i: rwkv_sgroup_task(hp, i, S_))
                run_tasks(gens, SETS)
                chk(3)
                P.barrier()
                prr[0] = 0
                pfr[0] = 0
            chk(7)
            qe, qo, ke, ko = rs, ks, vs, sg
            cosT, sinT = av, gT
            QR = AR
            dma("sp", cosT[:, :], cos_d[:, :], writes=[cosT])
            dma("sp", sinT[:, :], sin_d[:, :], writes=[sinT])
            Sr = [SBs(s2, "Sr%d" % i, [128, 256]) for i in range(2)]
            Srb = [SBs(s2, "Srb%d" % i, [128, 256], BF16) for i in range(2)]
            QS = SBs(s2, "QS", [128, 2, 64])
            qsm = TL(Cb[:, 0:2048].rearrange("p (d b c) -> p d b c", d=2, b=16))
            qsm.b = Cb.b
            ktm = SBs(s2, "ktm", [64, 16, 256], BF16)
            Sin = [TL(kkn[:, i * 512:(i + 1) * 512].rearrange("p (d e) -> p d e", d=2)) for i in range(3)]
            op("dve", lambda e: e.memset(kkn[:, 0:1], 0.0), [], [kkn])
            for s_i in Sin:
                s_i.b.w = kkn.b.w
            rjunk = SBs(s2, "rjunk", [128, 256])

            def make_rset(get):
                R_ = SetNS()
                R_.gt = get([128, 256], F32)
                R_.of32 = get([128, 128], F32)
                R_.rstat = get([128, 4], F32)
                R_.vtb = get([128, 256], BF16)
                R_.ktk = get([128, 2, 128], BF16)
                R_.scb = get([128, 128], BF16)
                R_.qsb = get([128, 2, 128], BF16)
                R_.onb = get([128, 256], BF16)
                return R_
            rcv = Carve([za, zp])
            RSETS = [make_rset(rcv.get) for _ in range(4)]
            rcnt = [0]

            def real_get(shape, dt):
                rcnt[0] += 1
                return SBs(s2, "rs5_%d" % rcnt[0], shape, dt)
            RSETS.append(make_rset(real_get))
            rstate = {"next": 0}

            def ret_opost(po, n, h, slot, R_):
                g_, rstat, onb_, of_ = R_.gt, R_.rstat, R_.onb, R_.of32
                op("act", lambda e: e.activation(out=rjunk[0:n, 0:256], in_=po[0:n, 0:256], func=AF.Square, accum_out=rstat[0:n, 0:1]), [po], [rjunk, rstat])
                yield
                ts("dve", rstat[0:n, 1:2], rstat[0:n, 0:1], 1.0 / 256.0, 1e-6, ALU.mult, ALU.add, [rstat], [rstat])
                yield
                act(rstat[0:n, 1:2], rstat[0:n, 1:2], AF.Ln, [rstat], [rstat])
                act(rstat[0:n, 1:2], rstat[0:n, 1:2], AF.Exp, [rstat], [rstat], scale=-0.5)
                act(g_[0:n, :], g_[0:n, :], AF.Silu, [g_], [g_])
                yield
                stt(onb_[0:n, :], po[0:n, 0:256], rstat[0:n, 1:2], g_[0:n, :], ALU.mult, ALU.mult, [po, rstat, g_], [onb_])
                yield

            def ret_opost2(n, h, slot, R_):
                onb_, of_ = R_.onb, R_.of32
                yield from wait_b()
                pt = free_b.pop(0)
                for j in range(2):
                    tr(pt[:, j * 128:j * 128 + n], onb_[0:n, j * 128:(j + 1) * 128], identb[0:n, 0:n], [onb_, identb], [pt])
                yield
                for j in range(2):
                    cp(of_[:, 0:n], pt[:, j * 128:j * 128 + n], [pt], [of_], eng="act")
                    yield
                    write_oT(8 + 2 * h + j, slot, of_[:, 0:n], n, [of_], None, eng="dve")
                    yield
                free_b.append(pt)

            def ret_task(h, ti, R_):
                g = GAM[h]
                c0, n = TILES[ti]
                tile = slice(c0, c0 + n)
                g_, vtb, ktk, scb, qsb = R_.gt, R_.vtb, R_.ktk, R_.scb, R_.qsb
                dma("pool", vtb[0:n, :], ztk_d[c0:c0 + n, 2 * h * 128:(2 * h + 2) * 128], writes=[vtb])
                dma("sp", g_[0:n, :], ztk_d[c0:c0 + n, 1024 + 2 * h * 128:1024 + (2 * h + 2) * 128], writes=[g_])
                var = 0 if ti == 0 else 1
                yield from wait_b()
                pt = free_b.pop(0)
                for dc in range(2):
                    tr(pt[0:n, dc * 128:(dc + 1) * 128], KRt[:, dc, tile], identb[:, :], [KRt, identb], [pt])
                yield
                ts("dve", ktk[0:n, :, :], pt[0:n, 0:256].rearrange("p (q c) -> p q c", c=128), cst[0:n, COFF["gk"] + var * 4 + h:COFF["gk"] + var * 4 + h + 1], None, ALU.mult, None, [pt, cst], [ktk])
                free_b.append(pt)
                yield
                yield from wait_f()
                ps_ = free_f.pop(0)
                mm(ps_[0:n, 0:n], KRt[:, 0, tile], QR[:, 0, tile], True, False, [KRt, QR], [ps_])
                mm(ps_[0:n, 0:n], KRt[:, 1, tile], QR[:, 1, tile], False, True, [KRt, QR], [ps_])
                yield
                dmv = cst[0:n, COFF["dm"]:COFF["dm"] + 512].rearrange("p (h i) -> p h i", i=128)[:, h, 0:n]
                tt("dve", scb[0:n, 0:n], ps_[0:n, 0:n], dmv, ALU.mult, [ps_, cst], [scb])
                free_f.append(ps_)
                gqv = cst[:, COFF["gq"]:COFF["gq"] + 512].rearrange("p (h i) -> p h i", i=128)[:, h, 0:n]
                tt("dve", qsb[:, 0, 0:n], QR[:, 0, tile], gqv, ALU.mult, [QR, cst], [qsb])
                tt("pool", qsb[:, 1, 0:n], QR[:, 1, tile], gqv, ALU.mult, [QR, cst], [qsb])
                yield
                while rstate["next"] != ti:
                    yield
                po = None
                if ti > 0:
                    yield from wait_f()
                    po = free_f.pop(0)
                    mm(po[0:n, 0:256], scb[0:n, 0:n], vtb[0:n, :], True, False, [scb, vtb], [po])
                    mm(po[0:n, 0:256], qsb[:, 0, 0:n], Srb[0][:, :], False, False, [qsb, Srb[0]], [po])
                    mm(po[0:n, 0:256], qsb[:, 1, 0:n], Srb[1][:, :], False, True, [qsb, Srb[1]], [po])
                yield from wait_f()
                pS = free_f.pop(0)
                for dc in range(2):
                    mm(pS[:, dc * 256:(dc + 1) * 256], ktk[0:n, dc, :], vtb[0:n, :], True, True, [ktk, vtb], [pS])
                yield
                stt(Sr[0][:, :], Sr[0][:, :], float(g ** n), pS[:, 0:256], ALU.mult, ALU.add, [Sr[0], pS], [Sr[0]])
                stt(Sr[1][:, :], Sr[1][:, :], float(g ** n), pS[:, 256:512], ALU.mult, ALU.add, [Sr[1], pS], [Sr[1]])
                free_f.append(pS)
                cp(Srb[0][:, :], Sr[0][:, :], [Sr[0]], [Srb[0]], eng="act")
                cp(Srb[1][:, :], Sr[1][:, :], [Sr[1]], [Srb[1]], eng="act")
                rstate["next"] = ti + 1
                yield
                if ti == 16:
                    for dc in range(2):
                        dma("sp", rto_d[0, h, :, :].rearrange("(m two) e -> m two e", two=2)[:, dc, :], Sr[dc][:, :], reads=[Sr[dc]])
                if po is not None:
                    yield from ret_opost(po, n, h, slot_of(ti), R_)
                    free_f.append(po)
                    yield from ret_opost2(n, h, slot_of(ti), R_)

            def ret_sample_task(h, R_):
                g = GAM[h]
                ti = 17
                c0, n = TILES[ti]
                tile = slice(c0, c0 + n)
                g_, vtb, ktk, scb = R_.gt, R_.vtb, R_.ktk, R_.scb
                dma("pool", vtb[0:n, :], ztk_d[c0:c0 + n, 2 * h * 128:(2 * h + 2) * 128], writes=[vtb])
                dma("sp", g_[0:n, :], ztk_d[c0:c0 + n, 1024 + 2 * h * 128:1024 + (2 * h + 2) * 128], writes=[g_])
                yield from wait_b()
                pt = free_b.pop(0)
                for dc in range(2):
                    tr(pt[0:n, dc * 128:(dc + 1) * 128], KRt[:, dc, tile], identb[:, :], [KRt, identb], [pt])
                yield
                ts("dve", ktk[0:n, :, :], pt[0:n, 0:256].rearrange("p (q c) -> p q c", c=128), cst[0:n, COFF["gk"] + 2 * 4 + h:COFF["gk"] + 2 * 4 + h + 1], None, ALU.mult, None, [pt, cst], [ktk])
                free_b.append(pt)
                yield
                yield from wait_f()
                ps_ = free_f.pop(0)
                mm(ps_[0:n, 0:n], KRt[:, 0, tile], QR[:, 0, tile], True, False, [KRt, QR], [ps_])
                mm(ps_[0:n, 0:n], KRt[:, 1, tile], QR[:, 1, tile], False, True, [KRt, QR], [ps_])
                yield
                dmv = cst[0:n, COFF["dms"]:COFF["dms"] + 256].rearrange("p (h i) -> p h i", i=64)[:, h, 0:n]
                tt("dve", scb[0:n, 0:n], ps_[0:n, 0:n], dmv, ALU.mult, [ps_, cst], [scb])
                free_f.append(ps_)
                gqsv = cst[:, COFF["gqs"]:COFF["gqs"] + 256].rearrange("p (h i) -> p h i", i=64)[:, h, :]
                bselv = cst[:, COFF["bsel"]:COFF["bsel"] + 1024].rearrange("p (b c) -> p b c", c=64)
                for dc in range(2):
                    tt("pool", QS[:, dc, :], QS[:, dc, :], gqsv, ALU.mult, [QS, cst], [QS])
                    tt("pool", qsm[:, dc, :, :], QS[:, dc, :].unsqueeze(1).broadcast_to([128, 16, 64]), bselv, ALU.mult, [QS, cst], [qsm])
                yield
                for b in range(16):
                    ts("dve", ktm[0:64, b, :], ktk[0:64, :, :].rearrange("p q c -> p (q c)"), cst[0:64, COFF["rsel"] + b:COFF["rsel"] + b + 1], None, ALU.mult, None, [ktk, cst], [ktm])
                    yield
                mm(pacc[0:64, 0:256], scb[0:64, 0:64], vtb[0:64, :], True, False, [scb, vtb], [pacc])
                for b in range(16):
                    si = Sin[b % 3]
                    dma("sp", si[:, :, :], srt_d[b, h, :, :].rearrange("(m two) e -> m two e", two=2), writes=[si])
                    yield
                    for dc in range(2):
                        mm(pacc[0:64, 0:256], qsm[:, dc, b, :], si[:, dc, :], False, (b == 15 and dc == 1), [qsm, si], [pacc])
                    for dc in range(2):
                        yield from wait_f()
                        pS = free_f.pop(0)
                        mm(pS[:, 0:256], ktm[0:64, b, dc * 128:(dc + 1) * 128], vtb[0:64, :], True, True, [ktm, vtb], [pS])
                        yield
                        stt(si[:, dc, :], si[:, dc, :], float(g ** 4), pS[:, 0:256], ALU.mult, ALU.add, [si, pS], [si])
                        free_f.append(pS)
                    dma("sp", rto_d[1 + b, h, :, :].rearrange("(m two) e -> m two e", two=2), si[:, :, :], reads=[si])
                    yield
                yield from ret_opost(pacc, 64, h, ("own", 1024), R_)
                yield from ret_opost2(64, h, ("own", 1024), R_)

            for h in range(4):
                for i_, dst in enumerate((qe, qo, ke, ko)):
                    ch = 27 + 4 * h + i_
                    dma("sp", dst[:, :], zT_d[ch * 128:(ch + 1) * 128, :], writes=[dst])
                for (xe, xo, OUT) in ((qe, qo, QR), (ke, ko, KRt)):
                    tt("dve", t1[:, :], xe[:, :], cosT[:, :], ALU.mult, [xe, cosT], [t1])
                    tt("pool", t2[:, :], xo[:, :], sinT[:, :], ALU.mult, [xo, sinT], [t2])
                    tt("dve", OUT[:, 0, :], t1[:, :], t2[:, :], ALU.subtract, [t1, t2], [OUT])
                    if OUT is QR:
                        tt("dve", QS[:, 0, :], t1[:, TP:T], t2[:, TP:T], ALU.subtract, [t1, t2], [QS])
                    tt("dve", t1[:, :], xe[:, :], sinT[:, :], ALU.mult, [xe, sinT], [t1])
                    tt("pool", t2[:, :], xo[:, :], cosT[:, :], ALU.mult, [xo, cosT], [t2])
                    tt("dve", OUT[:, 1, :], t1[:, :], t2[:, :], ALU.add, [t1, t2], [OUT])
                    if OUT is QR:
                        tt("dve", QS[:, 1, :], t1[:, TP:T], t2[:, TP:T], ALU.add, [t1, t2], [QS])
                for dc in range(2):
                    op("dve", lambda e, dc=dc: e.memset(Sr[dc][:, :], 0.0), [], [Sr[dc]])
                    op("dve", lambda e, dc=dc: e.memset(Srb[dc][:, :], 0.0), [], [Srb[dc]])
                P.barrier()
                free_f[:] = list(pbank[0:5])
                free_b[:] = list(pbf)
                rstate["next"] = 0
                makers = [(lambda R_, h=h: ret_sample_task(h, R_))]
                for ti in range(17):
                    makers.append(lambda R_, h=h, ti=ti: ret_task(h, ti, R_))
                run_tasks(makers, RSETS)
                P.barrier()
                prr[0] = 0
                pfr[0] = 0
            chk(8)
            if debug:
                dma("sp", dbg["oT"][:, :, :], oT[:], reads=[oT])
            P.flush()
        P.barrier()

        with ExitStack() as s3:
            gam = SBs(s3, "gam3", [128, D])
            env["junk"] = SBs(s3, "junk3", [128, D])
            hres = [SBs(s3, "h%d" % i, [128, D]) for i in range(9)]
            hnT = SBs(s3, "hnT", [128, KC, OWN], BF16)
            OT = [(i * 128, 128) for i in range(8)] + [(1024, 64)]
            with ExitStack() as s3a:
                wo = SBs(s3a, "wo", [128, KC, 512], BF16)
                xo_ = [SBs(s3a, "xo%d" % i, [128, 512]) for i in range(2)]
                for blk in range(4):
                    dma("pool", wo[:], w_out[blk, :, :, :], writes=[wo])
                    for ti, (c0, n) in enumerate(OT):
                        x_ = xo_[(blk * 9 + ti) % 2]
                        dma("sp", x_[0:n, :], x_own[c0:c0 + n, blk * 512:(blk + 1) * 512], writes=[x_])
                        pb_ = next_bank()
                        for kc in range(KC):
                            mm(pb_[0:n, :], oT[:, kc, c0:c0 + n], wo[:, kc, :], kc == 0, kc == KC - 1, [oT, wo], [pb_])
                        tt("dve", hres[ti][0:n, blk * 512:(blk + 1) * 512], pb_[0:n, :], x_[0:n, :], ALU.add, [pb_, x_], [hres[ti]])
                if debug:
                    for ti, (c0, n) in enumerate(OT):
                        dma("sp", dbg["h"][c0:c0 + n, :], hres[ti][0:n, :], reads=[hres[ti]])
                P.flush()
            P.barrier()
            with ExitStack() as s3b:
                hb = [SBs(s3b, "hb%d" % i, [128, D], BF16) for i in range(2)]
                dma("sp", gam[:], gam_d[1, :].partition_broadcast(128), writes=[gam])
                for ti, (c0, n) in enumerate(OT):
                    s_ = rms_rstd(hres[ti], n, ti % 2)
                    hb_ = hb[ti % 2]
                    stt(hb_[0:n, :], hres[ti][0:n, :], s_[0:n, 1:2], gam[0:n, :], ALU.mult, ALU.mult, [hres[ti], s_, gam], [hb_])
                    transpose16(hb_, n, hnT, c0)
                P.flush()
            P.barrier()
            with ExitStack() as s3c:
                G = 4
                wg = [SBs(s3c, "wg%d" % i, [128, KC, 128], BF16) for i in range(2)]
                wu = [SBs(s3c, "wu%d" % i, [128, KC, 128], BF16) for i in range(2)]
                actT = SBs(s3c, "actT", [128, G, OWN], BF16)
                sgl = SBs(s3c, "sgl", [128, 512])
                oflat = oT[:, :, :].rearrange("p k c -> p (k c)")
                wd = []
                for i in range(2 * G):
                    w_ = TL(oflat[:, i * D:(i + 1) * D])
                    w_.b.r = list(oT.b.r)
                    w_.b.w = oT.b.w
                    wd.append(w_)
                RB = [(0, 512), (512, 512), (1024, 64)]
                for grp in range(NFB // G):
                    for fi in range(G):
                        fb = grp * G + fi
                        g_ = wg[fb % 2]
                        u_ = wu[fb % 2]
                        d_ = wd[fb % (2 * G)]
                        dma("pool", g_[:], w_gate[fb, :, :, :], writes=[g_])
                        dma("pool", u_[:], w_up[fb, :, :, :], writes=[u_])
                        dma("pool", d_[:, :], w_down[fb, :, :], writes=[d_])
                        for (r0, rn) in RB:
                            pg = next_bank()
                            for kc in range(KC):
                                mm(pg[:, 0:rn], g_[:, kc, :], hnT[:, kc, r0:r0 + rn], kc == 0, kc == KC - 1, [g_, hnT], [pg])
                            pu_ = next_bank()
                            for kc in range(KC):
                                mm(pu_[:, 0:rn], u_[:, kc, :], hnT[:, kc, r0:r0 + rn], kc == 0, kc == KC - 1, [u_, hnT], [pu_])
                            act(sgl[:, 0:rn], pg[:, 0:rn], AF.Silu, [pg], [sgl])
                            tt("dve", actT[:, fi, r0:r0 + rn], sgl[:, 0:rn], pu_[:, 0:rn], ALU.mult, [sgl, pu_], [actT])
                    for ti, (c0, n) in enumerate(OT):
                        for blk in range(4):
                            pb_ = next_bank()
                            for fi in range(G):
                                d_ = wd[(grp * G + fi) % (2 * G)]
                                mm(pb_[0:n, :], actT[:, fi, c0:c0 + n], d_[:, blk * 512:(blk + 1) * 512], fi == 0, fi == G - 1, [actT, d_], [pb_])
                            tt("dve", hres[ti][0:n, blk * 512:(blk + 1) * 512], hres[ti][0:n, blk * 512:(blk + 1) * 512], pb_[0:n, :], ALU.add, [hres[ti], pb_], [hres[ti]])
                dma("sp", gam[:], gam_d[2, :].partition_broadcast(128), writes=[gam])
                for ti, (c0, n) in enumerate(OT):
                    s_ = rms_rstd(hres[ti], n, ti % 2)
                    stt(hres[ti][0:n, :], hres[ti][0:n, :], s_[0:n, 1:2], gam[0:n, :], ALU.mult, ALU.mult, [hres[ti], s_, gam], [hres[ti]])
                    dma("sp", y_d[c0:c0 + n, :], hres[ti][0:n, :], reads=[hres[ti]])
                P.finish()
                P.flush()


_NC_CACHE = {}


def _prep_weights(inp):
    f = lambda a: np.ascontiguousarray(np.asarray(a, dtype=np.float32))
    w_in = f(inp["w_in"])[0]
    cols = list(range(0, 3360)) + [-1] * 96
    for h in range(4):
        for base in (3360, 3360 + 1024):
            cols += list(range(base + h * 256, base + (h + 1) * 256, 2))
            cols += list(range(base + h * 256 + 1, base + (h + 1) * 256, 2))
    fm = list(range(0, 3360)) + [-1] * 96
    for h in range(4):
        qb, kb = 3360 + h * 256, 3360 + 1024 + h * 256
        fm += list(range(qb, qb + 256, 2)) + list(range(qb + 1, qb + 256, 2))
        fm += list(range(kb, kb + 256, 2)) + list(range(kb + 1, kb + 256, 2))
    fm += list(range(3360 + 2048, 3360 + 4096))
    fm = np.array(fm)
    wp = np.zeros((D, NCH * 128), np.float32)
    valid = fm >= 0
    wp[:, valid] = w_in[:, fm[valid]]
    w_in_r = np.ascontiguousarray(wp.reshape(KC, 128, NCH, 128).transpose(2, 1, 0, 3))
    w_out = f(inp["w_out"])[0]
    w_out_r = np.ascontiguousarray(w_out.reshape(KC, 128, 4, 512).transpose(2, 1, 0, 3))
    wg = np.ascontiguousarray(f(inp["w_gate"])[0].reshape(KC, 128, NFB, 128).transpose(2, 1, 0, 3))
    wu = np.ascontiguousarray(f(inp["w_up"])[0].reshape(KC, 128, NFB, 128).transpose(2, 1, 0, 3))
    wd = np.ascontiguousarray(f(inp["w_down"])[0].reshape(NFB, 128, D))
    gam = np.stack([f(inp["norm_mix"])[0], f(inp["norm_ffn"])[0], f(inp["norm_final"])])
    mu = np.zeros((27 * 128,), np.float32)
    mu[:3360] = f(inp["rwkv_mu"])[0]
    mu = np.ascontiguousarray(mu.reshape(27, 128).T)
    names = ["rwkv_w0", "rwkv_a0", "rwkv_kk", "rwkv_ka", "rwkv_rk", "rwkv_ln_w", "rwkv_ln_b"]
    prm = np.stack([f(inp[n_])[0].reshape(8, 128).T for n_ in names], axis=1)
    return dict(w_in=w_in_r, w_out=w_out_r, w_gate=wg, w_up=wu, w_down=wd, gam=np.ascontiguousarray(gam), mu=mu,
                prm=np.ascontiguousarray(prm), w2=f(inp["rwkv_w2"])[0], a2=f(inp["rwkv_a2"])[0], g2=f(inp["rwkv_g2"])[0],
                cst=CST, cos=COS_T, sin=SIN_T)


def kernel(**inp):
    f = lambda a: np.asarray(a, dtype=np.float32)
    shared = _prep_weights(inp)
    xp = f(inp["x_prompt"])
    xsm = f(inp["x_sample"])
    meta = f(inp["meta_tokens"])
    ssh = f(inp["state_shift"])[0]
    srw = f(inp["state_rwkv"])[0]
    srt = f(inp["state_ret"])[0]
    in_maps = []
    for c in range(8):
        b, hh = c // 2, c % 2
        xs_c = xsm[16 * c:16 * c + 16].reshape(64, D)
        xs = np.concatenate([meta, xp[b], xs_c], axis=0)
        x_own = np.concatenate([xp[b][hh * 1024:(hh + 1) * 1024], xs_c], axis=0)
        sel = np.zeros((128, 2), np.float32)
        sel[:, hh] = 1.0
        sshp = np.zeros((16, 27 * 128), np.float32)
        sshp[:, :3360] = ssh[16 * c:16 * c + 16]
        ssh_r = np.ascontiguousarray(sshp.reshape(16, 27, 128).transpose(1, 2, 0))
        m = dict(shared)
        m.update(xs=np.ascontiguousarray(xs), x_own=np.ascontiguousarray(x_own), sel=sel, ssh=ssh_r,
                 srw=np.ascontiguousarray(srw[16 * c:16 * c + 16]), srt=np.ascontiguousarray(srt[16 * c:16 * c + 16]))
        in_maps.append(m)
    if inp.get("_maps_only"):
        return in_maps
    if "nc" not in _NC_CACHE:
        _NC_CACHE["nc"] = build()
    res = run_bass_kernel_spmd(_NC_CACHE["nc"], in_maps, core_ids=list(range(8)))
    R = res.results
    y_prompt = np.zeros((4, SEQ, D), np.float32)
    y_sample = np.zeros((128, 4, D), np.float32)
    shift_p = np.zeros((1, 4, 3360), np.float32)
    rwkv_p = np.zeros((1, 4, 16, 64, 64), np.float32)
    ret_p = np.zeros((1, 4, 4, 256, 256), np.float32)
    shift_s = np.zeros((1, 128, 3360), np.float32)
    rwkv_s = np.zeros((1, 128, 16, 64, 64), np.float32)
    ret_s = np.zeros((1, 128, 4, 256, 256), np.float32)
    for c in range(8):
        b, hh = c // 2, c % 2
        r = R[c]
        y_prompt[b, hh * 1024:(hh + 1) * 1024] = r["y"][:1024]
        y_sample[16 * c:16 * c + 16] = r["y"][1024:].reshape(16, 4, D)
        shift_s[0, 16 * c:16 * c + 16] = r["sho"][1:17, :3360]
        rwkv_s[0, 16 * c:16 * c + 16] = r["rwo"][1:17]
        ret_s[0, 16 * c:16 * c + 16] = r["rto"][1:17]
        if hh == 0:
            shift_p[0, b] = r["sho"][0, :3360]
            rwkv_p[0, b] = r["rwo"][0]
            ret_p[0, b] = r["rto"][0]
    return (y_prompt, y_sample, shift_p, rwkv_p, ret_p, shift_s, rwkv_s, ret_s)
```

```python
import numpy as np
from contextlib import ExitStack
import concourse.bass as bass
import concourse.mybir as mybir
from concourse.bass_utils import run_bass_kernel_spmd

F32 = mybir.dt.float32
BF16 = mybir.dt.bfloat16
ALU = mybir.AluOpType
AF = mybir.ActivationFunctionType
AX = mybir.AxisListType

D = 2048
KC = 16
NMETA = 16
SEQ = 2048
TP = NMETA + SEQ
NS = 64
T = TP + NS
NCH = 59
NFM = 43
DFF = 5632
NFB = DFF // 128
OWN = 1024 + NS
C0 = float(np.exp(-0.5))
GAM = [1.0 - 2.0 ** (-5.0 - h) for h in range(4)]
TILES = [(0, 16)] + [(16 + 128 * i, 128) for i in range(16)] + [(TP, NS)]
BLKS = [(0, 512), (512, 512), (1024, 512), (1536, 512), (2048, T - 2048)]


class Buf:
    __slots__ = ("w", "r")

    def __init__(self):
        self.w = None
        self.r = []


class Prog:
    ENG = ("pe", "act", "dve", "pool", "sp")

    def __init__(self, nc, n_dma=48):
        self.nc = nc
        self.ops = {k: [] for k in self.ENG}
        self.cnt = {k: 0 for k in self.ENG}
        self.seen = {k: {} for k in self.ENG}
        self.n_dma = n_dma
        self.dma_use = [0] * n_dma
        self.dma_rr = 0
        self.dma_rr_pool = 0
        self.n_sp = n_dma - 12

    def _need(self, waits, eng, ev, kind):
        if ev is None:
            return
        sk, val, src = ev
        if src == eng and not isinstance(sk, tuple):
            if eng == "pe":
                return
        if waits.get(sk, 0) < val:
            waits[sk] = val

    def op(self, eng, fn, reads=(), writes=(), dma=False):
        waits = {}
        for b in reads:
            self._need(waits, eng, b.w, "raw")
        for b in writes:
            self._need(waits, eng, b.w, "waw")
            for ev in b.r:
                self._need(waits, eng, ev, "war")
        if dma:
            if eng == "pool":
                idx = self.n_sp + self.dma_rr_pool
                self.dma_rr_pool = (self.dma_rr_pool + 1) % (self.n_dma - self.n_sp)
            else:
                idx = self.dma_rr
                self.dma_rr = (self.dma_rr + 1) % self.n_sp
            sk = ("dma", idx)
            prev = 16 * self.dma_use[idx]
            if prev > 0 and waits.get(sk, 0) < prev:
                waits[sk] = prev
            self.dma_use[idx] += 1
            ev = (sk, 16 * self.dma_use[idx], eng)
            inc = (sk, 16)
        else:
            self.cnt[eng] += 1
            ev = (eng, self.cnt[eng], eng)
            inc = (eng, 1)
        seen = self.seen[eng]
        wl = []
        for sk, val in waits.items():
            if seen.get(sk, 0) < val:
                seen[sk] = val
                wl.append((sk, val))
        self.ops[eng].append((wl, fn, inc))
        for b in writes:
            b.w = ev
            b.r = []
        for b in reads:
            b.r.append(ev)
        return ev

    def barrier(self):
        allw = [(("dma", i), 16 * self.dma_use[i]) for i in range(self.n_dma) if self.dma_use[i] > 0]
        allw += [(k, self.cnt[k]) for k in self.ENG if self.cnt[k] > 0]
        for k in self.ENG:
            wl = []
            for sk, val in allw:
                if sk == k:
                    continue
                if self.seen[k].get(sk, 0) < val:
                    self.seen[k][sk] = val
                    wl.append((sk, val))
            self.ops[k].append((wl, None, None))

    def finish(self):
        waits = []
        for i in range(self.n_dma):
            if self.dma_use[i] > 0:
                waits.append((("dma", i), 16 * self.dma_use[i]))
        for k in self.ENG:
            if k != "sp" and self.cnt[k] > 0:
                waits.append((k, self.cnt[k]))
        self.ops["sp"].append((waits, None, None))

    def begin(self, st):
        nc = self.nc
        self.sems = {}
        for k in self.ENG:
            self.sems[k] = st.enter_context(nc.semaphore("s_" + k))
        for i in range(self.n_dma):
            self.sems[("dma", i)] = st.enter_context(nc.semaphore("d_%d" % i))

    def flush(self):
        nc = self.nc
        sems = self.sems
        with nc.Block() as block:

            def run(e, lst):
                for wl, fn, inc in lst:
                    for sk, val in wl:
                        e.wait_ge(sems[sk], val)
                    if fn is None:
                        continue
                    fn(e).then_inc(sems[inc[0]], inc[1])

            @block.tensor
            def _(e):
                run(e, self.ops["pe"])

            @block.scalar
            def _(e):
                run(e, self.ops["act"])

            @block.vector
            def _(e):
                run(e, self.ops["dve"])

            @block.gpsimd
            def _(e):
                run(e, self.ops["pool"])

            @block.sync
            def _(e):
                run(e, self.ops["sp"])
        self.ops = {k: [] for k in self.ENG}


class TL:
    def __init__(self, t):
        self.t = t
        self.b = Buf()

    def __getitem__(self, k):
        return self.t[k]


def host_consts():
    c = {}
    idx = np.arange(128)
    su = (idx[:, None] < idx[None, :]).astype(np.float32)
    ui = (idx[:, None] <= idx[None, :]).astype(np.float32)
    sl = (idx[:, None] > idx[None, :]).astype(np.float32)
    bd = ((idx[:, None] // 64) == (idx[None, :] // 64)).astype(np.float32)
    dm = np.zeros((128, 4, 128), np.float64)
    gq = np.zeros((128, 4, 128), np.float64)
    gk = np.zeros((128, 3, 4), np.float64)
    dms = np.zeros((128, 4, 64), np.float64)
    gqs = np.zeros((128, 4, 64), np.float64)
    for h in range(4):
        g = GAM[h]
        dd = idx[None, :] - idx[:, None]
        dm[:, h, :] = np.where(dd >= 0, g ** np.maximum(dd, 0), 0.0)
        gq[:, h, :] = g ** (idx[None, :] + 1.0)
        gk[:16, 0, h] = g ** (15.0 - idx[:16])
        gk[:, 1, h] = g ** (127.0 - idx)
        gk[:64, 2, h] = g ** (3.0 - (idx[:64] % 4))
        j = idx[:64]
        same = (j[:, None] // 4) == (j[None, :] // 4)
        dt_ = (j[None, :] % 4) - (j[:, None] % 4)
        dms[:64, h, :] = np.where(same & (dt_ >= 0), g ** np.maximum(dt_, 0), 0.0)
        gqs[:, h, :] = g ** ((j[None, :] % 4) + 1.0)
    bsel = np.zeros((128, 16, 64), np.float32)
    for b in range(16):
        bsel[:, b, 4 * b:4 * b + 4] = 1.0
    rsel = np.zeros((128, 16), np.float32)
    for r in range(64):
        rsel[r, r // 4] = 1.0
    m4 = np.concatenate([su, ui, su, ui], axis=1)
    blk = ((idx[:, None] // 4) == (idx[None, :] // 4)).astype(np.float32)
    m4s = np.concatenate([su * blk, ui * blk, su * blk, ui * blk], axis=1)
    slb = sl * blk
    parts = [su, ui, sl, bd, m4, m4s, slb, dm.reshape(128, -1), gq.reshape(128, -1), gk.reshape(128, -1),
             dms.reshape(128, -1), gqs.reshape(128, -1), bsel.reshape(128, -1), rsel]
    offs = {}
    o = 0
    names = ["su", "ui", "sl", "bd", "m4", "m4s", "slb", "dm", "gq", "gk", "dms", "gqs", "bsel", "rsel"]
    for n_, p in zip(names, parts):
        offs[n_] = o
        o += p.shape[1]
    cst = np.concatenate([p.astype(np.float32) for p in parts], axis=1)
    inv_freq = (1.0 / (10000.0 ** np.linspace(0.0, 1.0, 128, dtype=np.float32))).astype(np.float32)
    pos = np.concatenate([np.arange(TP), np.tile(16384 + np.arange(4), 16)]).astype(np.float32)
    ang = (pos[None, :] * inv_freq[:, None]).astype(np.float32)
    cs = np.cos(ang.astype(np.float64)).astype(np.float32)
    sn = np.sin(ang.astype(np.float64)).astype(np.float32)
    return cst, offs, cs, sn


CST, COFF, COS_T, SIN_T = host_consts()
NCST = CST.shape[1]


def build(debug=False, stop=None):
    nc = bass.Bass("TRN2", target_bir_lowering=False)
    DT = lambda name, shape, dt=F32, kind="ExternalInput": nc.dram_tensor(name, list(shape), dt, kind=kind).ap()
    xs = DT("xs", [T, D])
    x_own = DT("x_own", [OWN, D])
    sel_d = DT("sel", [128, 2])
    w_in = DT("w_in", [NCH, 128, KC, 128])
    w_out = DT("w_out", [4, 128, KC, 512])
    w_gate = DT("w_gate", [NFB, 128, KC, 128])
    w_up = DT("w_up", [NFB, 128, KC, 128])
    w_down = DT("w_down", [NFB, 128, D])
    gam_d = DT("gam", [3, D])
    mu_d = DT("mu", [128, 27])
    prm_d = DT("prm", [128, 7, 8])
    w2_d = DT("w2", [64, 1024])
    a2_d = DT("a2", [64, 1024])
    g2_d = DT("g2", [160, 1024])
    ssh_d = DT("ssh", [27, 128, 16])
    srw_d = DT("srw", [16, 16, 64, 64])
    srt_d = DT("srt", [16, 4, 256, 256])
    cst_d = DT("cst", [128, NCST])
    cos_d = DT("cos", [128, T])
    sin_d = DT("sin", [128, T])
    y_d = DT("y", [OWN, D], kind="ExternalOutput")
    sho_d = DT("sho", [17, 27 * 128], kind="ExternalOutput")
    rwo_d = DT("rwo", [17, 16, 64, 64], kind="ExternalOutput")
    rto_d = DT("rto", [17, 4, 256, 256], kind="ExternalOutput")
    zT_d = nc.dram_tensor("zT_scr", [NFM * 128, T], F32).ap()
    ztk_d = nc.dram_tensor("ztk_scr", [T, 16 * 128], F32).ap()
    dbg = {}
    if debug:
        dbg["oT"] = DT("dbg_oT", [128, KC, OWN], BF16, kind="ExternalOutput")
        dbg["h"] = DT("dbg_h", [OWN, D], kind="ExternalOutput")

    P = Prog(nc)
    try:
        _build_body(nc, P, debug, stop, locals())
    except _Stop:
        pass
    return nc


class _Stop(Exception):
    pass


def _build_body(nc, P, debug, stop, L):
    globals().update({k: v for k, v in L.items() if k not in ("nc", "P", "debug", "stop")})
    (xs, x_own, sel_d, w_in, w_out, w_gate, w_up, w_down, gam_d, mu_d, prm_d, w2_d, a2_d, g2_d, ssh_d, srw_d, srt_d,
     cst_d, cos_d, sin_d, y_d, sho_d, rwo_d, rto_d, zT_d, ztk_d, dbg) = [L[k] for k in (
        "xs", "x_own", "sel_d", "w_in", "w_out", "w_gate", "w_up", "w_down", "gam_d", "mu_d", "prm_d", "w2_d", "a2_d",
        "g2_d", "ssh_d", "srw_d", "srt_d", "cst_d", "cos_d", "sin_d", "y_d", "sho_d", "rwo_d", "rto_d", "zT_d", "ztk_d", "dbg")]

    def chk(k):
        if stop == k:
            P.finish()
            P.flush()
            raise _Stop()
    with ExitStack() as st:
        P.begin(st)
        def SBs(stack, name, shape, dt=F32):
            return TL(stack.enter_context(nc.sbuf_tensor("sb_" + name, list(shape), dt)))

        def SB(name, shape, dt=F32):
            return SBs(st, name, shape, dt)

        def PS(name, shape, dt=F32):
            return TL(st.enter_context(nc.psum_tensor(name, list(shape), dt)))

        def dma(eng, out, in_, reads=(), writes=()):
            P.op(eng, lambda e: e.dma_start(out=out, in_=in_), reads=[r.b for r in reads], writes=[w.b for w in writes], dma=True)

        def op(eng, fn, reads=(), writes=()):
            P.op(eng, fn, reads=[r.b for r in reads], writes=[w.b for w in writes])

        def mm(out, lhsT, rhs, start, stop, reads, writes):
            op("pe", lambda e: e.matmul(out, lhsT=lhsT, rhs=rhs, start=start, stop=stop), reads, writes)

        def tr(out, in_, ident, reads, writes):
            op("pe", lambda e: e.transpose(out=out, in_=in_, identity=ident), reads, writes)

        def tt(eng, out, a, b, o, reads, writes):
            op(eng, lambda e: e.tensor_tensor(out=out, in0=a, in1=b, op=o), reads, writes)

        def ts(eng, out, a, s1, s2, o0, o1, reads, writes):
            if s2 is None:
                op(eng, lambda e: e.tensor_scalar(out=out, in0=a, scalar1=s1, scalar2=None, op0=o0), reads, writes)
            else:
                op(eng, lambda e: e.tensor_scalar(out=out, in0=a, scalar1=s1, scalar2=s2, op0=o0, op1=o1), reads, writes)

        def stt(out, a, s, b, o0, o1, reads, writes):
            op("dve", lambda e: e.scalar_tensor_tensor(out=out, in0=a, scalar=s, in1=b, op0=o0, op1=o1), reads, writes)

        def act(out, in_, func, reads, writes, bias=0.0, scale=1.0):
            op("act", lambda e: e.activation(out=out, in_=in_, func=func, bias=bias, scale=scale), reads, writes)

        cp_i = [0]

        def cp(out, in_, reads, writes, eng=None):
            if eng is None:
                eng = "act" if cp_i[0] % 2 == 0 else "dve"
                cp_i[0] += 1
            if eng == "act":
                op("act", lambda e: e.copy(out=out, in_=in_), reads, writes)
            else:
                op(eng, lambda e: e.tensor_copy(out=out, in_=in_), reads, writes)

        pbank = [PS("pb%d" % i, [128, 512]) for i in range(6)]
        pbf = [PS("pbf%d" % i, [128, 1024], BF16) for i in range(2)]
        prr = [0]
        pfr = [0]

        def next_bank():
            b_ = pbank[prr[0] % 5]
            prr[0] += 1
            return b_

        def next_bf():
            b_ = pbf[pfr[0] % 2]
            pfr[0] += 1
            return b_
        pacc = pbank[5]

        cst = SB("cst", [128, NCST])
        dma("sp", cst[:], cst_d[:, :], writes=[cst])

        def CS(n_, w, p0=0, p1=128):
            return cst[p0:p1, COFF[n_]:COFF[n_] + w]
        sel = SB("sel", [128, 2])
        dma("sp", sel[:], sel_d[:, :], writes=[sel])
        identf = SB("identf", [128, 128])
        identb = SB("identb", [128, 128], BF16)
        ones = SB("ones", [128, 128])
        op("pool", lambda e: e.memset(identf[:], 0.0), writes=[identf])
        op("pool", lambda e: e.affine_select(out=identf[:], in_=identf[:], pattern=[[-1, 128]], compare_op=ALU.not_equal, fill=1.0, base=0, channel_multiplier=1), reads=[identf], writes=[identf])
        op("dve", lambda e: e.tensor_copy(out=identb[:], in_=identf[:]), reads=[identf], writes=[identb])
        op("dve", lambda e: e.memset(ones[:], 1.0), writes=[ones])
        oT = SB("oT", [128, KC, OWN], BF16)
        sshT = SB("sshT", [128, 27, 16])
        dma("sp", sshT[:], ssh_d.rearrange("c p b -> p c b"), writes=[sshT])
        mu = SB("mu", [128, 27])
        dma("sp", mu[:], mu_d[:, :], writes=[mu])
        prm = SB("prm", [128, 7, 8])
        dma("sp", prm[:], prm_d[:, :, :], writes=[prm])
        omka = SB("omka", [128, 8])
        ts("dve", omka[:], prm[:, 3, :], -1.0, 1.0, ALU.mult, ALU.add, [prm], [omka])
        stat = [SB("stat%d" % i, [128, 4]) for i in range(2)]
        env = {}

        def rms_rstd(src_tile, n, sti):
            junk = env["junk"]
            s_ = stat[sti]
            op("act", lambda e: e.activation(out=junk[0:n, :], in_=src_tile[0:n, :], func=AF.Square, accum_out=s_[0:n, 0:1]), reads=[src_tile], writes=[junk, s_])
            ts("dve", s_[0:n, 1:2], s_[0:n, 0:1], 1.0 / D, 1e-6, ALU.mult, ALU.add, [s_], [s_])
            op("act", lambda e: e.sqrt(out=s_[0:n, 1:2], in_=s_[0:n, 1:2]), reads=[s_], writes=[s_])
            op("dve", lambda e: e.reciprocal(out=s_[0:n, 1:2], in_=s_[0:n, 1:2]), reads=[s_], writes=[s_])
            return s_

        def transpose16(xb_, n, dst, c0):
            for half in range(2):
                pt = next_bf()
                for k8 in range(8):
                    kc = half * 8 + k8
                    tr(pt[:, k8 * 128:k8 * 128 + n], xb_[0:n, kc * 128:(kc + 1) * 128], identb[0:n, 0:n], [xb_, identb], [pt])
                src = pt[:, :].rearrange("p (k c) -> p k c", c=128)[:, :, 0:n]
                cp(dst[:, half * 8:half * 8 + 8, c0:c0 + n], src, [pt], [dst])

        with ExitStack() as s1:
            gam = SBs(s1, "gam1", [128, D])
            env["junk"] = SBs(s1, "junk1", [128, D])
            xnT = SBs(s1, "xnT", [128, KC, T], BF16)
            xt = [SBs(s1, "xt%d" % i, [128, D]) for i in range(2)]
            xnb = [SBs(s1, "xnb%d" % i, [128, D], BF16) for i in range(2)]
            wt = [SBs(s1, "wt%d" % i, [128, KC, 128], BF16) for i in range(4)]
            zst = [SBs(s1, "zst%d" % i, [128, T]) for i in range(2)]
            ztk = [SBs(s1, "ztk%d" % i, [128, 18, 128]) for i in range(1)]
            shs = [SBs(s1, "shs%d" % i, [17, 128]) for i in range(2)]
            xl = SBs(s1, "xl", [128, KC, 17], BF16)
            dma("sp", gam[:], gam_d[0, :].partition_broadcast(128), writes=[gam])
            for ti, (c0, n) in enumerate(TILES):
                x_ = xt[ti % 2]
                xb_ = xnb[ti % 2]
                dma("sp", x_[0:n, :], xs[c0:c0 + n, :], writes=[x_])
                s_ = rms_rstd(x_, n, ti % 2)
                stt(xb_[0:n, :], x_[0:n, :], s_[0:n, 1:2], gam[0:n, :], ALU.mult, ALU.mult, [x_, s_, gam], [xb_])
                transpose16(xb_, n, xnT, c0)
            cp(xl[:, :, 1:17], xnT[:, :, TP + 3:T:4], [xnT], [xl], eng="dve")
            cp(xl[:, :, 0:1], xnT[:, :, TP - 1:TP], [xnT], [xl], eng="dve")
            chk(0)
            for ch in range(NCH):
                w_ = wt[ch % 4]
                dma("pool", w_[:], w_in[ch, :, :, :], writes=[w_])
                if ch < NFM:
                    z_ = zst[ch % 2]
                    scale = (1.0 / 16.0) if (ch >= 27 and (ch - 27) % 4 >= 2) else 1.0
                    for (b0, bn) in BLKS:
                        pb_ = next_bank()
                        for kc in range(KC):
                            mm(pb_[:, 0:bn], w_[:, kc, :], xnT[:, kc, b0:b0 + bn], kc == 0, kc == KC - 1, [w_, xnT], [pb_])
                        if cp_i[0] % 2 == 0:
                            op("act", lambda e, z_=z_, pb_=pb_, b0=b0, bn=bn, scale=scale: e.mul(out=z_[:, b0:b0 + bn], in_=pb_[:, 0:bn], mul=scale), reads=[pb_], writes=[z_])
                        else:
                            ts("dve", z_[:, b0:b0 + bn], pb_[:, 0:bn], scale, None, ALU.mult, None, [pb_], [z_])
                        cp_i[0] += 1
                    dma("sp", zT_d[ch * 128:(ch + 1) * 128, :], z_[:], reads=[z_])
                    if ch < 27:
                        pb_ = next_bank()
                        for kc in range(KC):
                            mm(pb_[0:17, 0:128], xl[:, kc, :], w_[:, kc, :], kc == 0, kc == KC - 1, [w_, xl], [pb_])
                        cp(shs[ch % 2][0:17, :], pb_[0:17, 0:128], [pb_], [shs[ch % 2]])
                        dma("sp", sho_d[0:17, ch * 128:(ch + 1) * 128], shs[ch % 2][0:17, :], reads=[shs[ch % 2]])
                else:
                    zk_ = ztk[0]
                    for g4 in range(5):
                        pb_ = next_bank()
                        tl_ = list(range(g4 * 4, min(g4 * 4 + 4, 18)))
                        for q_, ti in enumerate(tl_):
                            c0, n = TILES[ti]
                            for kc in range(KC):
                                mm(pb_[0:n, q_ * 128:(q_ + 1) * 128], xnT[:, kc, c0:c0 + n], w_[:, kc, :], kc == 0, kc == KC - 1, [w_, xnT], [pb_])
                        nt = len(tl_)
                        src = pb_[:, 0:nt * 128].rearrange("p (q c) -> p q c", c=128)
                        cp(zk_[:, g4 * 4:g4 * 4 + nt, :], src, [pb_], [zk_])
                    cc = ch - NFM
                    dma("sp", ztk_d[0:16, cc * 128:(cc + 1) * 128], zk_[0:16, 0, :], reads=[zk_])
                    dma("sp", ztk_d[16:TP, cc * 128:(cc + 1) * 128].rearrange("(t p) c -> p t c", p=128), zk_[:, 1:17, :], reads=[zk_])
                    dma("sp", ztk_d[TP:T, cc * 128:(cc + 1) * 128], zk_[0:64, 17, :], reads=[zk_])
            P.flush()

        P.barrier()

        with ExitStack() as s2:
            FB = [SBs(s2, "fb%d" % i, [128, T]) for i in range(10)]
            za, zp, rs, ks, vs, sg, av, gT, Cb, kkn = FB
            t1, t2 = za, zp
            bonus = SBs(s2, "bonus", [128, T], BF16)
            junk = SBs(s2, "junk2", [128, 256])
            AR = SBs(s2, "AR", [128, 2, T], BF16)
            KRt = SBs(s2, "KRt", [128, 2, T], BF16)
            vsb = SBs(s2, "vsb", [128, T], BF16)
            la = SBs(s2, "la", [128, T], BF16)
            sgd = SBs(s2, "sgd", [128, T], BF16)
            sgd2 = SBs(s2, "sgd2", [32, T], BF16)
            wa2 = SBs(s2, "wa2", [128, 1024], BF16)
            g2a = SBs(s2, "g2a", [128, 1024], BF16)
            g2b = SBs(s2, "g2b", [32, 1024], BF16)
            pcl = SBs(s2, "pcl", [128, 40])
            dma("pool", wa2[0:64, :], w2_d[:, :], writes=[wa2])
            dma("pool", wa2[64:128, :], a2_d[:, :], writes=[wa2])
            dma("pool", g2a[:, :], g2_d[0:128, :], writes=[g2a])
            dma("pool", g2b[:, :], g2_d[128:160, :], writes=[g2b])

            def load_only(ch, out):
                dma("sp", out[:, :], zT_d[ch * 128:(ch + 1) * 128, :], writes=[out])

            def shift_only(ch, out):
                tt("dve", zp[:, 1:T], out[:, 0:T - 1], out[:, 1:T], ALU.subtract, [out], [zp])
                ts("dve", zp[:, 0:1], out[:, 0:1], -1.0, None, ALU.mult, None, [out], [zp])
                tt("dve", zp[:, TP:T:4], sshT[:, ch, :], out[:, TP:T:4], ALU.subtract, [sshT, out], [zp])
                stt(out[:, :], zp[:, :], mu[:, ch:ch + 1], out[:, :], ALU.mult, ALU.add, [zp, mu, out], [out])

            def load_shift(ch, out, out_dt_tile=None):
                load_only(ch, out)
                shift_only(ch, out)

            load_shift(24, rs)
            act(la[0:64, :], rs[0:64, :], AF.Tanh, [rs], [la])
            cp(la[64:128, :], rs[64:128, :], [rs], [la], eng="dve")
            load_shift(25, ks)
            act(sgd[:, :], ks[:, :], AF.Sigmoid, [ks], [sgd])
            load_shift(26, vs)
            act(sgd2[0:32, :], vs[0:32, :], AF.Sigmoid, [vs], [sgd2])

            chk(2)
            onb = SBs(s2, "onb", [128, 256], BF16)
            of32 = SBs(s2, "of32", [128, 128])
            PSst = SBs(s2, "PSst", [128, 128])
            PSbb = SBs(s2, "PSbb", [128, 128], BF16)

            def write_oT(kc, slot, src, n, src_tiles, first, eng="dve"):
                if slot == "s":
                    return
                if slot[0] == "own":
                    c0 = slot[1]
                    cp(oT[:, kc, c0:c0 + n], src, src_tiles, [oT], eng=eng)
                else:
                    _, s_i, second = slot
                    c0 = s_i * 128
                    if not second:
                        ts(eng, oT[:, kc, c0:c0 + n], src, sel[:, 0:1], None, ALU.mult, None, src_tiles + [sel], [oT])
                    elif eng == "dve":
                        stt(oT[:, kc, c0:c0 + n], src, sel[:, 1:2], oT[:, kc, c0:c0 + n], ALU.mult, ALU.add, src_tiles + [sel, oT], [oT])
                    else:
                        ts(eng, src, src, sel[:, 1:2], None, ALU.mult, None, src_tiles + [sel], src_tiles)
                        tt(eng, oT[:, kc, c0:c0 + n], oT[:, kc, c0:c0 + n], src, ALU.add, src_tiles + [oT], [oT])

            def slot_of(ti, b=None):
                if ti == 0:
                    return "s"
                if ti <= 16:
                    return ("pred", (ti - 1) % 8, ti > 8)
                return ("own", 1024 + 4 * b)

            class Carve:
                def __init__(self, bufs):
                    self.bufs = bufs
                    self.i = 0
                    self.off = 0

                def get(self, shape, dt):
                    nelem = int(np.prod(shape[1:]))
                    nf = (nelem * (2 if dt == BF16 else 4) + 3) // 4
                    nf = (nf + 7) // 8 * 8
                    if self.off + nf > T:
                        self.i += 1
                        self.off = 0
                    fb = self.bufs[self.i]
                    v = fb.t[0:shape[0], self.off:self.off + nf]
                    self.off += nf
                    if dt == BF16:
                        v = v.bitcast(BF16)
                    v = v[:, 0:nelem]
                    if len(shape) == 3:
                        v = v.rearrange("p (a b) -> p a b", b=shape[2])
                    return TL(v)

            class SetNS:
                pass

            import os as _os
            NSET = int(_os.environ.get('KB_NSET', '5'))
            cv = Carve([za, zp, rs, ks, vs, sg, av, Cb, kkn])
            SETS = []
            for si in range(NSET):
                S_ = SetNS()
                S_.tokb = cv.get([128, 3, 128], BF16)
                S_.A4 = [cv.get([128, 4, 128], BF16) for h in range(2)]
                S_.MM = [cv.get([128, 4, 128], BF16) for i in range(2)]
                S_.Xb = cv.get([128, 2, 128], BF16)
                S_.Wb = cv.get([128, 128], BF16)
                S_.UTb = cv.get([128, 128], BF16)
                S_.Sbb = cv.get([128, 128], BF16)
                S_.onb = cv.get([128, 128], BF16)
                S_.Sst = cv.get([128, 128], F32)
                S_.tmpS = cv.get([128, 128], F32)
                S_.of32 = cv.get([128, 128], F32)
                S_.bst = cv.get([128, 2, 6], F32)
                S_.mv = cv.get([128, 2, 2], F32)
                S_.Sn = cv.get([64, 2, 64], F32)
                S_.T2 = cv.get([128, 64], F32)
                S_.So = cv.get([64, 2, 64], F32)
                SETS.append(S_)

            SG = SetNS()
            SG.SnA = cv.get([64, 8, 128], F32)
            SG.SstA = cv.get([128, 8, 128], F32)
            SG.SbbA = cv.get([128, 8, 128], BF16)
            SG.am = cv.get([128, 16, 32], BF16)
            SG.tokm = cv.get([128, 8, 256], BF16)
            SG.T2A = cv.get([128, 8, 64], F32)
            SG.tmp4 = cv.get([128, 4, 128], F32)

            free_f = []
            free_b = []

            def wait_f():
                while not free_f:
                    yield

            def wait_b():
                while not free_b:
                    yield

            pstate = {"next": 0}
            eps_gn = SBs(s2, "eps_gn", [128, 1])
            op("dve", lambda e: e.memset(eps_gn[:, :], 64e-5), [], [eps_gn])

            def rwkv_pre(S_, tile, n, lv, mk4, msl):
                tokb, A4, MM, Xb = S_.tokb, S_.A4, S_.MM, S_.Xb
                yield from wait_b()
                pt = free_b.pop(0)
                tr(pt[0:n, 0:128], KRt[:, 0, tile], identb[:, :], [KRt, identb], [pt])
                tr(pt[0:n, 128:256], KRt[:, 1, tile], identb[:, :], [KRt, identb], [pt])
                tr(pt[0:n, 256:384], vsb[:, tile], identb[:, :], [vsb, identb], [pt])
                yield
                cp(tokb[0:n, :, :], pt[0:n, 0:384].rearrange("p (q c) -> p q c", c=128), [pt], [tokb], eng="act")
                free_b.append(pt)
                yield
                msk4 = cst[0:n, COFF[mk4]:COFF[mk4] + 512].rearrange("p (q c) -> p q c", c=128)[:, :, 0:n]
                for h in range(2):
                    hs = slice(64 * h, 64 * h + 64)
                    yield from wait_f()
                    pn = free_f.pop(0)
                    mm(pn[0:n, 0:n], AR[hs, 0, tile], KRt[hs, 0, tile], True, True, [KRt, AR], [pn])
                    yield
                    tt("dve", MM[0][0:n, 2 * h + 1, 0:n], pn[0:n, 0:n], CS(msl, n, 0, n), ALU.mult, [pn, cst], [MM[0]])
                    free_f.append(pn)
                    yield from wait_f()
                    pb_ = free_f.pop(0)
                    for q_ in range(2):
                        mm(pb_[0:n, q_ * 128:q_ * 128 + n], KRt[hs, 0, tile], AR[hs, q_, tile], True, True, [KRt, AR], [pb_])
                        mm(pb_[0:n, 256 + q_ * 128:256 + q_ * 128 + n], KRt[hs, 1, tile], AR[hs, q_, tile], True, True, [KRt, AR], [pb_])
                    yield
                    o4 = pb_[0:n, 0:512].rearrange("p (q c) -> p q c", c=128)[:, :, 0:n]
                    tt("dve", A4[h][0:n, :, 0:n], o4, msk4, ALU.mult, [pb_, cst], [A4[h]])
                    free_f.append(pb_)
                    yield
                    tt("pool", Xb[0:n, h, 0:n], A4[h][0:n, 0, 0:n], identb[0:n, 0:n], ALU.add, [A4[h], identb], [Xb])
                for lev in range(1, lv):
                    cur = MM[(lev - 1) % 2]
                    nxt = MM[lev % 2]
                    last = (lev == lv - 1)
                    yield from wait_f()
                    pb_ = free_f.pop(0)
                    for h in range(2):
                        cM = A4[h][0:n, 0, 0:n] if lev == 1 else cur[0:n, 2 * h, 0:n]
                        cMT = cur[0:n, 2 * h + 1, 0:n]
                        rd = [cur, A4[h]]
                        if not last:
                            mm(pb_[0:n, (2 * h) * 128:(2 * h) * 128 + n], cMT, cM, True, True, rd, [pb_])
                        mm(pb_[0:n, (2 * h + 1) * 128:(2 * h + 1) * 128 + n], cM, cMT, True, True, rd, [pb_])
                    yield
                    if last:
                        for h in range(2):
                            cp(nxt[0:n, 2 * h + 1, 0:n], pb_[0:n, (2 * h + 1) * 128:(2 * h + 1) * 128 + n], [pb_], [nxt], eng=("act" if h == 0 else "dve"))
                    else:
                        cp(nxt[0:n, :, 0:n], pb_[0:n, 0:512].rearrange("p (q c) -> p q c", c=128)[:, :, 0:n], [pb_], [nxt], eng=("act" if lev % 2 == 0 else "dve"))
                    free_f.append(pb_)
                    yield
                    yield from wait_f()
                    px = free_f.pop(0)
                    for h in range(2):
                        mm(px[0:n, h * 128:h * 128 + n], identb[0:n, 0:n], Xb[0:n, h, 0:n], True, False, [identb, Xb], [px])
                        mm(px[0:n, h * 128:h * 128 + n], nxt[0:n, 2 * h + 1, 0:n], Xb[0:n, h, 0:n], False, True, [nxt, Xb], [px])
                    yield
                    cp(Xb[0:n, :, 0:n], px[0:n, 0:256].rearrange("p (q c) -> p q c", c=128)[:, :, 0:n], [px], [Xb], eng=("dve" if lev % 2 == 0 else "act"))
                    free_f.append(px)
                    yield

            def rwkv_post(S_, po, hp, tile, n, slot):
                bst, mv, onb_, of_ = S_.bst, S_.mv, S_.onb, S_.of32
                for h in range(2):
                    hc = slice(64 * h, 64 * h + 64)
                    op("dve", lambda e, h=h, hc=hc: e.bn_stats(out=bst[0:n, h, :], in_=po[0:n, hc]), [po], [bst])
                    op("dve", lambda e, h=h: e.bn_aggr(out=mv[0:n, h, :], in_=bst[0:n, h, :]), [bst], [mv])
                yield
                if _os.environ.get('KB_LN', '1') == '1':
                    act(mv[0:n, :, 1], mv[0:n, :, 1], AF.Ln, [mv], [mv], bias=eps_gn[0:n, 0:1])
                    act(mv[0:n, :, 1], mv[0:n, :, 1], AF.Exp, [mv], [mv], scale=-0.5)
                else:
                    ts("dve", mv[0:n, :, 1], mv[0:n, :, 1], 64e-5, None, ALU.add, None, [mv], [mv])
                    op("act", lambda e: e.sqrt(out=mv[0:n, :, 1], in_=mv[0:n, :, 1]), [mv], [mv])
                    op("dve", lambda e: e.reciprocal(out=mv[0:n, :, 1], in_=mv[0:n, :, 1]), [mv], [mv])
                yield
                for h in range(2):
                    hc = slice(64 * h, 64 * h + 64)
                    ts("dve", onb_[0:n, hc], po[0:n, hc], mv[0:n, h, 0:1], mv[0:n, h, 1:2], ALU.subtract, ALU.mult, [po, mv], [onb_])
                free_f.append(po)
                yield
                yield from wait_b()
                pt = free_b.pop(0)
                tr(pt[:, 0:n], onb_[0:n, 0:128], identb[0:n, 0:n], [onb_, identb], [pt])
                yield
                op("act", lambda e: e.activation(out=of_[:, 0:n], in_=pt[:, 0:n], func=AF.Identity, bias=prm[:, 6, hp:hp + 1], scale=prm[:, 5, hp:hp + 1]), [pt, prm], [of_])
                free_b.append(pt)
                yield
                tt("pool", of_[:, 0:n], of_[:, 0:n], bonus[:, tile], ALU.add, [of_, bonus], [of_])
                tt("pool", of_[:, 0:n], of_[:, 0:n], gT[:, tile], ALU.mult, [of_, gT], [of_])
                write_oT(hp, slot, of_[:, 0:n], n, [of_], None, eng="pool")
                yield


            def rwkv_task(hp, c0, n, pidx, slot, S_, Sst, Sbb, ti, b):
                tile = slice(c0, c0 + n)
                lv = int(np.ceil(np.log2(n)))
                tokb, A4, MM, Xb, Wb, UTb, tmpS, bst, mv, onb_, of_ = S_.tokb, S_.A4, S_.MM, S_.Xb, S_.Wb, S_.UTb, S_.tmpS, S_.bst, S_.mv, S_.onb, S_.of32
                if b is not None:
                    Sn, T2, So = S_.Sn, S_.T2, S_.So
                    dma("sp", Sn[:, :, :], srw_d[b, 2 * hp:2 * hp + 2, :, :].rearrange("h i j -> i h j"), writes=[Sn])
                    yield from wait_f()
                    pb_ = free_f.pop(0)
                    tr(pb_[:, 0:64], Sn[:, :, :].rearrange("p h j -> p (h j)"), identf[0:64, 0:64], [Sn, identf], [pb_])
                    op("pool", lambda e: e.memset(Sst[:, :], 0.0), [], [Sst])
                    yield
                    cp(Sst[0:64, 0:64], pb_[0:64, 0:64], [pb_], [Sst], eng="dve")
                    cp(Sst[64:128, 64:128], pb_[64:128, 0:64], [pb_], [Sst], eng="act")
                    free_f.append(pb_)
                    yield
                    cp(Sbb[:, :], Sst[:, :], [Sst], [Sbb], eng="pool")
                yield from rwkv_pre(S_, tile, n, lv, "m4", "sl")
                if b is None:
                    while pstate["next"] != ti:
                        yield
                yield from wait_f()
                pw = free_f.pop(0)
                mm(pw[0:n, 0:128], AR[:, 0, tile], Sbb[:, :], True, False, [AR, Sbb], [pw])
                for h in range(2):
                    hc = slice(64 * h, 64 * h + 64)
                    mm(pw[0:n, hc], A4[h][0:n, 2, 0:n], tokb[0:n, 2, hc], False, h == 1, [A4[h], tokb], [pw])
                yield
                cp(Wb[0:n, :], pw[0:n, 0:128], [pw], [Wb], eng="act")
                free_f.append(pw)
                yield
                yield from wait_f()
                pu = free_f.pop(0)
                for h in range(2):
                    hc = slice(64 * h, 64 * h + 64)
                    mm(pu[0:n, hc], Xb[0:n, h, 0:n], Wb[0:n, hc], True, True, [Xb, Wb], [pu])
                yield
                cp(UTb[0:n, :], pu[0:n, 0:128], [pu], [UTb], eng="dve")
                free_f.append(pu)
                yield
                yield from wait_f()
                psc = free_f.pop(0)
                mm(psc[:, 0:128], tokb[0:n, 0, :], UTb[0:n, :], True, False, [tokb, UTb], [psc])
                mm(psc[:, 0:128], tokb[0:n, 1, :], tokb[0:n, 2, :], False, True, [tokb], [psc])
                po = None
                if slot != "s":
                    yield from wait_f()
                    po = free_f.pop(0)
                    mm(po[0:n, 0:128], AR[:, 1, tile], Sbb[:, :], True, False, [AR, Sbb], [po])
                    for h in range(2):
                        hc = slice(64 * h, 64 * h + 64)
                        mm(po[0:n, hc], A4[h][0:n, 1, 0:n], UTb[0:n, hc], False, False, [A4[h], UTb], [po])
                        mm(po[0:n, hc], A4[h][0:n, 3, 0:n], tokb[0:n, 2, hc], False, h == 1, [A4[h], tokb], [po])
                yield
                stt(tmpS[:, :], psc[:, 0:128], pcl[:, pidx:pidx + 1], CS("bd", 128), ALU.mult, ALU.mult, [psc, pcl, cst], [tmpS])
                free_f.append(psc)
                stt(Sst[:, :], Sst[:, :], pcl[:, pidx:pidx + 1], tmpS[:, :], ALU.mult, ALU.add, [Sst, pcl, tmpS], [Sst])
                yield
                cp(Sbb[:, :], Sst[:, :], [Sst], [Sbb], eng="act")
                if b is None:
                    pstate["next"] = ti + 1
                yield
                if b is not None or ti == 16:
                    Sn, T2, So = S_.Sn, S_.T2, S_.So
                    cp(T2[0:64, :], Sst[0:64, 0:64], [Sst], [T2], eng="pool")
                    cp(T2[64:128, :], Sst[64:128, 64:128], [Sst], [T2], eng="pool")
                    yield from wait_f()
                    pb_ = free_f.pop(0)
                    tr(pb_[0:64, 0:128], T2[:, 0:64], identf[:, :], [T2, identf], [pb_])
                    yield
                    cp(So[:, :, :], pb_[0:64, 0:128].rearrange("p (h j) -> p h j", j=64), [pb_], [So], eng="act")
                    free_f.append(pb_)
                    row = 0 if b is None else 1 + b
                    dma("sp", rwo_d[row, 2 * hp:2 * hp + 2, :, :].rearrange("h i j -> i h j"), So[:, :, :], reads=[So])
                    yield
                if po is None:
                    return
                yield from rwkv_post(S_, po, hp, tile, n, slot)

            def rwkv_sgroup_task(hp, g_i, S_):
                ng, n, lv = 8, 32, 2
                c0 = TP + 32 * g_i
                tile = slice(c0, c0 + n)
                b0 = 8 * g_i
                tokb, A4, MM, Xb, Wb, UTb = S_.tokb, S_.A4, S_.MM, S_.Xb, S_.Wb, S_.UTb
                SnA, SstA, SbbA, am, tokm, T2A, tmp4 = SG.SnA, SG.SstA, SG.SbbA, SG.am, SG.tokm, SG.T2A, SG.tmp4
                for h in range(2):
                    dma("sp", SnA[:, :, h * 64:(h + 1) * 64], srw_d[b0:b0 + ng, 2 * hp + h, :, :].rearrange("b i j -> i b j"), writes=[SnA])
                op("pool", lambda e: e.memset(SstA[:, :, :], 0.0), [], [SstA])
                yield from wait_f()
                pk = free_f.pop(0)
                for bb in range(ng):
                    tr(pk[:, bb * 64:(bb + 1) * 64], SnA[:, bb, :], identf[0:64, 0:64], [SnA, identf], [pk])
                yield
                cp(SstA[0:64, :, 0:64], pk[0:64, 0:512].rearrange("p (b i) -> p b i", i=64), [pk], [SstA], eng="dve")
                cp(SstA[64:128, :, 64:128], pk[64:128, 0:512].rearrange("p (b i) -> p b i", i=64), [pk], [SstA], eng="act")
                free_f.append(pk)
                yield
                cp(SbbA[:, :, :], SstA[:, :, :], [SstA], [SbbA], eng="pool")
                yield
                yield from rwkv_pre(S_, tile, n, lv, "m4s", "slb")
                bselv = cst[:, COFF["bsel"]:COFF["bsel"] + 1024].rearrange("p (b c) -> p b c", c=64)[:, 0:ng, 0:n]
                for w_ in range(2):
                    tt("pool" if w_ else "dve", am[:, w_ * ng:(w_ + 1) * ng, :], AR[:, w_, tile].unsqueeze(1).broadcast_to([128, ng, n]), bselv, ALU.mult, [AR, cst], [am])
                rselv = cst[0:n, COFF["rsel"]:COFF["rsel"] + ng].unsqueeze(2).broadcast_to([n, ng, 256])
                tt("dve", tokm[0:n, :, :], tokb[0:n, 0:2, :].rearrange("p q c -> p (q c)").unsqueeze(1).broadcast_to([n, ng, 256]), rselv, ALU.mult, [tokb, cst], [tokm])
                yield
                yield from wait_f()
                pw = free_f.pop(0)
                for bb in range(ng):
                    mm(pw[0:n, 0:128], am[:, bb, :], SbbA[:, bb, :], bb == 0, False, [am, SbbA], [pw])
                for h in range(2):
                    hc = slice(64 * h, 64 * h + 64)
                    mm(pw[0:n, hc], A4[h][0:n, 2, 0:n], tokb[0:n, 2, hc], False, h == 1, [A4[h], tokb], [pw])
                yield
                cp(Wb[0:n, :], pw[0:n, 0:128], [pw], [Wb], eng="act")
                free_f.append(pw)
                yield
                yield from wait_f()
                pu = free_f.pop(0)
                for h in range(2):
                    hc = slice(64 * h, 64 * h + 64)
                    mm(pu[0:n, hc], Xb[0:n, h, 0:n], Wb[0:n, hc], True, True, [Xb, Wb], [pu])
                yield
                cp(UTb[0:n, :], pu[0:n, 0:128], [pu], [UTb], eng="dve")
                free_f.append(pu)
                yield
                yield from wait_f()
                po = free_f.pop(0)
                for bb in range(ng):
                    mm(po[0:n, 0:128], am[:, ng + bb, :], SbbA[:, bb, :], bb == 0, False, [am, SbbA], [po])
                for h in range(2):
                    hc = slice(64 * h, 64 * h + 64)
                    mm(po[0:n, hc], A4[h][0:n, 1, 0:n], UTb[0:n, hc], False, False, [A4[h], UTb], [po])
                    mm(po[0:n, hc], A4[h][0:n, 3, 0:n], tokb[0:n, 2, hc], False, h == 1, [A4[h], tokb], [po])
                yield
                for q_ in range(2):
                    yield from wait_f()
                    psc = free_f.pop(0)
                    for b4 in range(4):
                        bb = 4 * q_ + b4
                        mm(psc[:, b4 * 128:(b4 + 1) * 128], tokm[0:n, bb, 0:128], UTb[0:n, :], True, False, [tokm, UTb], [psc])
                        mm(psc[:, b4 * 128:(b4 + 1) * 128], tokm[0:n, bb, 128:256], tokb[0:n, 2, :], False, True, [tokm, tokb], [psc])
                    yield
                    pcb = pcl[:, 17 + b0 + 4 * q_:17 + b0 + 4 * q_ + 4].unsqueeze(2).broadcast_to([128, 4, 128])
                    bdb = CS("bd", 128).unsqueeze(1).broadcast_to([128, 4, 128])
                    tt("dve", tmp4[:, :, :], psc[:, 0:512].rearrange("p (b c) -> p b c", c=128), pcb, ALU.mult, [psc, pcl], [tmp4])
                    free_f.append(psc)
                    tt("pool", tmp4[:, :, :], tmp4[:, :, :], bdb, ALU.mult, [tmp4, cst], [tmp4])
                    tt("dve", SstA[:, 4 * q_:4 * q_ + 4, :], SstA[:, 4 * q_:4 * q_ + 4, :], pcb, ALU.mult, [SstA, pcl], [SstA])
                    yield
                    tt("pool", SstA[:, 4 * q_:4 * q_ + 4, :], SstA[:, 4 * q_:4 * q_ + 4, :], tmp4[:, :, :], ALU.add, [SstA, tmp4], [SstA])
                    yield
                cp(T2A[0:64, :, :], SstA[0:64, :, 0:64], [SstA], [T2A], eng="dve")
                cp(T2A[64:128, :, :], SstA[64:128, :, 64:128], [SstA], [T2A], eng="pool")
                yield
                for q_ in range(2):
                    yield from wait_f()
                    pk = free_f.pop(0)
                    for b4 in range(4):
                        tr(pk[0:64, b4 * 128:(b4 + 1) * 128], T2A[:, 4 * q_ + b4, :], identf[:, :], [T2A, identf], [pk])
                    yield
                    cp(SnA[:, 4 * q_:4 * q_ + 4, :], pk[0:64, 0:512].rearrange("p (b c) -> p b c", c=128), [pk], [SnA], eng="act")
                    free_f.append(pk)
                    yield
                for h in range(2):
                    dma("sp", rwo_d[1 + b0:1 + b0 + ng, 2 * hp + h, :, :].rearrange("b i j -> i b j"), SnA[:, :, h * 64:(h + 1) * 64], reads=[SnA])
                yield
                yield from rwkv_post(S_, po, hp, tile, n, ("own", 1024 + 32 * g_i))

            def run_tasks(makers, sets):
                active = []
                makers = list(makers)
                free_sets = list(sets)
                while makers or active:
                    while makers and free_sets:
                        S_ = free_sets.pop(0)
                        active.append((makers.pop(0)(S_), S_))
                    for item in list(active):
                        try:
                            next(item[0])
                        except StopIteration:
                            active.remove(item)
                            free_sets.append(item[1])

            for hp in range(8):
                load_only(hp, rs)
                load_only(8 + hp, ks)
                load_only(16 + hp, vs)
                shift_only(hp, rs)
                shift_only(8 + hp, ks)
                shift_only(16 + hp, vs)
                for (b0, bn) in BLKS:
                    bl = slice(b0, b0 + bn)
                    pb_ = next_bank()
                    mm(pb_[:, 0:bn], wa2[0:64, hp * 128:(hp + 1) * 128], la[0:64, bl], True, True, [wa2, la], [pb_])
                    act(sg[:, bl], pb_[:, 0:bn], AF.Sigmoid, [pb_, prm], [sg], bias=prm[:, 0, hp:hp + 1])
                    pb_ = next_bank()
                    mm(pb_[:, 0:bn], wa2[64:128, hp * 128:(hp + 1) * 128], la[64:128, bl], True, True, [wa2, la], [pb_])
                    act(av[:, bl], pb_[:, 0:bn], AF.Sigmoid, [pb_, prm], [av], bias=prm[:, 1, hp:hp + 1])
                    pb_ = next_bank()
                    mm(pb_[:, 0:bn], g2a[:, hp * 128:(hp + 1) * 128], sgd[:, bl], True, False, [g2a, sgd], [pb_])
                    mm(pb_[:, 0:bn], g2b[0:32, hp * 128:(hp + 1) * 128], sgd2[0:32, bl], False, True, [g2b, sgd2], [pb_])
                    cp(gT[:, bl], pb_[:, 0:bn], [pb_], [gT])
                ts("dve", t1[:, :], ks[:, :], prm[:, 2, hp:hp + 1], None, ALU.mult, None, [ks, prm], [t1])
                op("act", lambda e: e.square(out=t2[:, :], in_=t1[:, :]), [t1], [t2])
                for (b0, bn) in BLKS:
                    bl = slice(b0, b0 + bn)
                    pb_ = next_bank()
                    mm(pb_[:, 0:bn], CS("bd", 128), t2[:, bl], True, True, [cst, t2], [pb_])
                    op("act", lambda e, pb_=pb_, bl=bl, bn=bn: e.sqrt(out=kkn[:, bl], in_=pb_[:, 0:bn]), [pb_], [kkn])
                ts("dve", kkn[:, :], kkn[:, :], 1e-12, None, ALU.max, None, [kkn], [kkn])
                op("dve", lambda e: e.reciprocal(out=kkn[:, :], in_=kkn[:, :]), [kkn], [kkn])
                tt("pool", kkn[:, :], kkn[:, :], t1[:, :], ALU.mult, [kkn, t1], [kkn])
                tt("pool", t2[:, :], kkn[:, :], av[:, :], ALU.mult, [kkn, av], [t2])
                ts("dve", t1[:, :], av[:, :], prm[:, 3, hp:hp + 1], omka[:, hp:hp + 1], ALU.mult, ALU.add, [av, prm, omka], [t1])
                tt("dve", ks[:, :], ks[:, :], t1[:, :], ALU.mult, [ks, t1], [ks])
                stt(t1[:, :], rs[:, :], prm[:, 4, hp:hp + 1], ks[:, :], ALU.mult, ALU.mult, [rs, prm, ks], [t1])
                for (b0, bn) in BLKS:
                    bl = slice(b0, b0 + bn)
                    pb_ = next_bank()
                    mm(pb_[:, 0:bn], CS("bd", 128), t1[:, bl], True, True, [cst, t1], [pb_])
                    tt("dve", bonus[:, bl], pb_[:, 0:bn], vs[:, bl], ALU.mult, [pb_, vs], [bonus])
                for ti in range(17):
                    c0, n = TILES[ti]
                    op("dve", lambda e, c0=c0, n=n: e.tensor_tensor_scan(out=Cb[:, c0:c0 + n], data0=ones[:, 0:n], data1=sg[:, c0:c0 + n], initial=0.0, op0=ALU.mult, op1=ALU.add), [ones, sg], [Cb])
                cp(Cb[:, TP:T:4], sg[:, TP:T:4], [sg], [Cb], eng="dve")
                for t_ in range(1, 4):
                    tt("dve", Cb[:, TP + t_:T:4], Cb[:, TP + t_ - 1:T:4], sg[:, TP + t_:T:4], ALU.add, [Cb, sg], [Cb])
                act(t1[:, :], Cb[:, :], AF.Exp, [Cb], [t1], scale=-C0)
                for ti in range(17):
                    c0, n = TILES[ti]
                    cp(pcl[:, ti:ti + 1], t1[:, c0 + n - 1:c0 + n], [t1], [pcl], eng="dve")
                cp(pcl[:, 17:33], t1[:, TP + 3:T:4], [t1], [pcl], eng="dve")
                tt("pool", AR[:, 1, :], rs[:, :], t1[:, :], ALU.mult, [rs, t1], [AR])
                tt("dve", sg[:, :], Cb[:, :], sg[:, :], ALU.subtract, [Cb, sg], [sg])
                act(sg[:, :], sg[:, :], AF.Exp, [sg], [sg], scale=-C0)
                stt(AR[:, 0, :], kkn[:, :], -1.0, sg[:, :], ALU.mult, ALU.mult, [kkn, sg], [AR])
                act(rs[:, :], Cb[:, :], AF.Exp, [Cb, AR], [rs], scale=C0)
                tt("dve", KRt[:, 0, :], t2[:, :], rs[:, :], ALU.mult, [t2, rs], [KRt])
                tt("pool", KRt[:, 1, :], ks[:, :], rs[:, :], ALU.mult, [ks, rs], [KRt])
                cp(vsb[:, :], vs[:, :], [vs], [vsb], eng="act")
                op("dve", lambda e: e.memset(PSst[:, :], 0.0), [], [PSst])
                op("dve", lambda e: e.memset(PSbb[:, :], 0.0), [], [PSbb])
                P.barrier()
                free_f[:] = list(pbank[0:6])
                free_b[:] = list(pbf)
                pstate["next"] = 0
                gens = []
                order = []
                for i in range(17):
                    order.append(("p", i))
                    if i in (1, 3):
                        order.append(("s", i // 2))
                order = order[:int(_os.environ.get('KB_NT', '99'))]
                for k_, (kind, i) in enumerate(order):
                    if kind == "p":
                        c0, n = TILES[i]
                        gens.append(lambda S_, c0=c0, n=n, i=i: rwkv_task(hp, c0, n, i, slot_of(i), S_, PSst, PSbb, i, None))
                    else:
                        gens.append(lambda S_, i=i: rwkv_sgroup_task(hp, i, S_))
                run_tasks(gens, SETS)
                chk(3)
                P.barrier()
                prr[0] = 0
                pfr[0] = 0
            chk(7)
            qe, qo, ke, ko = rs, ks, vs, sg
            cosT, sinT = av, gT
            QR = AR
            dma("sp", cosT[:, :], cos_d[:, :], writes=[cosT])
            dma("sp", sinT[:, :], sin_d[:, :], writes=[sinT])
            Sr = [SBs(s2, "Sr%d" % i, [128, 256]) for i in range(2)]
            Srb = [SBs(s2, "Srb%d" % i, [128, 256], BF16) for i in range(2)]
            QS = SBs(s2, "QS", [128, 2, 64])
            qsm = TL(Cb[:, 0:2048].rearrange("p (d b c) -> p d b c", d=2, b=16))
            qsm.b = Cb.b
            ktm = SBs(s2, "ktm", [64, 16, 256], BF16)
            Sin = [TL(kkn[:, i * 512:(i + 1) * 512].rearrange("p (d e) -> p d e", d=2)) for i in range(3)]
            op("dve", lambda e: e.memset(kkn[:, 0:1], 0.0), [], [kkn])
            for s_i in Sin:
                s_i.b.w = kkn.b.w
            rjunk = SBs(s2, "rjunk", [128, 256])

            def make_rset(get):
                R_ = SetNS()
                R_.gt = get([128, 256], F32)
                R_.of32 = get([128, 128], F32)
                R_.rstat = get([128, 4], F32)
                R_.vtb = get([128, 256], BF16)
                R_.ktk = get([128, 2, 128], BF16)
                R_.scb = get([128, 128], BF16)
                R_.qsb = get([128, 2, 128], BF16)
                R_.onb = get([128, 256], BF16)
                return R_
            rcv = Carve([za, zp])
            RSETS = [make_rset(rcv.get) for _ in range(4)]
            rcnt = [0]

            def real_get(shape, dt):
                rcnt[0] += 1
                return SBs(s2, "rs5_%d" % rcnt[0], shape, dt)
            RSETS.append(make_rset(real_get))
            rstate = {"next": 0}

            def ret_opost(po, n, h, slot, R_):
                g_, rstat, onb_, of_ = R_.gt, R_.rstat, R_.onb, R_.of32
                op("act", lambda e: e.activation(out=rjunk[0:n, 0:256], in_=po[0:n, 0:256], func=AF.Square, accum_out=rstat[0:n, 0:1]), [po], [rjunk, rstat])
                yield
                ts("dve", rstat[0:n, 1:2], rstat[0:n, 0:1], 1.0 / 256.0, 1e-6, ALU.mult, ALU.add, [rstat], [rstat])
                yield
                act(rstat[0:n, 1:2], rstat[0:n, 1:2], AF.Ln, [rstat], [rstat])
                act(rstat[0:n, 1:2], rstat[0:n, 1:2], AF.Exp, [rstat], [rstat], scale=-0.5)
                act(g_[0:n, :], g_[0:n, :], AF.Silu, [g_], [g_])
                yield
                stt(onb_[0:n, :], po[0:n, 0:256], rstat[0:n, 1:2], g_[0:n, :], ALU.mult, ALU.mult, [po, rstat, g_], [onb_])
                yield

            def ret_opost2(n, h, slot, R_):
                onb_, of_ = R_.onb, R_.of32
                yield from wait_b()
                pt = free_b.pop(0)
                for j in range(2):
                    tr(pt[:, j * 128:j * 128 + n], onb_[0:n, j * 128:(j + 1) * 128], identb[0:n, 0:n], [onb_, identb], [pt])
                yield
                for j in range(2):
                    cp(of_[:, 0:n], pt[:, j * 128:j * 128 + n], [pt], [of_], eng="act")
                    yield
                    write_oT(8 + 2 * h + j, slot, of_[:, 0:n], n, [of_], None, eng="dve")
                    yield
                free_b.append(pt)

            def ret_task(h, ti, R_):
                g = GAM[h]
                c0, n = TILES[ti]
                tile = slice(c0, c0 + n)
                g_, vtb, ktk, scb, qsb = R_.gt, R_.vtb, R_.ktk, R_.scb, R_.qsb
                dma("pool", vtb[0:n, :], ztk_d[c0:c0 + n, 2 * h * 128:(2 * h + 2) * 128], writes=[vtb])
                dma("sp", g_[0:n, :], ztk_d[c0:c0 + n, 1024 + 2 * h * 128:1024 + (2 * h + 2) * 128], writes=[g_])
                var = 0 if ti == 0 else 1
                yield from wait_b()
                pt = free_b.pop(0)
                for dc in range(2):
                    tr(pt[0:n, dc * 128:(dc + 1) * 128], KRt[:, dc, tile], identb[:, :], [KRt, identb], [pt])
                yield
                ts("dve", ktk[0:n, :, :], pt[0:n, 0:256].rearrange("p (q c) -> p q c", c=128), cst[0:n, COFF["gk"] + var * 4 + h:COFF["gk"] + var * 4 + h + 1], None, ALU.mult, None, [pt, cst], [ktk])
                free_b.append(pt)
                yield
                yield from wait_f()
                ps_ = free_f.pop(0)
                mm(ps_[0:n, 0:n], KRt[:, 0, tile], QR[:, 0, tile], True, False, [KRt, QR], [ps_])
                mm(ps_[0:n, 0:n], KRt[:, 1, tile], QR[:, 1, tile], False, True, [KRt, QR], [ps_])
                yield
                dmv = cst[0:n, COFF["dm"]:COFF["dm"] + 512].rearrange("p (h i) -> p h i", i=128)[:, h, 0:n]
                tt("dve", scb[0:n, 0:n], ps_[0:n, 0:n], dmv, ALU.mult, [ps_, cst], [scb])
                free_f.append(ps_)
                gqv = cst[:, COFF["gq"]:COFF["gq"] + 512].rearrange("p (h i) -> p h i", i=128)[:, h, 0:n]
                tt("dve", qsb[:, 0, 0:n], QR[:, 0, tile], gqv, ALU.mult, [QR, cst], [qsb])
                tt("pool", qsb[:, 1, 0:n], QR[:, 1, tile], gqv, ALU.mult, [QR, cst], [qsb])
                yield
                while rstate["next"] != ti:
                    yield
                po = None
                if ti > 0:
                    yield from wait_f()
                    po = free_f.pop(0)
                    mm(po[0:n, 0:256], scb[0:n, 0:n], vtb[0:n, :], True, False, [scb, vtb], [po])
                    mm(po[0:n, 0:256], qsb[:, 0, 0:n], Srb[0][:, :], False, False, [qsb, Srb[0]], [po])
                    mm(po[0:n, 0:256], qsb[:, 1, 0:n], Srb[1][:, :], False, True, [qsb, Srb[1]], [po])
                yield from wait_f()
                pS = free_f.pop(0)
                for dc in range(2):
                    mm(pS[:, dc * 256:(dc + 1) * 256], ktk[0:n, dc, :], vtb[0:n, :], True, True, [ktk, vtb], [pS])
                yield
                stt(Sr[0][:, :], Sr[0][:, :], float(g ** n), pS[:, 0:256], ALU.mult, ALU.add, [Sr[0], pS], [Sr[0]])
                stt(Sr[1][:, :], Sr[1][:, :], float(g ** n), pS[:, 256:512], ALU.mult, ALU.add, [Sr[1], pS], [Sr[1]])
                free_f.append(pS)
                cp(Srb[0][:, :], Sr[0][:, :], [Sr[0]], [Srb[0]], eng="act")
                cp(Srb[1][:, :], Sr[1][:, :], [Sr[1]], [Srb[1]], eng="act")
                rstate["next"] = ti + 1
                yield
                if ti == 16:
                    for dc in range(2):
                        dma("sp", rto_d[0, h, :, :].rearrange("(m two) e -> m two e", two=2)[:, dc, :], Sr[dc][:, :], reads=[Sr[dc]])
                if po is not None:
                    yield from ret_opost(po, n, h, slot_of(ti), R_)
                    free_f.append(po)
                    yield from ret_opost2(n, h, slot_of(ti), R_)

            def ret_sample_task(h, R_):
                g = GAM[h]
                ti = 17
                c0, n = TILES[ti]
                tile = slice(c0, c0 + n)
                g_, vtb, ktk, scb = R_.gt, R_.vtb, R_.ktk, R_.scb
                dma("pool", vtb[0:n, :], ztk_d[c0:c0 + n, 2 * h * 128:(2 * h + 2) * 128], writes=[vtb])
                dma("sp", g_[0:n, :], ztk_d[c0:c0 + n, 1024 + 2 * h * 128:1024 + (2 * h + 2) * 128], writes=[g_])
                yield from wait_b()
                pt = free_b.pop(0)
                for dc in range(2):
                    tr(pt[0:n, dc * 128:(dc + 1) * 128], KRt[:, dc, tile], identb[:, :], [KRt, identb], [pt])
                yield
                ts("dve", ktk[0:n, :, :], pt[0:n, 0:256].rearrange("p (q c) -> p q c", c=128), cst[0:n, COFF["gk"] + 2 * 4 + h:COFF["gk"] + 2 * 4 + h + 1], None, ALU.mult, None, [pt, cst], [ktk])
                free_b.append(pt)
                yield
                yield from wait_f()
                ps_ = free_f.pop(0)
                mm(ps_[0:n, 0:n], KRt[:, 0, tile], QR[:, 0, tile], True, False, [KRt, QR], [ps_])
                mm(ps_[0:n, 0:n], KRt[:, 1, tile], QR[:, 1, tile], False, True, [KRt, QR], [ps_])
                yield
                dmv = cst[0:n, COFF["dms"]:COFF["dms"] + 256].rearrange("p (h i) -> p h i", i=64)[:, h, 0:n]
                tt("dve", scb[0:n, 0:n], ps_[0:n, 0:n], dmv, ALU.mult, [ps_, cst], [scb])
                free_f.append(ps_)
                gqsv = cst[:, COFF["gqs"]:COFF["gqs"] + 256].rearrange("p (h i) -> p h i", i=64)[:, h, :]
                bselv = cst[:, COFF["bsel"]:COFF["bsel"] + 1024].rearrange("p (b c) -> p b c", c=64)
                for dc in range(2):
                    tt("pool", QS[:, dc, :], QS[:, dc, :], gqsv, ALU.mult, [QS, cst], [QS])
                    tt("pool", qsm[:, dc, :, :], QS[:, dc, :].unsqueeze(1).broadcast_to([128, 16, 64]), bselv, ALU.mult, [QS, cst], [qsm])
                yield
                for b in range(16):
                    ts("dve", ktm[0:64, b, :], ktk[0:64, :, :].rearrange("p q c -> p (q c)"), cst[0:64, COFF["rsel"] + b:COFF["rsel"] + b + 1], None, ALU.mult, None, [ktk, cst], [ktm])
                    yield
                mm(pacc[0:64, 0:256], scb[0:64, 0:64], vtb[0:64, :], True, False, [scb, vtb], [pacc])
                for b in range(16):
                    si = Sin[b % 3]
                    dma("sp", si[:, :, :], srt_d[b, h, :, :].rearrange("(m two) e -> m two e", two=2), writes=[si])
                    yield
                    for dc in range(2):
                        mm(pacc[0:64, 0:256], qsm[:, dc, b, :], si[:, dc, :], False, (b == 15 and dc == 1), [qsm, si], [pacc])
                    for dc in range(2):
                        yield from wait_f()
                        pS = free_f.pop(0)
                        mm(pS[:, 0:256], ktm[0:64, b, dc * 128:(dc + 1) * 128], vtb[0:64, :], True, True, [ktm, vtb], [pS])
                        yield
                        stt(si[:, dc, :], si[:, dc, :], float(g ** 4), pS[:, 0:256], ALU.mult, ALU.add, [si, pS], [si])
                        free_f.append(pS)
                    dma("sp", rto_d[1 + b, h, :, :].rearrange("(m two) e -> m two e", two=2), si[:, :, :], reads=[si])
                    yield
                yield from ret_opost(pacc, 64, h, ("own", 1024), R_)
                yield from ret_opost2(64, h, ("own", 1024), R_)

            for h in range(4):
                for i_, dst in enumerate((qe, qo, ke, ko)):
                    ch = 27 + 4 * h + i_
                    dma("sp", dst[:, :], zT_d[ch * 128:(ch + 1) * 128, :], writes=[dst])
                for (xe, xo, OUT) in ((qe, qo, QR), (ke, ko, KRt)):
                    tt("dve", t1[:, :], xe[:, :], cosT[:, :], ALU.mult, [xe, cosT], [t1])
                    tt("pool", t2[:, :], xo[:, :], sinT[:, :], ALU.mult, [xo, sinT], [t2])
                    tt("dve", OUT[:, 0, :], t1[:, :], t2[:, :], ALU.subtract, [t1, t2], [OUT])
                    if OUT is QR:
                        tt("dve", QS[:, 0, :], t1[:, TP:T], t2[:, TP:T], ALU.subtract, [t1, t2], [QS])
                    tt("dve", t1[:, :], xe[:, :], sinT[:, :], ALU.mult, [xe, sinT], [t1])
                    tt("pool", t2[:, :], xo[:, :], cosT[:, :], ALU.mult, [xo, cosT], [t2])
                    tt("dve", OUT[:, 1, :], t1[:, :], t2[:, :], ALU.add, [t1, t2], [OUT])
                    if OUT is QR:
                        tt("dve", QS[:, 1, :], t1[:, TP:T], t2[:, TP:T], ALU.add, [t1, t2], [QS])
                for dc in range(2):
                    op("dve", lambda e, dc=dc: e.memset(Sr[dc][:, :], 0.0), [], [Sr[dc]])
                    op("dve", lambda e, dc=dc: e.memset(Srb[dc][:, :], 0.0), [], [Srb[dc]])
                P.barrier()
                free_f[:] = list(pbank[0:5])
                free_b[:] = list(pbf)
                rstate["next"] = 0
                makers = [(lambda R_, h=h: ret_sample_task(h, R_))]
                for ti in range(17):
                    makers.append(lambda R_, h=h, ti=ti: ret_task(h, ti, R_))
                run_tasks(makers, RSETS)
                P.barrier()
                prr[0] = 0
                pfr[0] = 0
            chk(8)
            if debug:
                dma("sp", dbg["oT"][:, :, :], oT[:], reads=[oT])
            P.flush()
        P.barrier()

        with ExitStack() as s3:
            gam = SBs(s3, "gam3", [128, D])
            env["junk"] = SBs(s3, "junk3", [128, D])
            hres = [SBs(s3, "h%d" % i, [128, D]) for i in range(9)]
            hnT = SBs(s3, "hnT", [128, KC, OWN], BF16)
            OT = [(i * 128, 128) for i in range(8)] + [(1024, 64)]
            with ExitStack() as s3a:
                wo = SBs(s3a, "wo", [128, KC, 512], BF16)
                xo_ = [SBs(s3a, "xo%d" % i, [128, 512]) for i in range(2)]
                for blk in range(4):
                    dma("pool", wo[:], w_out[blk, :, :, :], writes=[wo])
                    for ti, (c0, n) in enumerate(OT):
                        x_ = xo_[(blk * 9 + ti) % 2]
                        dma("sp", x_[0:n, :], x_own[c0:c0 + n, blk * 512:(blk + 1) * 512], writes=[x_])
                        pb_ = next_bank()
                        for kc in range(KC):
                            mm(pb_[0:n, :], oT[:, kc, c0:c0 + n], wo[:, kc, :], kc == 0, kc == KC - 1, [oT, wo], [pb_])
                        tt("dve", hres[ti][0:n, blk * 512:(blk + 1) * 512], pb_[0:n, :], x_[0:n, :], ALU.add, [pb_, x_], [hres[ti]])
                if debug:
                    for ti, (c0, n) in enumerate(OT):
                        dma("sp", dbg["h"][c0:c0 + n, :], hres[ti][0:n, :], reads=[hres[ti]])
                P.flush()
            P.barrier()
            with ExitStack() as s3b:
                hb = [SBs(s3b, "hb%d" % i, [128, D], BF16) for i in range(2)]
                dma("sp", gam[:], gam_d[1, :].partition_broadcast(128), writes=[gam])
                for ti, (c0, n) in enumerate(OT):
                    s_ = rms_rstd(hres[ti], n, ti % 2)
                    hb_ = hb[ti % 2]
                    stt(hb_[0:n, :], hres[ti][0:n, :], s_[0:n, 1:2], gam[0:n, :], ALU.mult, ALU.mult, [hres[ti], s_, gam], [hb_])
                    transpose16(hb_, n, hnT, c0)
                P.flush()
            P.barrier()
            with ExitStack() as s3c:
                G = 4
                wg = [SBs(s3c, "wg%d" % i, [128, KC, 128], BF16) for i in range(2)]
                wu = [SBs(s3c, "wu%d" % i, [128, KC, 128], BF16) for i in range(2)]
                actT = SBs(s3c, "actT", [128, G, OWN], BF16)
                sgl = SBs(s3c, "sgl", [128, 512])
                oflat = oT[:, :, :].rearrange("p k c -> p (k c)")
                wd = []
                for i in range(2 * G):
                    w_ = TL(oflat[:, i * D:(i + 1) * D])
                    w_.b.r = list(oT.b.r)
                    w_.b.w = oT.b.w
                    wd.append(w_)
                RB = [(0, 512), (512, 512), (1024, 64)]
                for grp in range(NFB // G):
                    for fi in range(G):
                        fb = grp * G + fi
                        g_ = wg[fb % 2]
                        u_ = wu[fb % 2]
                        d_ = wd[fb % (2 * G)]
                        dma("pool", g_[:], w_gate[fb, :, :, :], writes=[g_])
                        dma("pool", u_[:], w_up[fb, :, :, :], writes=[u_])
                        dma("pool", d_[:, :], w_down[fb, :, :], writes=[d_])
                        for (r0, rn) in RB:
                            pg = next_bank()
                            for kc in range(KC):
                                mm(pg[:, 0:rn], g_[:, kc, :], hnT[:, kc, r0:r0 + rn], kc == 0, kc == KC - 1, [g_, hnT], [pg])
                            pu_ = next_bank()
                            for kc in range(KC):
                                mm(pu_[:, 0:rn], u_[:, kc, :], hnT[:, kc, r0:r0 + rn], kc == 0, kc == KC - 1, [u_, hnT], [pu_])
                            act(sgl[:, 0:rn], pg[:, 0:rn], AF.Silu, [pg], [sgl])
                            tt("dve", actT[:, fi, r0:r0 + rn], sgl[:, 0:rn], pu_[:, 0:rn], ALU.mult, [sgl, pu_], [actT])
                    for ti, (c0, n) in enumerate(OT):
                        for blk in range(4):
                            pb_ = next_bank()
                            for fi in range(G):
                                d_ = wd[(grp * G + fi) % (2 * G)]
                                mm(pb_[0:n, :], actT[:, fi, c0:c0 + n], d_[:, blk * 512:(blk + 1) * 512], fi == 0, fi == G - 1, [actT, d_], [pb_])
                            tt("dve", hres[ti][0:n, blk * 512:(blk + 1) * 512], hres[ti][0:n, blk * 512:(blk + 1) * 512], pb_[0:n, :], ALU.add, [hres[ti], pb_], [hres[ti]])
                dma("sp", gam[:], gam_d[2, :].partition_broadcast(128), writes=[gam])
                for ti, (c0, n) in enumerate(OT):
                    s_ = rms_rstd(hres[ti], n, ti % 2)
                    stt(hres[ti][0:n, :], hres[ti][0:n, :], s_[0:n, 1:2], gam[0:n, :], ALU.mult, ALU.mult, [hres[ti], s_, gam], [hres[ti]])
                    dma("sp", y_d[c0:c0 + n, :], hres[ti][0:n, :], reads=[hres[ti]])
                P.finish()
                P.flush()


_NC_CACHE = {}


def _prep_weights(inp):
    f = lambda a: np.ascontiguousarray(np.asarray(a, dtype=np.float32))
    w_in = f(inp["w_in"])[0]
    cols = list(range(0, 3360)) + [-1] * 96
    for h in range(4):
        for base in (3360, 3360 + 1024):
            cols += list(range(base + h * 256, base + (h + 1) * 256, 2))
            cols += list(range(base + h * 256 + 1, base + (h + 1) * 256, 2))
    fm = list(range(0, 3360)) + [-1] * 96
    for h in range(4):
        qb, kb = 3360 + h * 256, 3360 + 1024 + h * 256
        fm += list(range(qb, qb + 256, 2)) + list(range(qb + 1, qb + 256, 2))
        fm += list(range(kb, kb + 256, 2)) + list(range(kb + 1, kb + 256, 2))
    fm += list(range(3360 + 2048, 3360 + 4096))
    fm = np.array(fm)
    wp = np.zeros((D, NCH * 128), np.float32)
    valid = fm >= 0
    wp[:, valid] = w_in[:, fm[valid]]
    w_in_r = np.ascontiguousarray(wp.reshape(KC, 128, NCH, 128).transpose(2, 1, 0, 3))
    w_out = f(inp["w_out"])[0]
    w_out_r = np.ascontiguousarray(w_out.reshape(KC, 128, 4, 512).transpose(2, 1, 0, 3))
    wg = np.ascontiguousarray(f(inp["w_gate"])[0].reshape(KC, 128, NFB, 128).transpose(2, 1, 0, 3))
    wu = np.ascontiguousarray(f(inp["w_up"])[0].reshape(KC, 128, NFB, 128).transpose(2, 1, 0, 3))
    wd = np.ascontiguousarray(f(inp["w_down"])[0].reshape(NFB, 128, D))
    gam = np.stack([f(inp["norm_mix"])[0], f(inp["norm_ffn"])[0], f(inp["norm_final"])])
    mu = np.zeros((27 * 128,), np.float32)
    mu[:3360] = f(inp["rwkv_mu"])[0]
    mu = np.ascontiguousarray(mu.reshape(27, 128).T)
    names = ["rwkv_w0", "rwkv_a0", "rwkv_kk", "rwkv_ka", "rwkv_rk", "rwkv_ln_w", "rwkv_ln_b"]
    prm = np.stack([f(inp[n_])[0].reshape(8, 128).T for n_ in names], axis=1)
    return dict(w_in=w_in_r, w_out=w_out_r, w_gate=wg, w_up=wu, w_down=wd, gam=np.ascontiguousarray(gam), mu=mu,
                prm=np.ascontiguousarray(prm), w2=f(inp["rwkv_w2"])[0], a2=f(inp["rwkv_a2"])[0], g2=f(inp["rwkv_g2"])[0],
                cst=CST, cos=COS_T, sin=SIN_T)


def kernel(**inp):
    f = lambda a: np.asarray(a, dtype=np.float32)
    shared = _prep_weights(inp)
    xp = f(inp["x_prompt"])
    xsm = f(inp["x_sample"])
    meta = f(inp["meta_tokens"])
    ssh = f(inp["state_shift"])[0]
    srw = f(inp["state_rwkv"])[0]
    srt = f(inp["state_ret"])[0]
    in_maps = []
    for c in range(8):
        b, hh = c // 2, c % 2
        xs_c = xsm[16 * c:16 * c + 16].reshape(64, D)
        xs = np.concatenate([meta, xp[b], xs_c], axis=0)
        x_own = np.concatenate([xp[b][hh * 1024:(hh + 1) * 1024], xs_c], axis=0)
        sel = np.zeros((128, 2), np.float32)
        sel[:, hh] = 1.0
        sshp = np.zeros((16, 27 * 128), np.float32)
        sshp[:, :3360] = ssh[16 * c:16 * c + 16]
        ssh_r = np.ascontiguousarray(sshp.reshape(16, 27, 128).transpose(1, 2, 0))
        m = dict(shared)
        m.update(xs=np.ascontiguousarray(xs), x_own=np.ascontiguousarray(x_own), sel=sel, ssh=ssh_r,
                 srw=np.ascontiguousarray(srw[16 * c:16 * c + 16]), srt=np.ascontiguousarray(srt[16 * c:16 * c + 16]))
        in_maps.append(m)
    if inp.get("_maps_only"):
        return in_maps
    if "nc" not in _NC_CACHE:
        _NC_CACHE["nc"] = build()
    res = run_bass_kernel_spmd(_NC_CACHE["nc"], in_maps, core_ids=list(range(8)))
    R = res.results
    y_prompt = np.zeros((4, SEQ, D), np.float32)
    y_sample = np.zeros((128, 4, D), np.float32)
    shift_p = np.zeros((1, 4, 3360), np.float32)
    rwkv_p = np.zeros((1, 4, 16, 64, 64), np.float32)
    ret_p = np.zeros((1, 4, 4, 256, 256), np.float32)
    shift_s = np.zeros((1, 128, 3360), np.float32)
    rwkv_s = np.zeros((1, 128, 16, 64, 64), np.float32)
    ret_s = np.zeros((1, 128, 4, 256, 256), np.float32)
    for c in range(8):
        b, hh = c // 2, c % 2
        r = R[c]
        y_prompt[b, hh * 1024:(hh + 1) * 1024] = r["y"][:1024]
        y_sample[16 * c:16 * c + 16] = r["y"][1024:].reshape(16, 4, D)
        shift_s[0, 16 * c:16 * c + 16] = r["sho"][1:17, :3360]
        rwkv_s[0, 16 * c:16 * c + 16] = r["rwo"][1:17]
        ret_s[0, 16 * c:16 * c + 16] = r["rto"][1:17]
        if hh == 0:
            shift_p[0, b] = r["sho"][0, :3360]
            rwkv_p[0, b] = r["rwo"][0]
            ret_p[0, b] = r["rto"][0]
    return (y_prompt, y_sample, shift_p, rwkv_p, ret_p, shift_s, rwkv_s, ret_s)
```

```python
import numpy as np
from contextlib import ExitStack
import concourse.bass as bass
import concourse.mybir as mybir
from concourse.bass_utils import run_bass_kernel_spmd

F32 = mybir.dt.float32
BF16 = mybir.dt.bfloat16
ALU = mybir.AluOpType
AF = mybir.ActivationFunctionType
AX = mybir.AxisListType

D = 2048
KC = 16
NMETA = 16
SEQ = 2048
TP = NMETA + SEQ
NS = 64
T = TP + NS
NCH = 59
NFM = 43
DFF = 5632
NFB = DFF // 128
OWN = 1024 + NS
C0 = float(np.exp(-0.5))
GAM = [1.0 - 2.0 ** (-5.0 - h) for h in range(4)]
TILES = [(0, 16)] + [(16 + 128 * i, 128) for i in range(16)] + [(TP, NS)]
BLKS = [(0, 512), (512, 512), (1024, 512), (1536, 512), (2048, T - 2048)]


class Buf:
    __slots__ = ("w", "r")

    def __init__(self):
        self.w = None
        self.r = []


class Prog:
    ENG = ("pe", "act", "dve", "pool", "sp")

    def __init__(self, nc, n_dma=48):
        self.nc = nc
        self.ops = {k: [] for k in self.ENG}
        self.cnt = {k: 0 for k in self.ENG}
        self.seen = {k: {} for k in self.ENG}
        self.n_dma = n_dma
        self.dma_use = [0] * n_dma
        self.dma_rr = 0
        self.dma_rr_pool = 0
        self.n_sp = n_dma - 12

    def _need(self, waits, eng, ev, kind):
        if ev is None:
            return
        sk, val, src = ev
        if src == eng and not isinstance(sk, tuple):
            if eng == "pe":
                return
        if waits.get(sk, 0) < val:
            waits[sk] = val

    def op(self, eng, fn, reads=(), writes=(), dma=False):
        waits = {}
        for b in reads:
            self._need(waits, eng, b.w, "raw")
        for b in writes:
            self._need(waits, eng, b.w, "waw")
            for ev in b.r:
                self._need(waits, eng, ev, "war")
        if dma:
            if eng == "pool":
                idx = self.n_sp + self.dma_rr_pool
                self.dma_rr_pool = (self.dma_rr_pool + 1) % (self.n_dma - self.n_sp)
            else:
                idx = self.dma_rr
                self.dma_rr = (self.dma_rr + 1) % self.n_sp
            sk = ("dma", idx)
            prev = 16 * self.dma_use[idx]
            if prev > 0 and waits.get(sk, 0) < prev:
                waits[sk] = prev
            self.dma_use[idx] += 1
            ev = (sk, 16 * self.dma_use[idx], eng)
            inc = (sk, 16)
        else:
            self.cnt[eng] += 1
            ev = (eng, self.cnt[eng], eng)
            inc = (eng, 1)
        seen = self.seen[eng]
        wl = []
        for sk, val in waits.items():
            if seen.get(sk, 0) < val:
                seen[sk] = val
                wl.append((sk, val))
        self.ops[eng].append((wl, fn, inc))
        for b in writes:
            b.w = ev
            b.r = []
        for b in reads:
            b.r.append(ev)
        return ev

    def barrier(self):
        allw = [(("dma", i), 16 * self.dma_use[i]) for i in range(self.n_dma) if self.dma_use[i] > 0]
        allw += [(k, self.cnt[k]) for k in self.ENG if self.cnt[k] > 0]
        for k in self.ENG:
            wl = []
            for sk, val in allw:
                if sk == k:
                    continue
                if self.seen[k].get(sk, 0) < val:
                    self.seen[k][sk] = val
                    wl.append((sk, val))
            self.ops[k].append((wl, None, None))

    def finish(self):
        waits = []
        for i in range(self.n_dma):
            if self.dma_use[i] > 0:
                waits.append((("dma", i), 16 * self.dma_use[i]))
        for k in self.ENG:
            if k != "sp" and self.cnt[k] > 0:
                waits.append((k, self.cnt[k]))
        self.ops["sp"].append((waits, None, None))

    def begin(self, st):
        nc = self.nc
        self.sems = {}
        for k in self.ENG:
            self.sems[k] = st.enter_context(nc.semaphore("s_" + k))
        for i in range(self.n_dma):
            self.sems[("dma", i)] = st.enter_context(nc.semaphore("d_%d" % i))

    def flush(self):
        nc = self.nc
        sems = self.sems
        with nc.Block() as block:

            def run(e, lst):
                for wl, fn, inc in lst:
                    for sk, val in wl:
                        e.wait_ge(sems[sk], val)
                    if fn is None:
                        continue
                    fn(e).then_inc(sems[inc[0]], inc[1])

            @block.tensor
            def _(e):
                run(e, self.ops["pe"])

            @block.scalar
            def _(e):
                run(e, self.ops["act"])

            @block.vector
            def _(e):
                run(e, self.ops["dve"])

            @block.gpsimd
            def _(e):
                run(e, self.ops["pool"])

            @block.sync
            def _(e):
                run(e, self.ops["sp"])
        self.ops = {k: [] for k in self.ENG}


class TL:
    def __init__(self, t):
        self.t = t
        self.b = Buf()

    def __getitem__(self, k):
        return self.t[k]


def host_consts():
    c = {}
    idx = np.arange(128)
    su = (idx[:, None] < idx[None, :]).astype(np.float32)
    ui = (idx[:, None] <= idx[None, :]).astype(np.float32)
    sl = (idx[:, None] > idx[None, :]).astype(np.float32)
    bd = ((idx[:, None] // 64) == (idx[None, :] // 64)).astype(np.float32)
    dm = np.zeros((128, 4, 128), np.float64)
    gq = np.zeros((128, 4, 128), np.float64)
    gk = np.zeros((128, 3, 4), np.float64)
    dms = np.zeros((128, 4, 64), np.float64)
    gqs = np.zeros((128, 4, 64), np.float64)
    for h in range(4):
        g = GAM[h]
        dd = idx[None, :] - idx[:, None]
        dm[:, h, :] = np.where(dd >= 0, g ** np.maximum(dd, 0), 0.0)
        gq[:, h, :] = g ** (idx[None, :] + 1.0)
        gk[:16, 0, h] = g ** (15.0 - idx[:16])
        gk[:, 1, h] = g ** (127.0 - idx)
        gk[:64, 2, h] = g ** (3.0 - (idx[:64] % 4))
        j = idx[:64]
        same = (j[:, None] // 4) == (j[None, :] // 4)
        dt_ = (j[None, :] % 4) - (j[:, None] % 4)
        dms[:64, h, :] = np.where(same & (dt_ >= 0), g ** np.maximum(dt_, 0), 0.0)
        gqs[:, h, :] = g ** ((j[None, :] % 4) + 1.0)
    bsel = np.zeros((128, 16, 64), np.float32)
    for b in range(16):
        bsel[:, b, 4 * b:4 * b + 4] = 1.0
    rsel = np.zeros((128, 16), np.float32)
    for r in range(64):
        rsel[r, r // 4] = 1.0
    m4 = np.concatenate([su, ui, su, ui], axis=1)
    blk = ((idx[:, None] // 4) == (idx[None, :] // 4)).astype(np.float32)
    m4s = np.concatenate([su * blk, ui * blk, su * blk, ui * blk], axis=1)
    slb = sl * blk
    parts = [su, ui, sl, bd, m4, m4s, slb, dm.reshape(128, -1), gq.reshape(128, -1), gk.reshape(128, -1),
             dms.reshape(128, -1), gqs.reshape(128, -1), bsel.reshape(128, -1), rsel]
    offs = {}
    o = 0
    names = ["su", "ui", "sl", "bd", "m4", "m4s", "slb", "dm", "gq", "gk", "dms", "gqs", "bsel", "rsel"]
    for n_, p in zip(names, parts):
        offs[n_] = o
        o += p.shape[1]
    cst = np.concatenate([p.astype(np.float32) for p in parts], axis=1)
    inv_freq = (1.0 / (10000.0 ** np.linspace(0.0, 1.0, 128, dtype=np.float32))).astype(np.float32)
    pos = np.concatenate([np.arange(TP), np.tile(16384 + np.arange(4), 16)]).astype(np.float32)
    ang = (pos[None, :] * inv_freq[:, None]).astype(np.float32)
    cs = np.cos(ang.astype(np.float64)).astype(np.float32)
    sn = np.sin(ang.astype(np.float64)).astype(np.float32)
    return cst, offs, cs, sn


CST, COFF, COS_T, SIN_T = host_consts()
NCST = CST.shape[1]


def build(debug=False, stop=None):
    nc = bass.Bass("TRN2", target_bir_lowering=False)
    DT = lambda name, shape, dt=F32, kind="ExternalInput": nc.dram_tensor(name, list(shape), dt, kind=kind).ap()
    xs = DT("xs", [T, D])
    x_own = DT("x_own", [OWN, D])
    sel_d = DT("sel", [128, 2])
    w_in = DT("w_in", [NCH, 128, KC, 128])
    w_out = DT("w_out", [4, 128, KC, 512])
    w_gate = DT("w_gate", [NFB, 128, KC, 128])
    w_up = DT("w_up", [NFB, 128, KC, 128])
    w_down = DT("w_down", [NFB, 128, D])
    gam_d = DT("gam", [3, D])
    mu_d = DT("mu", [128, 27])
    prm_d = DT("prm", [128, 7, 8])
    w2_d = DT("w2", [64, 1024])
    a2_d = DT("a2", [64, 1024])
    g2_d = DT("g2", [160, 1024])
    ssh_d = DT("ssh", [27, 128, 16])
    srw_d = DT("srw", [16, 16, 64, 64])
    srt_d = DT("srt", [16, 4, 256, 256])
    cst_d = DT("cst", [128, NCST])
    cos_d = DT("cos", [128, T])
    sin_d = DT("sin", [128, T])
    y_d = DT("y", [OWN, D], kind="ExternalOutput")
    sho_d = DT("sho", [17, 27 * 128], kind="ExternalOutput")
    rwo_d = DT("rwo", [17, 16, 64, 64], kind="ExternalOutput")
    rto_d = DT("rto", [17, 4, 256, 256], kind="ExternalOutput")
    zT_d = nc.dram_tensor("zT_scr", [NFM * 128, T], F32).ap()
    ztk_d = nc.dram_tensor("ztk_scr", [T, 16 * 128], F32).ap()
    dbg = {}
    if debug:
        dbg["oT"] = DT("dbg_oT", [128, KC, OWN], BF16, kind="ExternalOutput")
        dbg["h"] = DT("dbg_h", [OWN, D], kind="ExternalOutput")

    P = Prog(nc)
    try:
        _build_body(nc, P, debug, stop, locals())
    except _Stop:
        pass
    return nc


class _Stop(Exception):
    pass


def _build_body(nc, P, debug, stop, L):
    globals().update({k: v for k, v in L.items() if k not in ("nc", "P", "debug", "stop")})
    (xs, x_own, sel_d, w_in, w_out, w_gate, w_up, w_down, gam_d, mu_d, prm_d, w2_d, a2_d, g2_d, ssh_d, srw_d, srt_d,
     cst_d, cos_d, sin_d, y_d, sho_d, rwo_d, rto_d, zT_d, ztk_d, dbg) = [L[k] for k in (
        "xs", "x_own", "sel_d", "w_in", "w_out", "w_gate", "w_up", "w_down", "gam_d", "mu_d", "prm_d", "w2_d", "a2_d",
        "g2_d", "ssh_d", "srw_d", "srt_d", "cst_d", "cos_d", "sin_d", "y_d", "sho_d", "rwo_d", "rto_d", "zT_d", "ztk_d", "dbg")]

    def chk(k):
        if stop == k:
            P.finish()
            P.flush()
            raise _Stop()
    with ExitStack() as st:
        P.begin(st)
        def SBs(stack, name, shape, dt=F32):
            return TL(stack.enter_context(nc.sbuf_tensor("sb_" + name, list(shape), dt)))

        def SB(name, shape, dt=F32):
            return SBs(st, name, shape, dt)

        def PS(name, shape, dt=F32):
            return TL(st.enter_context(nc.psum_tensor(name, list(shape), dt)))

        def dma(eng, out, in_, reads=(), writes=()):
            P.op(eng, lambda e: e.dma_start(out=out, in_=in_), reads=[r.b for r in reads], writes=[w.b for w in writes], dma=True)

        def op(eng, fn, reads=(), writes=()):
            P.op(eng, fn, reads=[r.b for r in reads], writes=[w.b for w in writes])

        def mm(out, lhsT, rhs, start, stop, reads, writes):
            op("pe", lambda e: e.matmul(out, lhsT=lhsT, rhs=rhs, start=start, stop=stop), reads, writes)

        def tr(out, in_, ident, reads, writes):
            op("pe", lambda e: e.transpose(out=out, in_=in_, identity=ident), reads, writes)

        def tt(eng, out, a, b, o, reads, writes):
            op(eng, lambda e: e.tensor_tensor(out=out, in0=a, in1=b, op=o), reads, writes)

        def ts(eng, out, a, s1, s2, o0, o1, reads, writes):
            if s2 is None:
                op(eng, lambda e: e.tensor_scalar(out=out, in0=a, scalar1=s1, scalar2=None, op0=o0), reads, writes)
            else:
                op(eng, lambda e: e.tensor_scalar(out=out, in0=a, scalar1=s1, scalar2=s2, op0=o0, op1=o1), reads, writes)

        def stt(out, a, s, b, o0, o1, reads, writes):
            op("dve", lambda e: e.scalar_tensor_tensor(out=out, in0=a, scalar=s, in1=b, op0=o0, op1=o1), reads, writes)

        def act(out, in_, func, reads, writes, bias=0.0, scale=1.0):
            op("act", lambda e: e.activation(out=out, in_=in_, func=func, bias=bias, scale=scale), reads, writes)

        cp_i = [0]

        def cp(out, in_, reads, writes, eng=None):
            if eng is None:
                eng = "act" if cp_i[0] % 2 == 0 else "dve"
                cp_i[0] += 1
            if eng == "act":
                op("act", lambda e: e.copy(out=out, in_=in_), reads, writes)
            else:
                op(eng, lambda e: e.tensor_copy(out=out, in_=in_), reads, writes)

        pbank = [PS("pb%d" % i, [128, 512]) for i in range(6)]
        pbf = [PS("pbf%d" % i, [128, 1024], BF16) for i in range(2)]
        prr = [0]
        pfr = [0]

        def next_bank():
            b_ = pbank[prr[0] % 5]
            prr[0] += 1
            return b_

        def next_bf():
            b_ = pbf[pfr[0] % 2]
            pfr[0] += 1
            return b_
        pacc = pbank[5]

        cst = SB("cst", [128, NCST])
        dma("sp", cst[:], cst_d[:, :], writes=[cst])

        def CS(n_, w, p0=0, p1=128):
            return cst[p0:p1, COFF[n_]:COFF[n_] + w]
        sel = SB("sel", [128, 2])
        dma("sp", sel[:], sel_d[:, :], writes=[sel])
        identf = SB("identf", [128, 128])
        identb = SB("identb", [128, 128], BF16)
        ones = SB("ones", [128, 128])
        op("pool", lambda e: e.memset(identf[:], 0.0), writes=[identf])
        op("pool", lambda e: e.affine_select(out=identf[:], in_=identf[:], pattern=[[-1, 128]], compare_op=ALU.not_equal, fill=1.0, base=0, channel_multiplier=1), reads=[identf], writes=[identf])
        op("dve", lambda e: e.tensor_copy(out=identb[:], in_=identf[:]), reads=[identf], writes=[identb])
        op("dve", lambda e: e.memset(ones[:], 1.0), writes=[ones])
        oT = SB("oT", [128, KC, OWN], BF16)
        sshT = SB("sshT", [128, 27, 16])
        dma("sp", sshT[:], ssh_d.rearrange("c p b -> p c b"), writes=[sshT])
        mu = SB("mu", [128, 27])
        dma("sp", mu[:], mu_d[:, :], writes=[mu])
        prm = SB("prm", [128, 7, 8])
        dma("sp", prm[:], prm_d[:, :, :], writes=[prm])
        omka = SB("omka", [128, 8])
        ts("dve", omka[:], prm[:, 3, :], -1.0, 1.0, ALU.mult, ALU.add, [prm], [omka])
        stat = [SB("stat%d" % i, [128, 4]) for i in range(2)]
        env = {}

        def rms_rstd(src_tile, n, sti):
            junk = env["junk"]
            s_ = stat[sti]
            op("act", lambda e: e.activation(out=junk[0:n, :], in_=src_tile[0:n, :], func=AF.Square, accum_out=s_[0:n, 0:1]), reads=[src_tile], writes=[junk, s_])
            ts("dve", s_[0:n, 1:2], s_[0:n, 0:1], 1.0 / D, 1e-6, ALU.mult, ALU.add, [s_], [s_])
            op("act", lambda e: e.sqrt(out=s_[0:n, 1:2], in_=s_[0:n, 1:2]), reads=[s_], writes=[s_])
            op("dve", lambda e: e.reciprocal(out=s_[0:n, 1:2], in_=s_[0:n, 1:2]), reads=[s_], writes=[s_])
            return s_

        def transpose16(xb_, n, dst, c0):
            for half in range(2):
                pt = next_bf()
                for k8 in range(8):
                    kc = half * 8 + k8
                    tr(pt[:, k8 * 128:k8 * 128 + n], xb_[0:n, kc * 128:(kc + 1) * 128], identb[0:n, 0:n], [xb_, identb], [pt])
                src = pt[:, :].rearrange("p (k c) -> p k c", c=128)[:, :, 0:n]
                cp(dst[:, half * 8:half * 8 + 8, c0:c0 + n], src, [pt], [dst])

        with ExitStack() as s1:
            gam = SBs(s1, "gam1", [128, D])
            env["junk"] = SBs(s1, "junk1", [128, D])
            xnT = SBs(s1, "xnT", [128, KC, T], BF16)
            xt = [SBs(s1, "xt%d" % i, [128, D]) for i in range(2)]
            xnb = [SBs(s1, "xnb%d" % i, [128, D], BF16) for i in range(2)]
            wt = [SBs(s1, "wt%d" % i, [128, KC, 128], BF16) for i in range(4)]
            zst = [SBs(s1, "zst%d" % i, [128, T]) for i in range(2)]
            ztk = [SBs(s1, "ztk%d" % i, [128, 18, 128]) for i in range(1)]
            shs = [SBs(s1, "shs%d" % i, [17, 128]) for i in range(2)]
            xl = SBs(s1, "xl", [128, KC, 17], BF16)
            dma("sp", gam[:], gam_d[0, :].partition_broadcast(128), writes=[gam])
            for ti, (c0, n) in enumerate(TILES):
                x_ = xt[ti % 2]
                xb_ = xnb[ti % 2]
                dma("sp", x_[0:n, :], xs[c0:c0 + n, :], writes=[x_])
                s_ = rms_rstd(x_, n, ti % 2)
                stt(xb_[0:n, :], x_[0:n, :], s_[0:n, 1:2], gam[0:n, :], ALU.mult, ALU.mult, [x_, s_, gam], [xb_])
                transpose16(xb_, n, xnT, c0)
            cp(xl[:, :, 1:17], xnT[:, :, TP + 3:T:4], [xnT], [xl], eng="dve")
            cp(xl[:, :, 0:1], xnT[:, :, TP - 1:TP], [xnT], [xl], eng="dve")
            chk(0)
            for ch in range(NCH):
                w_ = wt[ch % 4]
                dma("pool", w_[:], w_in[ch, :, :, :], writes=[w_])
                if ch < NFM:
                    z_ = zst[ch % 2]
                    scale = (1.0 / 16.0) if (ch >= 27 and (ch - 27) % 4 >= 2) else 1.0
                    for (b0, bn) in BLKS:
                        pb_ = next_bank()
                        for kc in range(KC):
                            mm(pb_[:, 0:bn], w_[:, kc, :], xnT[:, kc, b0:b0 + bn], kc == 0, kc == KC - 1, [w_, xnT], [pb_])
                        if cp_i[0] % 2 == 0:
                            op("act", lambda e, z_=z_, pb_=pb_, b0=b0, bn=bn, scale=scale: e.mul(out=z_[:, b0:b0 + bn], in_=pb_[:, 0:bn], mul=scale), reads=[pb_], writes=[z_])
                        else:
                            ts("dve", z_[:, b0:b0 + bn], pb_[:, 0:bn], scale, None, ALU.mult, None, [pb_], [z_])
                        cp_i[0] += 1
                    dma("sp", zT_d[ch * 128:(ch + 1) * 128, :], z_[:], reads=[z_])
                    if ch < 27:
                        pb_ = next_bank()
                        for kc in range(KC):
                            mm(pb_[0:17, 0:128], xl[:, kc, :], w_[:, kc, :], kc == 0, kc == KC - 1, [w_, xl], [pb_])
                        cp(shs[ch % 2][0:17, :], pb_[0:17, 0:128], [pb_], [shs[ch % 2]])
                        dma("sp", sho_d[0:17, ch * 128:(ch + 1) * 128], shs[ch % 2][0:17, :], reads=[shs[ch % 2]])
                else:
                    zk_ = ztk[0]
                    for g4 in range(5):
                        pb_ = next_bank()
                        tl_ = list(range(g4 * 4, min(g4 * 4 + 4, 18)))
                        for q_, ti in enumerate(tl_):
                            c0, n = TILES[ti]
                            for kc in range(KC):
                                mm(pb_[0:n, q_ * 128:(q_ + 1) * 128], xnT[:, kc, c0:c0 + n], w_[:, kc, :], kc == 0, kc == KC - 1, [w_, xnT], [pb_])
                        nt = len(tl_)
                        src = pb_[:, 0:nt * 128].rearrange("p (q c) -> p q c", c=128)
                        cp(zk_[:, g4 * 4:g4 * 4 + nt, :], src, [pb_], [zk_])
                    cc = ch - NFM
                    dma("sp", ztk_d[0:16, cc * 128:(cc + 1) * 128], zk_[0:16, 0, :], reads=[zk_])
                    dma("sp", ztk_d[16:TP, cc * 128:(cc + 1) * 128].rearrange("(t p) c -> p t c", p=128), zk_[:, 1:17, :], reads=[zk_])
                    dma("sp", ztk_d[TP:T, cc * 128:(cc + 1) * 128], zk_[0:64, 17, :], reads=[zk_])
            P.flush()

        P.barrier()

        with ExitStack() as s2:
            FB = [SBs(s2, "fb%d" % i, [128, T]) for i in range(10)]
            za, zp, rs, ks, vs, sg, av, gT, Cb, kkn = FB
            t1, t2 = za, zp
            bonus = SBs(s2, "bonus", [128, T], BF16)
            junk = SBs(s2, "junk2", [128, 256])
            AR = SBs(s2, "AR", [128, 2, T], BF16)
            KRt = SBs(s2, "KRt", [128, 2, T], BF16)
            vsb = SBs(s2, "vsb", [128, T], BF16)
            la = SBs(s2, "la", [128, T], BF16)
            sgd = SBs(s2, "sgd", [128, T], BF16)
            sgd2 = SBs(s2, "sgd2", [32, T], BF16)
            wa2 = SBs(s2, "wa2", [128, 1024], BF16)
            g2a = SBs(s2, "g2a", [128, 1024], BF16)
            g2b = SBs(s2, "g2b", [32, 1024], BF16)
            pcl = SBs(s2, "pcl", [128, 40])
            dma("pool", wa2[0:64, :], w2_d[:, :], writes=[wa2])
            dma("pool", wa2[64:128, :], a2_d[:, :], writes=[wa2])
            dma("pool", g2a[:, :], g2_d[0:128, :], writes=[g2a])
            dma("pool", g2b[:, :], g2_d[128:160, :], writes=[g2b])

            def load_only(ch, out):
                dma("sp", out[:, :], zT_d[ch * 128:(ch + 1) * 128, :], writes=[out])

            def shift_only(ch, out):
                tt("dve", zp[:, 1:T], out[:, 0:T - 1], out[:, 1:T], ALU.subtract, [out], [zp])
                ts("dve", zp[:, 0:1], out[:, 0:1], -1.0, None, ALU.mult, None, [out], [zp])
                tt("dve", zp[:, TP:T:4], sshT[:, ch, :], out[:, TP:T:4], ALU.subtract, [sshT, out], [zp])
                stt(out[:, :], zp[:, :], mu[:, ch:ch + 1], out[:, :], ALU.mult, ALU.add, [zp, mu, out], [out])

            def load_shift(ch, out, out_dt_tile=None):
                load_only(ch, out)
                shift_only(ch, out)

            load_shift(24, rs)
            act(la[0:64, :], rs[0:64, :], AF.Tanh, [rs], [la])
            cp(la[64:128, :], rs[64:128, :], [rs], [la], eng="dve")
            load_shift(25, ks)
            act(sgd[:, :], ks[:, :], AF.Sigmoid, [ks], [sgd])
            load_shift(26, vs)
            act(sgd2[0:32, :], vs[0:32, :], AF.Sigmoid, [vs], [sgd2])

            chk(2)
            onb = SBs(s2, "onb", [128, 256], BF16)
            of32 = SBs(s2, "of32", [128, 128])
            PSst = SBs(s2, "PSst", [128, 128])
            PSbb = SBs(s2, "PSbb", [128, 128], BF16)

            def write_oT(kc, slot, src, n, src_tiles, first, eng="dve"):
                if slot == "s":
                    return
                if slot[0] == "own":
                    c0 = slot[1]
                    cp(oT[:, kc, c0:c0 + n], src, src_tiles, [oT], eng=eng)
                else:
                    _, s_i, second = slot
                    c0 = s_i * 128
                    if not second:
                        ts(eng, oT[:, kc, c0:c0 + n], src, sel[:, 0:1], None, ALU.mult, None, src_tiles + [sel], [oT])
                    elif eng == "dve":
                        stt(oT[:, kc, c0:c0 + n], src, sel[:, 1:2], oT[:, kc, c0:c0 + n], ALU.mult, ALU.add, src_tiles + [sel, oT], [oT])
                    else:
                        ts(eng, src, src, sel[:, 1:2], None, ALU.mult, None, src_tiles + [sel], src_tiles)
                        tt(eng, oT[:, kc, c0:c0 + n], oT[:, kc, c0:c0 + n], src, ALU.add, src_tiles + [oT], [oT])

            def slot_of(ti, b=None):
                if ti == 0:
                    return "s"
                if ti <= 16:
                    return ("pred", (ti - 1) % 8, ti > 8)
                return ("own", 1024 + 4 * b)

            class Carve:
                def __init__(self, bufs):
                    self.bufs = bufs
                    self.i = 0
                    self.off = 0

                def get(self, shape, dt):
                    nelem = int(np.prod(shape[1:]))
                    nf = (nelem * (2 if dt == BF16 else 4) + 3) // 4
                    nf = (nf + 7) // 8 * 8
                    if self.off + nf > T:
                        self.i += 1
                        self.off = 0
                    fb = self.bufs[self.i]
                    v = fb.t[0:shape[0], self.off:self.off + nf]
                    self.off += nf
                    if dt == BF16:
                        v = v.bitcast(BF16)
                    v = v[:, 0:nelem]
                    if len(shape) == 3:
                        v = v.rearrange("p (a b) -> p a b", b=shape[2])
                    return TL(v)

            class SetNS:
                pass

            import os as _os
            NSET = int(_os.environ.get('KB_NSET', '5'))
            cv = Carve([za, zp, rs, ks, vs, sg, av, Cb, kkn])
            SETS = []
            for si in range(NSET):
                S_ = SetNS()
                S_.tokb = cv.get([128, 3, 128], BF16)
                S_.A4 = [cv.get([128, 4, 128], BF16) for h in range(2)]
                S_.MM = [cv.get([128, 4, 128], BF16) for i in range(2)]
                S_.Xb = cv.get([128, 2, 128], BF16)
                S_.Wb = cv.get([128, 128], BF16)
                S_.UTb = cv.get([128, 128], BF16)
                S_.Sbb = cv.get([128, 128], BF16)
                S_.onb = cv.get([128, 128], BF16)
                S_.Sst = cv.get([128, 128], F32)
                S_.tmpS = cv.get([128, 128], F32)
                S_.of32 = cv.get([128, 128], F32)
                S_.bst = cv.get([128, 2, 6], F32)
                S_.mv = cv.get([128, 2, 2], F32)
                S_.Sn = cv.get([64, 2, 64], F32)
                S_.T2 = cv.get([128, 64], F32)
                S_.So = cv.get([64, 2, 64], F32)
                SETS.append(S_)

            SG = SetNS()
            SG.SnA = cv.get([64, 8, 128], F32)
            SG.SstA = cv.get([128, 8, 128], F32)
            SG.SbbA = cv.get([128, 8, 128], BF16)
            SG.am = cv.get([128, 16, 32], BF16)
            SG.tokm = cv.get([128, 8, 256], BF16)
            SG.T2A = cv.get([128, 8, 64], F32)
            SG.tmp4 = cv.get([128, 4, 128], F32)

            free_f = []
            free_b = []

            def wait_f():
                while not free_f:
                    yield

            def wait_b():
                while not free_b:
                    yield

            pstate = {"next": 0}
            eps_gn = SBs(s2, "eps_gn", [128, 1])
            op("dve", lambda e: e.memset(eps_gn[:, :], 64e-5), [], [eps_gn])

            def rwkv_pre(S_, tile, n, lv, mk4, msl):
                tokb, A4, MM, Xb = S_.tokb, S_.A4, S_.MM, S_.Xb
                yield from wait_b()
                pt = free_b.pop(0)
                tr(pt[0:n, 0:128], KRt[:, 0, tile], identb[:, :], [KRt, identb], [pt])
                tr(pt[0:n, 128:256], KRt[:, 1, tile], identb[:, :], [KRt, identb], [pt])
                tr(pt[0:n, 256:384], vsb[:, tile], identb[:, :], [vsb, identb], [pt])
                yield
                cp(tokb[0:n, :, :], pt[0:n, 0:384].rearrange("p (q c) -> p q c", c=128), [pt], [tokb], eng="act")
                free_b.append(pt)
                yield
                msk4 = cst[0:n, COFF[mk4]:COFF[mk4] + 512].rearrange("p (q c) -> p q c", c=128)[:, :, 0:n]
                for h in range(2):
                    hs = slice(64 * h, 64 * h + 64)
                    yield from wait_f()
                    pn = free_f.pop(0)
                    mm(pn[0:n, 0:n], AR[hs, 0, tile], KRt[hs, 0, tile], True, True, [KRt, AR], [pn])
                    yield
                    tt("dve", MM[0][0:n, 2 * h + 1, 0:n], pn[0:n, 0:n], CS(msl, n, 0, n), ALU.mult, [pn, cst], [MM[0]])
                    free_f.append(pn)
                    yield from wait_f()
                    pb_ = free_f.pop(0)
                    for q_ in range(2):
                        mm(pb_[0:n, q_ * 128:q_ * 128 + n], KRt[hs, 0, tile], AR[hs, q_, tile], True, True, [KRt, AR], [pb_])
                        mm(pb_[0:n, 256 + q_ * 128:256 + q_ * 128 + n], KRt[hs, 1, tile], AR[hs, q_, tile], True, True, [KRt, AR], [pb_])
                    yield
                    o4 = pb_[0:n, 0:512].rearrange("p (q c) -> p q c", c=128)[:, :, 0:n]
                    tt("dve", A4[h][0:n, :, 0:n], o4, msk4, ALU.mult, [pb_, cst], [A4[h]])
                    free_f.append(pb_)
                    yield
                    tt("pool", Xb[0:n, h, 0:n], A4[h][0:n, 0, 0:n], identb[0:n, 0:n], ALU.add, [A4[h], identb], [Xb])
                for lev in range(1, lv):
                    cur = MM[(lev - 1) % 2]
                    nxt = MM[lev % 2]
                    last = (lev == lv - 1)
                    yield from wait_f()
                    pb_ = free_f.pop(0)
                    for h in range(2):
                        cM = A4[h][0:n, 0, 0:n] if lev == 1 else cur[0:n, 2 * h, 0:n]
                        cMT = cur[0:n, 2 * h + 1, 0:n]
                        rd = [cur, A4[h]]
                        if not last:
                            mm(pb_[0:n, (2 * h) * 128:(2 * h) * 128 + n], cMT, cM, True, True, rd, [pb_])
                        mm(pb_[0:n, (2 * h + 1) * 128:(2 * h + 1) * 128 + n], cM, cMT, True, True, rd, [pb_])
                    yield
                    if last:
                        for h in range(2):
                            cp(nxt[0:n, 2 * h + 1, 0:n], pb_[0:n, (2 * h + 1) * 128:(2 * h + 1) * 128 + n], [pb_], [nxt], eng=("act" if h == 0 else "dve"))
                    else:
                        cp(nxt[0:n, :, 0:n], pb_[0:n, 0:512].rearrange("p (q c) -> p q c", c=128)[:, :, 0:n], [pb_], [nxt], eng=("act" if lev % 2 == 0 else "dve"))
                    free_f.append(pb_)
                    yield
                    yield from wait_f()
                    px = free_f.pop(0)
                    for h in range(2):
                        mm(px[0:n, h * 128:h * 128 + n], identb[0:n, 0:n], Xb[0:n, h, 0:n], True, False, [identb, Xb], [px])
                        mm(px[0:n, h * 128:h * 128 + n], nxt[0:n, 2 * h + 1, 0:n], Xb[0:n, h, 0:n], False, True, [nxt, Xb], [px])
                    yield
                    cp(Xb[0:n, :, 0:n], px[0:n, 0:256].rearrange("p (q c) -> p q c", c=128)[:, :, 0:n], [px], [Xb], eng=("dve" if lev % 2 == 0 else "act"))
                    free_f.append(px)
                    yield

            def rwkv_post(S_, po, hp, tile, n, slot):
                bst, mv, onb_, of_ = S_.bst, S_.mv, S_.onb, S_.of32
                for h in range(2):
                    hc = slice(64 * h, 64 * h + 64)
                    op("dve", lambda e, h=h, hc=hc: e.bn_stats(out=bst[0:n, h, :], in_=po[0:n, hc]), [po], [bst])
                    op("dve", lambda e, h=h: e.bn_aggr(out=mv[0:n, h, :], in_=bst[0:n, h, :]), [bst], [mv])
                yield
                if _os.environ.get('KB_LN', '1') == '1':
                    act(mv[0:n, :, 1], mv[0:n, :, 1], AF.Ln, [mv], [mv], bias=eps_gn[0:n, 0:1])
                    act(mv[0:n, :, 1], mv[0:n, :, 1], AF.Exp, [mv], [mv], scale=-0.5)
                else:
                    ts("dve", mv[0:n, :, 1], mv[0:n, :, 1], 64e-5, None, ALU.add, None, [mv], [mv])
                    op("act", lambda e: e.sqrt(out=mv[0:n, :, 1], in_=mv[0:n, :, 1]), [mv], [mv])
                    op("dve", lambda e: e.reciprocal(out=mv[0:n, :, 1], in_=mv[0:n, :, 1]), [mv], [mv])
                yield
                for h in range(2):
                    hc = slice(64 * h, 64 * h + 64)
                    ts("dve", onb_[0:n, hc], po[0:n, hc], mv[0:n, h, 0:1], mv[0:n, h, 1:2], ALU.subtract, ALU.mult, [po, mv], [onb_])
                free_f.append(po)
                yield
                yield from wait_b()
                pt = free_b.pop(0)
                tr(pt[:, 0:n], onb_[0:n, 0:128], identb[0:n, 0:n], [onb_, identb], [pt])
                yield
                op("act", lambda e: e.activation(out=of_[:, 0:n], in_=pt[:, 0:n], func=AF.Identity, bias=prm[:, 6, hp:hp + 1], scale=prm[:, 5, hp:hp + 1]), [pt, prm], [of_])
                free_b.append(pt)
                yield
                tt("pool", of_[:, 0:n], of_[:, 0:n], bonus[:, tile], ALU.add, [of_, bonus], [of_])
                tt("pool", of_[:, 0:n], of_[:, 0:n], gT[:, tile], ALU.mult, [of_, gT], [of_])
                write_oT(hp, slot, of_[:, 0:n], n, [of_], None, eng="pool")
                yield


            def rwkv_task(hp, c0, n, pidx, slot, S_, Sst, Sbb, ti, b):
                tile = slice(c0, c0 + n)
                lv = int(np.ceil(np.log2(n)))
                tokb, A4, MM, Xb, Wb, UTb, tmpS, bst, mv, onb_, of_ = S_.tokb, S_.A4, S_.MM, S_.Xb, S_.Wb, S_.UTb, S_.tmpS, S_.bst, S_.mv, S_.onb, S_.of32
                if b is not None:
                    Sn, T2, So = S_.Sn, S_.T2, S_.So
                    dma("sp", Sn[:, :, :], srw_d[b, 2 * hp:2 * hp + 2, :, :].rearrange("h i j -> i h j"), writes=[Sn])
                    yield from wait_f()
                    pb_ = free_f.pop(0)
                    tr(pb_[:, 0:64], Sn[:, :, :].rearrange("p h j -> p (h j)"), identf[0:64, 0:64], [Sn, identf], [pb_])
                    op("pool", lambda e: e.memset(Sst[:, :], 0.0), [], [Sst])
                    yield
                    cp(Sst[0:64, 0:64], pb_[0:64, 0:64], [pb_], [Sst], eng="dve")
                    cp(Sst[64:128, 64:128], pb_[64:128, 0:64], [pb_], [Sst], eng="act")
                    free_f.append(pb_)
                    yield
                    cp(Sbb[:, :], Sst[:, :], [Sst], [Sbb], eng="pool")
                yield from rwkv_pre(S_, tile, n, lv, "m4", "sl")
                if b is None:
                    while pstate["next"] != ti:
                        yield
                yield from wait_f()
                pw = free_f.pop(0)
                mm(pw[0:n, 0:128], AR[:, 0, tile], Sbb[:, :], True, False, [AR, Sbb], [pw])
                for h in range(2):
                    hc = slice(64 * h, 64 * h + 64)
                    mm(pw[0:n, hc], A4[h][0:n, 2, 0:n], tokb[0:n, 2, hc], False, h == 1, [A4[h], tokb], [pw])
                yield
                cp(Wb[0:n, :], pw[0:n, 0:128], [pw], [Wb], eng="act")
                free_f.append(pw)
                yield
                yield from wait_f()
                pu = free_f.pop(0)
                for h in range(2):
                    hc = slice(64 * h, 64 * h + 64)
                    mm(pu[0:n, hc], Xb[0:n, h, 0:n], Wb[0:n, hc], True, True, [Xb, Wb], [pu])
                yield
                cp(UTb[0:n, :], pu[0:n, 0:128], [pu], [UTb], eng="dve")
                free_f.append(pu)
                yield
                yield from wait_f()
                psc = free_f.pop(0)
                mm(psc[:, 0:128], tokb[0:n, 0, :], UTb[0:n, :], True, False, [tokb, UTb], [psc])
                mm(psc[:, 0:128], tokb[0:n, 1, :], tokb[0:n, 2, :], False, True, [tokb], [psc])
                po = None
                if slot != "s":
                    yield from wait_f()
                    po = free_f.pop(0)
                    mm(po[0:n, 0:128], AR[:, 1, tile], Sbb[:, :], True, False, [AR, Sbb], [po])
                    for h in range(2):
                        hc = slice(64 * h, 64 * h + 64)
                        mm(po[0:n, hc], A4[h][0:n, 1, 0:n], UTb[0:n, hc], False, False, [A4[h], UTb], [po])
                        mm(po[0:n, hc], A4[h][0:n, 3, 0:n], tokb[0:n, 2, hc], False, h == 1, [A4[h], tokb], [po])
                yield
                stt(tmpS[:, :], psc[:, 0:128], pcl[:, pidx:pidx + 1], CS("bd", 128), ALU.mult, ALU.mult, [psc, pcl, cst], [tmpS])
                free_f.append(psc)
                stt(Sst[:, :], Sst[:, :], pcl[:, pidx:pidx + 1], tmpS[:, :], ALU.mult, ALU.add, [Sst, pcl, tmpS], [Sst])
                yield
                cp(Sbb[:, :], Sst[:, :], [Sst], [Sbb], eng="act")
                if b is None:
                    pstate["next"] = ti + 1
                yield
                if b is not None or ti == 16:
                    Sn, T2, So = S_.Sn, S_.T2, S_.So
                    cp(T2[0:64, :], Sst[0:64, 0:64], [Sst], [T2], eng="pool")
                    cp(T2[64:128, :], Sst[64:128, 64:128], [Sst], [T2], eng="pool")
                    yield from wait_f()
                    pb_ = free_f.pop(0)
                    tr(pb_[0:64, 0:128], T2[:, 0:64], identf[:, :], [T2, identf], [pb_])
                    yield
                    cp(So[:, :, :], pb_[0:64, 0:128].rearrange("p (h j) -> p h j", j=64), [pb_], [So], eng="act")
                    free_f.append(pb_)
                    row = 0 if b is None else 1 + b
                    dma("sp", rwo_d[row, 2 * hp:2 * hp + 2, :, :].rearrange("h i j -> i h j"), So[:, :, :], reads=[So])
                    yield
                if po is None:
                    return
                yield from rwkv_post(S_, po, hp, tile, n, slot)

            def rwkv_sgroup_task(hp, g_i, S_):
                ng, n, lv = 8, 32, 2
                c0 = TP + 32 * g_i
                tile = slice(c0, c0 + n)
                b0 = 8 * g_i
                tokb, A4, MM, Xb, Wb, UTb = S_.tokb, S_.A4, S_.MM, S_.Xb, S_.Wb, S_.UTb
                SnA, SstA, SbbA, am, tokm, T2A, tmp4 = SG.SnA, SG.SstA, SG.SbbA, SG.am, SG.tokm, SG.T2A, SG.tmp4
                for h in range(2):
                    dma("sp", SnA[:, :, h * 64:(h + 1) * 64], srw_d[b0:b0 + ng, 2 * hp + h, :, :].rearrange("b i j -> i b j"), writes=[SnA])
                op("pool", lambda e: e.memset(SstA[:, :, :], 0.0), [], [SstA])
                yield from wait_f()
                pk = free_f.pop(0)
                for bb in range(ng):
                    tr(pk[:, bb * 64:(bb + 1) * 64], SnA[:, bb, :], identf[0:64, 0:64], [SnA, identf], [pk])
                yield
                cp(SstA[0:64, :, 0:64], pk[0:64, 0:512].rearrange("p (b i) -> p b i", i=64), [pk], [SstA], eng="dve")
                cp(SstA[64:128, :, 64:128], pk[64:128, 0:512].rearrange("p (b i) -> p b i", i=64), [pk], [SstA], eng="act")
                free_f.append(pk)
                yield
                cp(SbbA[:, :, :], SstA[:, :, :], [SstA], [SbbA], eng="pool")
                yield
                yield from rwkv_pre(S_, tile, n, lv, "m4s", "slb")
                bselv = cst[:, COFF["bsel"]:COFF["bsel"] + 1024].rearrange("p (b c) -> p b c", c=64)[:, 0:ng, 0:n]
                for w_ in range(2):
                    tt("pool" if w_ else "dve", am[:, w_ * ng:(w_ + 1) * ng, :], AR[:, w_, tile].unsqueeze(1).broadcast_to([128, ng, n]), bselv, ALU.mult, [AR, cst], [am])
                rselv = cst[0:n, COFF["rsel"]:COFF["rsel"] + ng].unsqueeze(2).broadcast_to([n, ng, 256])
                tt("dve", tokm[0:n, :, :], tokb[0:n, 0:2, :].rearrange("p q c -> p (q c)").unsqueeze(1).broadcast_to([n, ng, 256]), rselv, ALU.mult, [tokb, cst], [tokm])
                yield
                yield from wait_f()
                pw = free_f.pop(0)
                for bb in range(ng):
                    mm(pw[0:n, 0:128], am[:, bb, :], SbbA[:, bb, :], bb == 0, False, [am, SbbA], [pw])
                for h in range(2):
                    hc = slice(64 * h, 64 * h + 64)
                    mm(pw[0:n, hc], A4[h][0:n, 2, 0:n], tokb[0:n, 2, hc], False, h == 1, [A4[h], tokb], [pw])
                yield
                cp(Wb[0:n, :], pw[0:n, 0:128], [pw], [Wb], eng="act")
                free_f.append(pw)
                yield
                yield from wait_f()
                pu = free_f.pop(0)
                for h in range(2):
                    hc = slice(64 * h, 64 * h + 64)
                    mm(pu[0:n, hc], Xb[0:n, h, 0:n], Wb[0:n, hc], True, True, [Xb, Wb], [pu])
                yield
                cp(UTb[0:n, :], pu[0:n, 0:128], [pu], [UTb], eng="dve")
                free_f.append(pu)
                yield
                yield from wait_f()
                po = free_f.pop(0)
                for bb in range(ng):
                    mm(po[0:n, 0:128], am[:, ng + bb, :], SbbA[:, bb, :], bb == 0, False, [am, SbbA], [po])
                for h in range(2):
                    hc = slice(64 * h, 64 * h + 64)
                    mm(po[0:n, hc], A4[h][0:n, 1, 0:n], UTb[0:n, hc], False, False, [A4[h], UTb], [po])
                    mm(po[0:n, hc], A4[h][0:n, 3, 0:n], tokb[0:n, 2, hc], False, h == 1, [A4[h], tokb], [po])
                yield
                for q_ in range(2):
                    yield from wait_f()
                    psc = free_f.pop(0)
                    for b4 in range(4):
                        bb = 4 * q_ + b4
                        mm(psc[:, b4 * 128:(b4 + 1) * 128], tokm[0:n, bb, 0:128], UTb[0:n, :], True, False, [tokm, UTb], [psc])
                        mm(psc[:, b4 * 128:(b4 + 1) * 128], tokm[0:n, bb, 128:256], tokb[0:n, 2, :], False, True, [tokm, tokb], [psc])
                    yield
                    pcb = pcl[:, 17 + b0 + 4 * q_:17 + b0 + 4 * q_ + 4].unsqueeze(2).broadcast_to([128, 4, 128])
                    bdb = CS("bd", 128).unsqueeze(1).broadcast_to([128, 4, 128])
                    tt("dve", tmp4[:, :, :], psc[:, 0:512].rearrange("p (b c) -> p b c", c=128), pcb, ALU.mult, [psc, pcl], [tmp4])
                    free_f.append(psc)
                    tt("pool", tmp4[:, :, :], tmp4[:, :, :], bdb, ALU.mult, [tmp4, cst], [tmp4])
                    tt("dve", SstA[:, 4 * q_:4 * q_ + 4, :], SstA[:, 4 * q_:4 * q_ + 4, :], pcb, ALU.mult, [SstA, pcl], [SstA])
                    yield
                    tt("pool", SstA[:, 4 * q_:4 * q_ + 4, :], SstA[:, 4 * q_:4 * q_ + 4, :], tmp4[:, :, :], ALU.add, [SstA, tmp4], [SstA])
                    yield
                cp(T2A[0:64, :, :], SstA[0:64, :, 0:64], [SstA], [T2A], eng="dve")
                cp(T2A[64:128, :, :], SstA[64:128, :, 64:128], [SstA], [T2A], eng="pool")
                yield
                for q_ in range(2):
                    yield from wait_f()
                    pk = free_f.pop(0)
                    for b4 in range(4):
                        tr(pk[0:64, b4 * 128:(b4 + 1) * 128], T2A[:, 4 * q_ + b4, :], identf[:, :], [T2A, identf], [pk])
                    yield
                    cp(SnA[:, 4 * q_:4 * q_ + 4, :], pk[0:64, 0:512].rearrange("p (b c) -> p b c", c=128), [pk], [SnA], eng="act")
                    free_f.append(pk)
                    yield
                for h in range(2):
                    dma("sp", rwo_d[1 + b0:1 + b0 + ng, 2 * hp + h, :, :].rearrange("b i j -> i b j"), SnA[:, :, h * 64:(h + 1) * 64], reads=[SnA])
                yield
                yield from rwkv_post(S_, po, hp, tile, n, ("own", 1024 + 32 * g_i))

            def run_tasks(makers, sets):
                active = []
                makers = list(makers)
                free_sets = list(sets)
                while makers or active:
                    while makers and free_sets:
                        S_ = free_sets.pop(0)
                        active.append((makers.pop(0)(S_), S_))
                    for item in list(active):
                        try:
                            next(item[0])
                        except StopIteration:
                            active.remove(item)
                            free_sets.append(item[1])

            for hp in range(8):
                load_only(hp, rs)
                load_only(8 + hp, ks)
                load_only(16 + hp, vs)
                shift_only(hp, rs)
                shift_only(8 + hp, ks)
                shift_only(16 + hp, vs)
                for (b0, bn) in BLKS:
                    bl = slice(b0, b0 + bn)
                    pb_ = next_bank()
                    mm(pb_[:, 0:bn], wa2[0:64, hp * 128:(hp + 1) * 128], la[0:64, bl], True, True, [wa2, la], [pb_])
                    act(sg[:, bl], pb_[:, 0:bn], AF.Sigmoid, [pb_, prm], [sg], bias=prm[:, 0, hp:hp + 1])
                    pb_ = next_bank()
                    mm(pb_[:, 0:bn], wa2[64:128, hp * 128:(hp + 1) * 128], la[64:128, bl], True, True, [wa2, la], [pb_])
                    act(av[:, bl], pb_[:, 0:bn], AF.Sigmoid, [pb_, prm], [av], bias=prm[:, 1, hp:hp + 1])
                    pb_ = next_bank()
                    mm(pb_[:, 0:bn], g2a[:, hp * 128:(hp + 1) * 128], sgd[:, bl], True, False, [g2a, sgd], [pb_])
                    mm(pb_[:, 0:bn], g2b[0:32, hp * 128:(hp + 1) * 128], sgd2[0:32, bl], False, True, [g2b, sgd2], [pb_])
                    cp(gT[:, bl], pb_[:, 0:bn], [pb_], [gT])
                op("act", lambda e, hp=hp: e.mul(out=t1[:, :], in_=ks[:, :], mul=prm[:, 2, hp:hp + 1]), [ks, prm], [t1])
                op("act", lambda e: e.square(out=t2[:, :], in_=t1[:, :]), [t1], [t2])
                for (b0, bn) in BLKS:
                    bl = slice(b0, b0 + bn)
                    pb_ = next_bank()
                    mm(pb_[:, 0:bn], CS("bd", 128), t2[:, bl], True, True, [cst, t2], [pb_])
                    ts("dve", kkn[:, bl], pb_[:, 0:bn], 1e-24, None, ALU.max, None, [pb_], [kkn])
                act(kkn[:, :], kkn[:, :], AF.Ln, [kkn], [kkn])
                act(kkn[:, :], kkn[:, :], AF.Exp, [kkn], [kkn], scale=-0.5)
                tt("pool", kkn[:, :], kkn[:, :], t1[:, :], ALU.mult, [kkn, t1], [kkn])
                tt("pool", t2[:, :], kkn[:, :], av[:, :], ALU.mult, [kkn, av], [t2])
                act(t1[:, :], av[:, :], AF.Identity, [av, prm, omka], [t1], bias=omka[:, hp:hp + 1], scale=prm[:, 3, hp:hp + 1])
                tt("dve", ks[:, :], ks[:, :], t1[:, :], ALU.mult, [ks, t1], [ks])
                stt(t1[:, :], rs[:, :], prm[:, 4, hp:hp + 1], ks[:, :], ALU.mult, ALU.mult, [rs, prm, ks], [t1])
                for (b0, bn) in BLKS:
                    bl = slice(b0, b0 + bn)
                    pb_ = next_bank()
                    mm(pb_[:, 0:bn], CS("bd", 128), t1[:, bl], True, True, [cst, t1], [pb_])
                    tt("dve", bonus[:, bl], pb_[:, 0:bn], vs[:, bl], ALU.mult, [pb_, vs], [bonus])
                for ti in range(17):
                    c0, n = TILES[ti]
                    op("dve", lambda e, c0=c0, n=n: e.tensor_tensor_scan(out=Cb[:, c0:c0 + n], data0=ones[:, 0:n], data1=sg[:, c0:c0 + n], initial=0.0, op0=ALU.mult, op1=ALU.add), [ones, sg], [Cb])
                cp(Cb[:, TP:T:4], sg[:, TP:T:4], [sg], [Cb], eng="dve")
                for t_ in range(1, 4):
                    tt("dve", Cb[:, TP + t_:T:4], Cb[:, TP + t_ - 1:T:4], sg[:, TP + t_:T:4], ALU.add, [Cb, sg], [Cb])
                act(t1[:, :], Cb[:, :], AF.Exp, [Cb], [t1], scale=-C0)
                cp(pcl[:, 0:1], t1[:, 15:16], [t1], [pcl], eng="dve")
                cp(pcl[:, 1:17], t1[:, 143:TP:128], [t1], [pcl], eng="dve")
                cp(pcl[:, 17:33], t1[:, TP + 3:T:4], [t1], [pcl], eng="dve")
                tt("pool", AR[:, 1, :], rs[:, :], t1[:, :], ALU.mult, [rs, t1], [AR])
                tt("dve", sg[:, :], Cb[:, :], sg[:, :], ALU.subtract, [Cb, sg], [sg])
                act(sg[:, :], sg[:, :], AF.Exp, [sg], [sg], scale=-C0)
                stt(AR[:, 0, :], kkn[:, :], -1.0, sg[:, :], ALU.mult, ALU.mult, [kkn, sg], [AR])
                act(rs[:, :], Cb[:, :], AF.Exp, [Cb, AR], [rs], scale=C0)
                tt("dve", KRt[:, 0, :], t2[:, :], rs[:, :], ALU.mult, [t2, rs], [KRt])
                tt("pool", KRt[:, 1, :], ks[:, :], rs[:, :], ALU.mult, [ks, rs], [KRt])
                cp(vsb[:, :], vs[:, :], [vs], [vsb], eng="act")
                op("dve", lambda e: e.memset(PSst[:, :], 0.0), [], [PSst])
                op("dve", lambda e: e.memset(PSbb[:, :], 0.0), [], [PSbb])
                P.barrier()
                free_f[:] = list(pbank[0:6])
                free_b[:] = list(pbf)
                pstate["next"] = 0
                gens = []
                order = []
                for i in range(17):
                    order.append(("p", i))
                    if i in (1, 3):
                        order.append(("s", i // 2))
                order = order[:int(_os.environ.get('KB_NT', '99'))]
                for k_, (kind, i) in enumerate(order):
                    if kind == "p":
                        c0, n = TILES[i]
                        gens.append(lambda S_, c0=c0, n=n, i=i: rwkv_task(hp, c0, n, i, slot_of(i), S_, PSst, PSbb, i, None))
                    else:
                        gens.append(lambda S_, i=i: rwkv_sgroup_task(hp, i, S_))
                run_tasks(gens, SETS)
                chk(3)
                P.barrier()
                prr[0] = 0
                pfr[0] = 0
            chk(7)
            qe, qo, ke, ko = rs, ks, vs, sg
            cosT, sinT = av, gT
            QR = AR
            dma("sp", cosT[:, :], cos_d[:, :], writes=[cosT])
            dma("sp", sinT[:, :], sin_d[:, :], writes=[sinT])
            Sr = [SBs(s2, "Sr%d" % i, [128, 256]) for i in range(2)]
            Srb = [SBs(s2, "Srb%d" % i, [128, 256], BF16) for i in range(2)]
            QS = SBs(s2, "QS", [128, 2, 64])
            qsm = TL(Cb[:, 0:2048].rearrange("p (d b c) -> p d b c", d=2, b=16))
            qsm.b = Cb.b
            ktm = SBs(s2, "ktm", [64, 16, 256], BF16)
            Sin = [TL(kkn[:, i * 512:(i + 1) * 512].rearrange("p (d e) -> p d e", d=2)) for i in range(3)]
            op("dve", lambda e: e.memset(kkn[:, 0:1], 0.0), [], [kkn])
            for s_i in Sin:
                s_i.b.w = kkn.b.w
            rjunk = SBs(s2, "rjunk", [128, 256])

            def make_rset(get):
                R_ = SetNS()
                R_.gt = get([128, 256], F32)
                R_.of32 = get([128, 128], F32)
                R_.rstat = get([128, 4], F32)
                R_.vtb = get([128, 256], BF16)
                R_.ktk = get([128, 2, 128], BF16)
                R_.scb = get([128, 128], BF16)
                R_.qsb = get([128, 2, 128], BF16)
                R_.onb = get([128, 256], BF16)
                return R_
            rcv = Carve([za, zp])
            RSETS = [make_rset(rcv.get) for _ in range(4)]
            rcnt = [0]

            def real_get(shape, dt):
                rcnt[0] += 1
                return SBs(s2, "rs5_%d" % rcnt[0], shape, dt)
            RSETS.append(make_rset(real_get))
            rstate = {"next": 0}

            def ret_opost(po, n, h, slot, R_):
                g_, rstat, onb_, of_ = R_.gt, R_.rstat, R_.onb, R_.of32
                op("act", lambda e: e.activation(out=rjunk[0:n, 0:256], in_=po[0:n, 0:256], func=AF.Square, accum_out=rstat[0:n, 0:1]), [po], [rjunk, rstat])
                yield
                ts("dve", rstat[0:n, 1:2], rstat[0:n, 0:1], 1.0 / 256.0, 1e-6, ALU.mult, ALU.add, [rstat], [rstat])
                yield
                act(rstat[0:n, 1:2], rstat[0:n, 1:2], AF.Ln, [rstat], [rstat])
                act(rstat[0:n, 1:2], rstat[0:n, 1:2], AF.Exp, [rstat], [rstat], scale=-0.5)
                act(g_[0:n, :], g_[0:n, :], AF.Silu, [g_], [g_])
                yield
                stt(onb_[0:n, :], po[0:n, 0:256], rstat[0:n, 1:2], g_[0:n, :], ALU.mult, ALU.mult, [po, rstat, g_], [onb_])
                yield

            def ret_opost2(n, h, slot, R_):
                onb_, of_ = R_.onb, R_.of32
                yield from wait_b()
                pt = free_b.pop(0)
                for j in range(2):
                    tr(pt[:, j * 128:j * 128 + n], onb_[0:n, j * 128:(j + 1) * 128], identb[0:n, 0:n], [onb_, identb], [pt])
                yield
                for j in range(2):
                    cp(of_[:, 0:n], pt[:, j * 128:j * 128 + n], [pt], [of_], eng="act")
                    yield
                    write_oT(8 + 2 * h + j, slot, of_[:, 0:n], n, [of_], None, eng="dve")
                    yield
                free_b.append(pt)

            def ret_task(h, ti, R_):
                g = GAM[h]
                c0, n = TILES[ti]
                tile = slice(c0, c0 + n)
                g_, vtb, ktk, scb, qsb = R_.gt, R_.vtb, R_.ktk, R_.scb, R_.qsb
                dma("pool", vtb[0:n, :], ztk_d[c0:c0 + n, 2 * h * 128:(2 * h + 2) * 128], writes=[vtb])
                dma("sp", g_[0:n, :], ztk_d[c0:c0 + n, 1024 + 2 * h * 128:1024 + (2 * h + 2) * 128], writes=[g_])
                var = 0 if ti == 0 else 1
                yield from wait_b()
                pt = free_b.pop(0)
                for dc in range(2):
                    tr(pt[0:n, dc * 128:(dc + 1) * 128], KRt[:, dc, tile], identb[:, :], [KRt, identb], [pt])
                yield
                ts("dve", ktk[0:n, :, :], pt[0:n, 0:256].rearrange("p (q c) -> p q c", c=128), cst[0:n, COFF["gk"] + var * 4 + h:COFF["gk"] + var * 4 + h + 1], None, ALU.mult, None, [pt, cst], [ktk])
                free_b.append(pt)
                yield
                yield from wait_f()
                ps_ = free_f.pop(0)
                mm(ps_[0:n, 0:n], KRt[:, 0, tile], QR[:, 0, tile], True, False, [KRt, QR], [ps_])
                mm(ps_[0:n, 0:n], KRt[:, 1, tile], QR[:, 1, tile], False, True, [KRt, QR], [ps_])
                yield
                dmv = cst[0:n, COFF["dm"]:COFF["dm"] + 512].rearrange("p (h i) -> p h i", i=128)[:, h, 0:n]
                tt("dve", scb[0:n, 0:n], ps_[0:n, 0:n], dmv, ALU.mult, [ps_, cst], [scb])
                free_f.append(ps_)
                gqv = cst[:, COFF["gq"]:COFF["gq"] + 512].rearrange("p (h i) -> p h i", i=128)[:, h, 0:n]
                tt("dve", qsb[:, 0, 0:n], QR[:, 0, tile], gqv, ALU.mult, [QR, cst], [qsb])
                tt("pool", qsb[:, 1, 0:n], QR[:, 1, tile], gqv, ALU.mult, [QR, cst], [qsb])
                yield
                while rstate["next"] != ti:
                    yield
                po = None
                if ti > 0:
                    yield from wait_f()
                    po = free_f.pop(0)
                    mm(po[0:n, 0:256], scb[0:n, 0:n], vtb[0:n, :], True, False, [scb, vtb], [po])
                    mm(po[0:n, 0:256], qsb[:, 0, 0:n], Srb[0][:, :], False, False, [qsb, Srb[0]], [po])
                    mm(po[0:n, 0:256], qsb[:, 1, 0:n], Srb[1][:, :], False, True, [qsb, Srb[1]], [po])
                yield from wait_f()
                pS = free_f.pop(0)
                for dc in range(2):
                    mm(pS[:, dc * 256:(dc + 1) * 256], ktk[0:n, dc, :], vtb[0:n, :], True, True, [ktk, vtb], [pS])
                yield
                stt(Sr[0][:, :], Sr[0][:, :], float(g ** n), pS[:, 0:256], ALU.mult, ALU.add, [Sr[0], pS], [Sr[0]])
                stt(Sr[1][:, :], Sr[1][:, :], float(g ** n), pS[:, 256:512], ALU.mult, ALU.add, [Sr[1], pS], [Sr[1]])
                free_f.append(pS)
                cp(Srb[0][:, :], Sr[0][:, :], [Sr[0]], [Srb[0]], eng="act")
                cp(Srb[1][:, :], Sr[1][:, :], [Sr[1]], [Srb[1]], eng="act")
                rstate["next"] = ti + 1
                yield
                if ti == 16:
                    for dc in range(2):
                        dma("sp", rto_d[0, h, :, :].rearrange("(m two) e -> m two e", two=2)[:, dc, :], Sr[dc][:, :], reads=[Sr[dc]])
                if po is not None:
                    yield from ret_opost(po, n, h, slot_of(ti), R_)
                    free_f.append(po)
                    yield from ret_opost2(n, h, slot_of(ti), R_)

            def ret_sample_task(h, R_):
                g = GAM[h]
                ti = 17
                c0, n = TILES[ti]
                tile = slice(c0, c0 + n)
                g_, vtb, ktk, scb = R_.gt, R_.vtb, R_.ktk, R_.scb
                dma("pool", vtb[0:n, :], ztk_d[c0:c0 + n, 2 * h * 128:(2 * h + 2) * 128], writes=[vtb])
                dma("sp", g_[0:n, :], ztk_d[c0:c0 + n, 1024 + 2 * h * 128:1024 + (2 * h + 2) * 128], writes=[g_])
                yield from wait_b()
                pt = free_b.pop(0)
                for dc in range(2):
                    tr(pt[0:n, dc * 128:(dc + 1) * 128], KRt[:, dc, tile], identb[:, :], [KRt, identb], [pt])
                yield
                ts("dve", ktk[0:n, :, :], pt[0:n, 0:256].rearrange("p (q c) -> p q c", c=128), cst[0:n, COFF["gk"] + 2 * 4 + h:COFF["gk"] + 2 * 4 + h + 1], None, ALU.mult, None, [pt, cst], [ktk])
                free_b.append(pt)
                yield
                yield from wait_f()
                ps_ = free_f.pop(0)
                mm(ps_[0:n, 0:n], KRt[:, 0, tile], QR[:, 0, tile], True, False, [KRt, QR], [ps_])
                mm(ps_[0:n, 0:n], KRt[:, 1, tile], QR[:, 1, tile], False, True, [KRt, QR], [ps_])
                yield
                dmv = cst[0:n, COFF["dms"]:COFF["dms"] + 256].rearrange("p (h i) -> p h i", i=64)[:, h, 0:n]
                tt("dve", scb[0:n, 0:n], ps_[0:n, 0:n], dmv, ALU.mult, [ps_, cst], [scb])
                free_f.append(ps_)
                gqsv = cst[:, COFF["gqs"]:COFF["gqs"] + 256].rearrange("p (h i) -> p h i", i=64)[:, h, :]
                bselv = cst[:, COFF["bsel"]:COFF["bsel"] + 1024].rearrange("p (b c) -> p b c", c=64)
                for dc in range(2):
                    tt("pool", QS[:, dc, :], QS[:, dc, :], gqsv, ALU.mult, [QS, cst], [QS])
                    tt("pool", qsm[:, dc, :, :], QS[:, dc, :].unsqueeze(1).broadcast_to([128, 16, 64]), bselv, ALU.mult, [QS, cst], [qsm])
                yield
                for b in range(16):
                    ts("dve", ktm[0:64, b, :], ktk[0:64, :, :].rearrange("p q c -> p (q c)"), cst[0:64, COFF["rsel"] + b:COFF["rsel"] + b + 1], None, ALU.mult, None, [ktk, cst], [ktm])
                    yield
                mm(pacc[0:64, 0:256], scb[0:64, 0:64], vtb[0:64, :], True, False, [scb, vtb], [pacc])
                for b in range(16):
                    si = Sin[b % 3]
                    dma("sp", si[:, :, :], srt_d[b, h, :, :].rearrange("(m two) e -> m two e", two=2), writes=[si])
                    yield
                    for dc in range(2):
                        mm(pacc[0:64, 0:256], qsm[:, dc, b, :], si[:, dc, :], False, (b == 15 and dc == 1), [qsm, si], [pacc])
                    for dc in range(2):
                        yield from wait_f()
                        pS = free_f.pop(0)
                        mm(pS[:, 0:256], ktm[0:64, b, dc * 128:(dc + 1) * 128], vtb[0:64, :], True, True, [ktm, vtb], [pS])
                        yield
                        stt(si[:, dc, :], si[:, dc, :], float(g ** 4), pS[:, 0:256], ALU.mult, ALU.add, [si, pS], [si])
                        free_f.append(pS)
                    dma("sp", rto_d[1 + b, h, :, :].rearrange("(m two) e -> m two e", two=2), si[:, :, :], reads=[si])
                    yield
                yield from ret_opost(pacc, 64, h, ("own", 1024), R_)
                yield from ret_opost2(64, h, ("own", 1024), R_)

            for h in range(4):
                for i_, dst in enumerate((qe, qo, ke, ko)):
                    ch = 27 + 4 * h + i_
                    dma("sp", dst[:, :], zT_d[ch * 128:(ch + 1) * 128, :], writes=[dst])
                for (xe, xo, OUT) in ((qe, qo, QR), (ke, ko, KRt)):
                    tt("dve", t1[:, :], xe[:, :], cosT[:, :], ALU.mult, [xe, cosT], [t1])
                    tt("pool", t2[:, :], xo[:, :], sinT[:, :], ALU.mult, [xo, sinT], [t2])
                    tt("dve", OUT[:, 0, :], t1[:, :], t2[:, :], ALU.subtract, [t1, t2], [OUT])
                    if OUT is QR:
                        tt("dve", QS[:, 0, :], t1[:, TP:T], t2[:, TP:T], ALU.subtract, [t1, t2], [QS])
                    tt("dve", t1[:, :], xe[:, :], sinT[:, :], ALU.mult, [xe, sinT], [t1])
                    tt("pool", t2[:, :], xo[:, :], cosT[:, :], ALU.mult, [xo, cosT], [t2])
                    tt("dve", OUT[:, 1, :], t1[:, :], t2[:, :], ALU.add, [t1, t2], [OUT])
                    if OUT is QR:
                        tt("dve", QS[:, 1, :], t1[:, TP:T], t2[:, TP:T], ALU.add, [t1, t2], [QS])
                for dc in range(2):
                    op("dve", lambda e, dc=dc: e.memset(Sr[dc][:, :], 0.0), [], [Sr[dc]])
                    op("dve", lambda e, dc=dc: e.memset(Srb[dc][:, :], 0.0), [], [Srb[dc]])
                P.barrier()
                free_f[:] = list(pbank[0:5])
                free_b[:] = list(pbf)
                rstate["next"] = 0
                makers = [(lambda R_, h=h: ret_sample_task(h, R_))]
                for ti in range(17):
                    makers.append(lambda R_, h=h, ti=ti: ret_task(h, ti, R_))
                run_tasks(makers, RSETS)
                P.barrier()
                prr[0] = 0
                pfr[0] = 0
            chk(8)
            if debug:
                dma("sp", dbg["oT"][:, :, :], oT[:], reads=[oT])
            P.flush()
        P.barrier()

        with ExitStack() as s3:
            gam = SBs(s3, "gam3", [128, D])
            env["junk"] = SBs(s3, "junk3", [128, D])
            hres = [SBs(s3, "h%d" % i, [128, D]) for i in range(9)]
            hnT = SBs(s3, "hnT", [128, KC, OWN], BF16)
            OT = [(i * 128, 128) for i in range(8)] + [(1024, 64)]
            with ExitStack() as s3a:
                wo = SBs(s3a, "wo", [128, KC, 512], BF16)
                xo_ = [SBs(s3a, "xo%d" % i, [128, 512]) for i in range(2)]
                for blk in range(4):
                    dma("pool", wo[:], w_out[blk, :, :, :], writes=[wo])
                    for ti, (c0, n) in enumerate(OT):
                        x_ = xo_[(blk * 9 + ti) % 2]
                        dma("sp", x_[0:n, :], x_own[c0:c0 + n, blk * 512:(blk + 1) * 512], writes=[x_])
                        pb_ = next_bank()
                        for kc in range(KC):
                            mm(pb_[0:n, :], oT[:, kc, c0:c0 + n], wo[:, kc, :], kc == 0, kc == KC - 1, [oT, wo], [pb_])
                        tt("dve", hres[ti][0:n, blk * 512:(blk + 1) * 512], pb_[0:n, :], x_[0:n, :], ALU.add, [pb_, x_], [hres[ti]])
                if debug:
                    for ti, (c0, n) in enumerate(OT):
                        dma("sp", dbg["h"][c0:c0 + n, :], hres[ti][0:n, :], reads=[hres[ti]])
                P.flush()
            P.barrier()
            with ExitStack() as s3b:
                hb = [SBs(s3b, "hb%d" % i, [128, D], BF16) for i in range(2)]
                dma("sp", gam[:], gam_d[1, :].partition_broadcast(128), writes=[gam])
                for ti, (c0, n) in enumerate(OT):
                    s_ = rms_rstd(hres[ti], n, ti % 2)
                    hb_ = hb[ti % 2]
                    stt(hb_[0:n, :], hres[ti][0:n, :], s_[0:n, 1:2], gam[0:n, :], ALU.mult, ALU.mult, [hres[ti], s_, gam], [hb_])
                    transpose16(hb_, n, hnT, c0)
                P.flush()
            P.barrier()
            with ExitStack() as s3c:
                G = 4
                wg = [SBs(s3c, "wg%d" % i, [128, KC, 128], BF16) for i in range(2)]
                wu = [SBs(s3c, "wu%d" % i, [128, KC, 128], BF16) for i in range(2)]
                actT = SBs(s3c, "actT", [128, G, OWN], BF16)
                sgl = SBs(s3c, "sgl", [128, 512])
                oflat = oT[:, :, :].rearrange("p k c -> p (k c)")
                wd = []
                for i in range(2 * G):
                    w_ = TL(oflat[:, i * D:(i + 1) * D])
                    w_.b.r = list(oT.b.r)
                    w_.b.w = oT.b.w
                    wd.append(w_)
                RB = [(0, 512), (512, 512), (1024, 64)]
                for grp in range(NFB // G):
                    for fi in range(G):
                        fb = grp * G + fi
                        g_ = wg[fb % 2]
                        u_ = wu[fb % 2]
                        d_ = wd[fb % (2 * G)]
                        dma("pool", g_[:], w_gate[fb, :, :, :], writes=[g_])
                        dma("pool", u_[:], w_up[fb, :, :, :], writes=[u_])
                        dma("pool", d_[:, :], w_down[fb, :, :], writes=[d_])
                        for (r0, rn) in RB:
                            pg = next_bank()
                            for kc in range(KC):
                                mm(pg[:, 0:rn], g_[:, kc, :], hnT[:, kc, r0:r0 + rn], kc == 0, kc == KC - 1, [g_, hnT], [pg])
                            pu_ = next_bank()
                            for kc in range(KC):
                                mm(pu_[:, 0:rn], u_[:, kc, :], hnT[:, kc, r0:r0 + rn], kc == 0, kc == KC - 1, [u_, hnT], [pu_])
                            act(sgl[:, 0:rn], pg[:, 0:rn], AF.Silu, [pg], [sgl])
                            tt("dve", actT[:, fi, r0:r0 + rn], sgl[:, 0:rn], pu_[:, 0:rn], ALU.mult, [sgl, pu_], [actT])
                    for ti, (c0, n) in enumerate(OT):
                        for blk in range(4):
                            pb_ = next_bank()
                            for fi in range(G):
                                d_ = wd[(grp * G + fi) % (2 * G)]
                                mm(pb_[0:n, :], actT[:, fi, c0:c0 + n], d_[:, blk * 512:(blk + 1) * 512], fi == 0, fi == G - 1, [actT, d_], [pb_])
                            tt("dve", hres[ti][0:n, blk * 512:(blk + 1) * 512], hres[ti][0:n, blk * 512:(blk + 1) * 512], pb_[0:n, :], ALU.add, [hres[ti], pb_], [hres[ti]])
                dma("sp", gam[:], gam_d[2, :].partition_broadcast(128), writes=[gam])
                for ti, (c0, n) in enumerate(OT):
                    s_ = rms_rstd(hres[ti], n, ti % 2)
                    stt(hres[ti][0:n, :], hres[ti][0:n, :], s_[0:n, 1:2], gam[0:n, :], ALU.mult, ALU.mult, [hres[ti], s_, gam], [hres[ti]])
                    dma("sp", y_d[c0:c0 + n, :], hres[ti][0:n, :], reads=[hres[ti]])
                P.finish()
                P.flush()


_NC_CACHE = {}


def _prep_weights(inp):
    f = lambda a: np.ascontiguousarray(np.asarray(a, dtype=np.float32))
    w_in = f(inp["w_in"])[0]
    cols = list(range(0, 3360)) + [-1] * 96
    for h in range(4):
        for base in (3360, 3360 + 1024):
            cols += list(range(base + h * 256, base + (h + 1) * 256, 2))
            cols += list(range(base + h * 256 + 1, base + (h + 1) * 256, 2))
    fm = list(range(0, 3360)) + [-1] * 96
    for h in range(4):
        qb, kb = 3360 + h * 256, 3360 + 1024 + h * 256
        fm += list(range(qb, qb + 256, 2)) + list(range(qb + 1, qb + 256, 2))
        fm += list(range(kb, kb + 256, 2)) + list(range(kb + 1, kb + 256, 2))
    fm += list(range(3360 + 2048, 3360 + 4096))
    fm = np.array(fm)
    wp = np.zeros((D, NCH * 128), np.float32)
    valid = fm >= 0
    wp[:, valid] = w_in[:, fm[valid]]
    w_in_r = np.ascontiguousarray(wp.reshape(KC, 128, NCH, 128).transpose(2, 1, 0, 3))
    w_out = f(inp["w_out"])[0]
    w_out_r = np.ascontiguousarray(w_out.reshape(KC, 128, 4, 512).transpose(2, 1, 0, 3))
    wg = np.ascontiguousarray(f(inp["w_gate"])[0].reshape(KC, 128, NFB, 128).transpose(2, 1, 0, 3))
    wu = np.ascontiguousarray(f(inp["w_up"])[0].reshape(KC, 128, NFB, 128).transpose(2, 1, 0, 3))
    wd = np.ascontiguousarray(f(inp["w_down"])[0].reshape(NFB, 128, D))
    gam = np.stack([f(inp["norm_mix"])[0], f(inp["norm_ffn"])[0], f(inp["norm_final"])])
    mu = np.zeros((27 * 128,), np.float32)
    mu[:3360] = f(inp["rwkv_mu"])[0]
    mu = np.ascontiguousarray(mu.reshape(27, 128).T)
    names = ["rwkv_w0", "rwkv_a0", "rwkv_kk", "rwkv_ka", "rwkv_rk", "rwkv_ln_w", "rwkv_ln_b"]
    prm = np.stack([f(inp[n_])[0].reshape(8, 128).T for n_ in names], axis=1)
    return dict(w_in=w_in_r, w_out=w_out_r, w_gate=wg, w_up=wu, w_down=wd, gam=np.ascontiguousarray(gam), mu=mu,
                prm=np.ascontiguousarray(prm), w2=f(inp["rwkv_w2"])[0], a2=f(inp["rwkv_a2"])[0], g2=f(inp["rwkv_g2"])[0],
                cst=CST, cos=COS_T, sin=SIN_T)


def kernel(**inp):
    f = lambda a: np.asarray(a, dtype=np.float32)
    shared = _prep_weights(inp)
    xp = f(inp["x_prompt"])
    xsm = f(inp["x_sample"])
    meta = f(inp["meta_tokens"])
    ssh = f(inp["state_shift"])[0]
    srw = f(inp["state_rwkv"])[0]
    srt = f(inp["state_ret"])[0]
    in_maps = []
    for c in range(8):
        b, hh = c // 2, c % 2
        xs_c = xsm[16 * c:16 * c + 16].reshape(64, D)
        xs = np.concatenate([meta, xp[b], xs_c], axis=0)
        x_own = np.concatenate([xp[b][hh * 1024:(hh + 1) * 1024], xs_c], axis=0)
        sel = np.zeros((128, 2), np.float32)
        sel[:, hh] = 1.0
        sshp = np.zeros((16, 27 * 128), np.float32)
        sshp[:, :3360] = ssh[16 * c:16 * c + 16]
        ssh_r = np.ascontiguousarray(sshp.reshape(16, 27, 128).transpose(1, 2, 0))
        m = dict(shared)
        m.update(xs=np.ascontiguousarray(xs), x_own=np.ascontiguousarray(x_own), sel=sel, ssh=ssh_r,
                 srw=np.ascontiguousarray(srw[16 * c:16 * c + 16]), srt=np.ascontiguousarray(srt[16 * c:16 * c + 16]))
        in_maps.append(m)
    if inp.get("_maps_only"):
        return in_maps
    if "nc" not in _NC_CACHE:
        _NC_CACHE["nc"] = build()
    res = run_bass_kernel_spmd(_NC_CACHE["nc"], in_maps, core_ids=list(range(8)))
    R = res.results
    y_prompt = np.zeros((4, SEQ, D), np.float32)
    y_sample = np.zeros((128, 4, D), np.float32)
    shift_p = np.zeros((1, 4, 3360), np.float32)
    rwkv_p = np.zeros((1, 4, 16, 64, 64), np.float32)
    ret_p = np.zeros((1, 4, 4, 256, 256), np.float32)
    shift_s = np.zeros((1, 128, 3360), np.float32)
    rwkv_s = np.zeros((1, 128, 16, 64, 64), np.float32)
    ret_s = np.zeros((1, 128, 4, 256, 256), np.float32)
    for c in range(8):
        b, hh = c // 2, c % 2
        r = R[c]
        y_prompt[b, hh * 1024:(hh + 1) * 1024] = r["y"][:1024]
        y_sample[16 * c:16 * c + 16] = r["y"][1024:].reshape(16, 4, D)
        shift_s[0, 16 * c:16 * c + 16] = r["sho"][1:17, :3360]
        rwkv_s[0, 16 * c:16 * c + 16] = r["rwo"][1:17]
        ret_s[0, 16 * c:16 * c + 16] = r["rto"][1:17]
        if hh == 0:
            shift_p[0, b] = r["sho"][0, :3360]
            rwkv_p[0, b] = r["rwo"][0]
            ret_p[0, b] = r["rto"][0]
    return (y_prompt, y_sample, shift_p, rwkv_p, ret_p, shift_s, rwkv_s, ret_s)
```
